# Optimizing a Trainium2 kernel written in Bass

```python
import math
import jax, jax.numpy as jnp
from jax import lax
import numpy as np

D_MODEL = 2048
BATCH = 4
SEQ = 2048
DEPTH = 1
DEC_BATCH = 128
DEC_SEQ = 4
PAST_LEN = 16384
PAGE_SIZE = 128

GDN_HEADS = 8
GDN_DK = 128
GDN_DV = 128
GDN_WIDTH = GDN_HEADS * GDN_DV
CONV_WIDTH = 4
CONV_CH = GDN_HEADS * (2 * GDN_DK + GDN_DV)
RET_HEADS = 8
RET_DK = 128
RET_DV = 128
RET_QK = RET_HEADS * RET_DK
RET_WIDTH = RET_HEADS * RET_DV
MIX_WIDTH = GDN_WIDTH + RET_WIDTH
N_IN = CONV_CH + GDN_WIDTH + 2 * GDN_HEADS + 2 * RET_QK + 2 * RET_WIDTH
A_COL_START = CONV_CH + GDN_WIDTH + GDN_HEADS
DECAY_PROJ_SCALE = 0.1
D_FF = 5632
MACARON_WEIGHT = 0.5
N_ADA = 9
CHUNK = 64
ROPE_BASE = 10000.0
NORM_EPS = 1e-6
L2_EPS = 1e-6

kernel_name = "hymba_gdn_retention_macaron_adaln_step"


def rms_norm(x, gain):
    xf = x.astype(jnp.float32)
    y = xf * lax.rsqrt(jnp.mean(xf * xf, axis=-1, keepdims=True) + NORM_EPS)
    return (y * gain.astype(jnp.float32)).astype(x.dtype)


def modulate(h, shift, scale):
    return h * (1.0 + scale[:, None, :]) + shift[:, None, :]


def swiglu(h, w_gate, w_up, w_down):
    return (jax.nn.silu(h @ w_gate) * (h @ w_up)) @ w_down


def l2_normalize(x):
    return x * lax.rsqrt(jnp.sum(x * x, axis=-1, keepdims=True) + L2_EPS)


def causal_short_conv(x, buf, w):
    t = x.shape[1]
    xp = jnp.concatenate([buf.astype(x.dtype), x], axis=1)
    out = xp[:, 0:t] * w[0]
    for i in range(1, CONV_WIDTH):
        out = out + xp[:, i:i + t] * w[i]
    return jax.nn.silu(out), xp[:, t:]


def rotary(x, pos):
    half = x.shape[-1] // 2
    inv_freq = ROPE_BASE ** (-jnp.arange(half, dtype=jnp.float32) / half)
    ang = pos.astype(jnp.float32)[:, None] * inv_freq[None, :]
    cos = jnp.cos(ang)[None, :, None, :]
    sin = jnp.sin(ang)[None, :, None, :]
    x1, x2 = x[..., :half], x[..., half:]
    return jnp.concatenate([x1 * cos - x2 * sin, x1 * sin + x2 * cos], axis=-1)


def to_chunks(a, n, c):
    b, _, h = a.shape[:3]
    a = a.reshape((b, n, c, h) + a.shape[3:])
    return jnp.moveaxis(jnp.swapaxes(a, 2, 3), 1, 0)


def from_chunks(o):
    n, b, h, c, d = o.shape
    return jnp.transpose(o, (1, 0, 3, 2, 4)).reshape(b, n * c, h, d)


def gated_delta_rule(q, k, v, beta, g, s0):
    t = q.shape[1]
    c = min(CHUNK, t)
    n = t // c
    dk = q.shape[-1]
    dv = v.shape[-1]
    q = to_chunks(q, n, c) * (dk ** -0.5)
    k = to_chunks(k, n, c)
    v = to_chunks(v, n, c)
    beta = to_chunks(beta, n, c)
    gc = jnp.cumsum(to_chunks(g, n, c), axis=-1)
    causal = jnp.tril(jnp.ones((c, c), dtype=bool))
    strict = jnp.tril(jnp.ones((c, c), dtype=bool), -1)
    decay = jnp.exp(jnp.where(causal, gc[..., :, None] - gc[..., None, :], -jnp.inf))
    kb = k * beta[..., None]
    a_mat = jnp.where(strict, jnp.einsum('nbhid,nbhjd->nbhij', kb, k) * decay, 0.0) + jnp.eye(c, dtype=jnp.float32)
    rhs = jnp.concatenate([v * beta[..., None], kb * jnp.exp(gc)[..., None]], axis=-1)
    sol = lax.linalg.triangular_solve(a_mat, rhs, left_side=True, lower=True, unit_diagonal=True)
    u, w = sol[..., :dv], sol[..., dv:]
    qk = jnp.einsum('nbhid,nbhjd->nbhij', q, k) * decay
    q_dec = q * jnp.exp(gc)[..., None]
    k_dec = k * jnp.exp(gc[..., -1:] - gc)[..., None]
    g_last = jnp.exp(gc[..., -1])[..., None, None]

    def step(s, inp):
        qk_i, qd_i, kd_i, u_i, w_i, gl_i = inp
        v_new = u_i - jnp.einsum('bhcd,bhde->bhce', w_i, s)
        o = jnp.einsum('bhcd,bhde->bhce', qd_i, s) + jnp.einsum('bhij,bhje->bhie', qk_i, v_new)
        s = s * gl_i + jnp.einsum('bhcd,bhce->bhde', kd_i, v_new)
        return s, o

    s_final, o = lax.scan(step, s0, (qk, q_dec, k_dec, u, w, g_last))
    return from_chunks(o), s_final


def multiscale_retention(q, k, v, s0):
    t = q.shape[1]
    c = min(CHUNK, t)
    n = t // c
    h = q.shape[2]
    dk = q.shape[-1]
    q = to_chunks(q, n, c)
    k = to_chunks(k, n, c) * (dk ** -0.5)
    v = to_chunks(v, n, c)
    log_gamma = jnp.log(1.0 - 2.0 ** (-5.0 - jnp.arange(h, dtype=jnp.float32)))
    idx = jnp.arange(c, dtype=jnp.float32)
    rel = idx[:, None] - idx[None, :]
    dmat = jnp.where(rel >= 0, jnp.exp(jnp.maximum(rel, 0.0) * log_gamma[:, None, None]), 0.0)
    inner = jnp.einsum('nbhid,nbhjd->nbhij', q, k) * dmat
    q_dec = q * jnp.exp((idx + 1.0) * log_gamma[:, None])[..., None]
    k_dec = k * jnp.exp((c - 1.0 - idx) * log_gamma[:, None])[..., None]
    g_chunk = jnp.exp(c * log_gamma)[:, None, None]

    def step(s, inp):
        in_i, qd_i, kd_i, v_i = inp
        o = jnp.einsum('bhcd,bhde->bhce', qd_i, s) + jnp.einsum('bhij,bhje->bhie', in_i, v_i)
        s = s * g_chunk + jnp.einsum('bhcd,bhce->bhde', kd_i, v_i)
        return s, o

    s_final, o = lax.scan(step, s0, (inner, q_dec, k_dec, v))
    return from_chunks(o), s_final


def token_mixer(h, pos, conv_buf, s_gdn, s_ret, w_in, conv_w, a_log, dt_bias,
                gdn_norm_w, ret_gn_w, ret_gn_b, w_out):
    b, t, _ = h.shape
    f32 = jnp.float32
    proj = (h @ w_in).astype(f32)
    o1 = CONV_CH
    o2 = o1 + GDN_WIDTH
    o3 = o2 + GDN_HEADS
    o4 = o3 + GDN_HEADS
    o5 = o4 + RET_QK
    o6 = o5 + RET_QK
    o7 = o6 + RET_WIDTH
    qkv, z, beta_raw, a_raw, rq, rk, rv, rg = jnp.split(proj, [o1, o2, o3, o4, o5, o6, o7], axis=-1)

    qkv, conv_new = causal_short_conv(qkv, conv_buf.astype(f32), conv_w.astype(f32))
    gq, gk, gv = jnp.split(qkv, [GDN_HEADS * GDN_DK, 2 * GDN_HEADS * GDN_DK], axis=-1)
    gq = l2_normalize(gq.reshape(b, t, GDN_HEADS, GDN_DK))
    gk = l2_normalize(gk.reshape(b, t, GDN_HEADS, GDN_DK))
    gv = gv.reshape(b, t, GDN_HEADS, GDN_DV)
    beta = jax.nn.sigmoid(beta_raw)
    g = -jnp.exp(a_log.astype(f32)) * jax.nn.softplus(a_raw + dt_bias.astype(f32))
    o_gdn, s_gdn_new = gated_delta_rule(gq, gk, gv, beta, g, s_gdn.astype(f32))
    o_gdn = o_gdn * lax.rsqrt(jnp.mean(o_gdn * o_gdn, axis=-1, keepdims=True) + NORM_EPS) * gdn_norm_w.astype(f32)
    o_gdn = o_gdn.reshape(b, t, GDN_WIDTH) * jax.nn.silu(z)

    rq = rotary(rq.reshape(b, t, RET_HEADS, RET_DK), pos)
    rk = rotary(rk.reshape(b, t, RET_HEADS, RET_DK), pos)
    rv = rv.reshape(b, t, RET_HEADS, RET_DV)
    o_ret, s_ret_new = multiscale_retention(rq, rk, rv, s_ret.astype(f32))
    mu = jnp.mean(o_ret, axis=-1, keepdims=True)
    var = jnp.mean(jnp.square(o_ret - mu), axis=-1, keepdims=True)
    o_ret = ((o_ret - mu) * lax.rsqrt(var + NORM_EPS)).reshape(b, t, RET_WIDTH)
    o_ret = (o_ret * ret_gn_w.astype(f32) + ret_gn_b.astype(f32)) * jax.nn.silu(rg)

    mixed = jnp.concatenate([o_gdn, o_ret], axis=-1).astype(h.dtype) @ w_out
    return mixed, s_gdn_new, conv_new, s_ret_new


def decoder_layer(x, c, pos, s_gdn, s_conv, s_ret, w_ada, b_ada, norm_ffn1, w1_gate, w1_up, w1_down,
                  norm_mix, w_in, conv_w, a_log, dt_bias, gdn_norm_w, ret_gn_w, ret_gn_b, w_out,
                  norm_ffn2, w2_gate, w2_up, w2_down):
    ada = jax.nn.silu(c) @ w_ada + b_ada
    sh1, sc1, g1, sh2, sc2, g2, sh3, sc3, g3 = jnp.split(ada, N_ADA, axis=-1)
    h = modulate(rms_norm(x, norm_ffn1), sh1, sc1)
    x = x + MACARON_WEIGHT * g1[:, None, :] * swiglu(h, w1_gate, w1_up, w1_down)
    h = modulate(rms_norm(x, norm_mix), sh2, sc2)
    mixed, s_gdn, s_conv, s_ret = token_mixer(h, pos, s_conv, s_gdn, s_ret, w_in, conv_w, a_log, dt_bias,
                                              gdn_norm_w, ret_gn_w, ret_gn_b, w_out)
    x = x + g2[:, None, :] * mixed
    h = modulate(rms_norm(x, norm_ffn2), sh3, sc3)
    x = x + MACARON_WEIGHT * g3[:, None, :] * swiglu(h, w2_gate, w2_up, w2_down)
    return x, s_gdn, s_conv, s_ret


def trunk(x, c, pos, s_gdn, s_conv, s_ret, layer_weights, w_ada_final, b_ada_final, norm_final):
    new_gdn, new_conv, new_ret = [], [], []
    for layer in range(DEPTH):
        x, sg, sc, sr = decoder_layer(x, c, pos, s_gdn[layer], s_conv[layer], s_ret[layer],
                                      *[w[layer] for w in layer_weights])
        new_gdn.append(sg.astype(s_gdn.dtype))
        new_conv.append(sc.astype(s_conv.dtype))
        new_ret.append(sr.astype(s_ret.dtype))
    shift, scale = jnp.split(jax.nn.silu(c) @ w_ada_final + b_ada_final, 2, axis=-1)
    y = modulate(rms_norm(x, norm_final), shift, scale)
    return y, jnp.stack(new_gdn), jnp.stack(new_conv), jnp.stack(new_ret)


def setup_inputs(seed: int = 0) -> dict:
    key = jax.random.key(seed)
    ks = jax.random.split(key, 32)
    f32 = jnp.float32

    def nrm(k, shape, scale):
        return jax.random.normal(k, shape, f32) * scale

    col_scale = jnp.ones((N_IN,), f32).at[A_COL_START:A_COL_START + GDN_HEADS].set(DECAY_PROJ_SCALE)
    dt = jnp.exp(jax.random.uniform(ks[17], (DEPTH, GDN_HEADS), f32, math.log(1e-3), math.log(1e-1)))
    return {
        "x_prompt": nrm(ks[0], (BATCH, SEQ, D_MODEL), 1.0),
        "x_sample": nrm(ks[1], (DEC_BATCH, DEC_SEQ, D_MODEL), 1.0),
        "state_gdn": nrm(ks[2], (DEPTH, DEC_BATCH, GDN_HEADS, GDN_DK, GDN_DV), 0.5),
        "state_conv": nrm(ks[3], (DEPTH, DEC_BATCH, CONV_WIDTH - 1, CONV_CH), 1.0),
        "state_ret": nrm(ks[4], (DEPTH, DEC_BATCH, RET_HEADS, RET_DK, RET_DV), 0.5),
        "c_prompt": nrm(ks[5], (BATCH, D_MODEL), 1.0),
        "c_sample": nrm(ks[6], (DEC_BATCH, D_MODEL), 1.0),
        "w_ada": nrm(ks[7], (DEPTH, D_MODEL, N_ADA * D_MODEL), 0.5 * D_MODEL ** -0.5),
        "b_ada": nrm(ks[8], (DEPTH, N_ADA * D_MODEL), 0.01),
        "norm_ffn1": 1.0 + nrm(ks[9], (DEPTH, D_MODEL), 0.02),
        "w1_gate": nrm(ks[10], (DEPTH, D_MODEL, D_FF), D_MODEL ** -0.5),
        "w1_up": nrm(ks[11], (DEPTH, D_MODEL, D_FF), D_MODEL ** -0.5),
        "w1_down": nrm(ks[12], (DEPTH, D_FF, D_MODEL), D_FF ** -0.5),
        "norm_mix": 1.0 + nrm(ks[13], (DEPTH, D_MODEL), 0.02),
        "w_in": nrm(ks[14], (DEPTH, D_MODEL, N_IN), D_MODEL ** -0.5) * col_scale,
        "conv_w": nrm(ks[15], (DEPTH, CONV_WIDTH, CONV_CH), CONV_WIDTH ** -0.5),
        "a_log": jnp.log(jax.random.uniform(ks[16], (DEPTH, GDN_HEADS), f32, 1.0, 16.0)),
        "dt_bias": dt + jnp.log(-jnp.expm1(-dt)),
        "gdn_norm_w": 1.0 + nrm(ks[18], (DEPTH, GDN_DV), 0.02),
        "ret_gn_w": 1.0 + nrm(ks[19], (DEPTH, RET_WIDTH), 0.02),
        "ret_gn_b": nrm(ks[20], (DEPTH, RET_WIDTH), 0.01),
        "w_out": nrm(ks[21], (DEPTH, MIX_WIDTH, D_MODEL), MIX_WIDTH ** -0.5),
        "norm_ffn2": 1.0 + nrm(ks[22], (DEPTH, D_MODEL), 0.02),
        "w2_gate": nrm(ks[23], (DEPTH, D_MODEL, D_FF), D_MODEL ** -0.5),
        "w2_up": nrm(ks[24], (DEPTH, D_MODEL, D_FF), D_MODEL ** -0.5),
        "w2_down": nrm(ks[25], (DEPTH, D_FF, D_MODEL), D_FF ** -0.5),
        "w_ada_final": nrm(ks[26], (D_MODEL, 2 * D_MODEL), 0.5 * D_MODEL ** -0.5),
        "b_ada_final": nrm(ks[27], (2 * D_MODEL,), 0.01),
        "norm_final": 1.0 + nrm(ks[28], (D_MODEL,), 0.02),
    }


def reference(x_prompt, x_sample, state_gdn, state_conv, state_ret, c_prompt, c_sample,
              w_ada, b_ada, norm_ffn1, w1_gate, w1_up, w1_down, norm_mix, w_in, conv_w, a_log, dt_bias,
              gdn_norm_w, ret_gn_w, ret_gn_b, w_out, norm_ffn2, w2_gate, w2_up, w2_down,
              w_ada_final, b_ada_final, norm_final):
    layer_weights = (w_ada, b_ada, norm_ffn1, w1_gate, w1_up, w1_down, norm_mix, w_in, conv_w, a_log,
                     dt_bias, gdn_norm_w, ret_gn_w, ret_gn_b, w_out, norm_ffn2, w2_gate, w2_up, w2_down)
    bp, tp = x_prompt.shape[0], x_prompt.shape[1]
    ts = x_sample.shape[1]
    pos_prompt = jnp.arange(tp, dtype=jnp.int32)
    pos_sample = PAST_LEN + jnp.arange(ts, dtype=jnp.int32)
    zero_gdn = jnp.zeros((DEPTH, bp, GDN_HEADS, GDN_DK, GDN_DV), state_gdn.dtype)
    zero_conv = jnp.zeros((DEPTH, bp, CONV_WIDTH - 1, CONV_CH), state_conv.dtype)
    zero_ret = jnp.zeros((DEPTH, bp, RET_HEADS, RET_DK, RET_DV), state_ret.dtype)
    y_prompt, gdn_p, conv_p, ret_p = trunk(x_prompt, c_prompt, pos_prompt, zero_gdn, zero_conv, zero_ret,
                                           layer_weights, w_ada_final, b_ada_final, norm_final)
    y_sample, gdn_s, conv_s, ret_s = trunk(x_sample, c_sample, pos_sample, state_gdn, state_conv, state_ret,
                                           layer_weights, w_ada_final, b_ada_final, norm_final)
    return (y_prompt, y_sample, gdn_p, conv_p, ret_p, gdn_s, conv_s, ret_s)
```

```python
import contextlib
import numpy as np
import concourse.bass as bass
import concourse.mybir as mybir
from concourse.bass_utils import run_bass_kernel_spmd

F32 = mybir.dt.float32
BF16 = mybir.dt.bfloat16
ALU = mybir.AluOpType
AF = mybir.ActivationFunctionType
AX = mybir.AxisListType

NCORES = 8
D = 2048
KC = 16
DFF = 5632
NFC = 44
NPASS = 2
TPP = 2048 // NPASS
NPT = TPP // 128
SPC = 16
SPP = SPC // NPASS
TS = 4 * SPP
T = TPP + TS
NSEG = 3
SEG = T // NSEG
assert SEG * NSEG == T and SEG <= 512
PAST = 16384
EPS = 1e-6
NEG = -30000.0
CU = 128
NWU = 6
NWD = 4

DEBUG = {}
STAGE = None
STAGGER = True
STAGGER_G = 350
STAGGER_R = 150


class _Stop(Exception):
    pass


class Buf:
    __slots__ = ("name", "w", "r")

    def __init__(self, name):
        self.name = name
        self.w = None
        self.r = []


class Op:
    __slots__ = ("eng", "fns", "deps", "signal", "semval", "stream", "dval")

    def __init__(self, eng, fns):
        self.eng = eng
        self.fns = fns
        self.deps = []
        self.signal = False
        self.semval = 0
        self.stream = None
        self.dval = 0


CL = [None]


class PX:
    def __init__(self, name):
        self._n = name

    def _t(self):
        return getattr(CL[0], self._n)

    def __getitem__(self, k):
        return self._t()[k]

    def __iter__(self):
        return iter(self._t())

    def __getattr__(self, a):
        return getattr(self._t(), a)


class BX:
    def __init__(self, name):
        self._n = name

    def get(self):
        return getattr(CL[0], self._n)


class _Rec:
    def __init__(self):
        self.calls = []

    def __getattr__(self, name):
        def f(*a, **k):
            a = tuple(x._t() if isinstance(x, PX) else x for x in a)
            k = {kk: (v._t() if isinstance(v, PX) else v) for kk, v in k.items()}
            self.calls.append((name, a, k))
            return self
        return f


class Lane:
    pass


class Prog:
    ENGS = ("pe", "act", "dve", "pool", "sp")

    def __init__(self, nc):
        self.nc = nc
        self.ops = {e: [] for e in self.ENGS}
        self.streams = {}
        self.last_stream_op = {}
        self.ovset = set()
        self.pstok = {}
        self.bPH = Buf("phase")
        self.rec_target = None
        self.rec_suffix = ""
        self.cur_atomic = None

    def buf(self, name):
        return Buf(name)

    def bufs(self, name, n):
        return [Buf(f"{name}{i}") for i in range(n)]

    def _add(self, eng, fns, reads, writes, stream=None):
        if not isinstance(fns, (list, tuple)):
            fns = [fns]
        rec = _Rec()
        for f in fns:
            f(rec)
        calls = rec.calls
        reads = [b.get() if isinstance(b, BX) else b for b in reads]
        writes = [b.get() if isinstance(b, BX) else b for b in writes]
        if self.rec_target is not None:
            if stream is not None and not stream.startswith("WU"):
                stream = stream + self.rec_suffix
            self.rec_target.append((eng, calls, reads, writes, stream, self.cur_atomic))
            return None
        return self._add_real(eng, calls, reads, writes, stream)

    def replay(self, entries):
        for (eng, calls, reads, writes, stream, _a) in entries:
            self._add_real(eng, calls, reads, writes, stream)

    def _add_real(self, eng, calls, reads, writes, stream=None):
        op = Op(eng, calls)
        deps = {}
        if any(id(b) in self.ovset for b in writes) and self.bPH not in reads:
            reads = list(reads) + [self.bPH]
        if eng != "pe":
            toks = [self.pstok[id(b)] for b in reads if id(b) in self.pstok]
            if toks:
                writes = list(writes) + toks
        for b in reads:
            if b.w is not None:
                deps[id(b.w)] = b.w
        for b in writes:
            if b.w is not None:
                deps[id(b.w)] = b.w
            for r in b.r:
                deps[id(r)] = r
        for d in deps.values():
            if d is op:
                continue
            if d.eng == "pe" and eng == "pe" and d.stream is None and stream is None:
                continue
            op.deps.append(d)
            d.signal = True
        for b in reads:
            b.r.append(op)
        for b in writes:
            b.w = op
            b.r = []
        if stream is not None:
            op.stream = stream
            self.streams[stream] = self.streams.get(stream, 0) + 1
            op.dval = 16 * self.streams[stream]
            self.last_stream_op[stream] = op
        self.ops[eng].append(op)
        return op

    def barrier(self, fn):
        rec = _Rec()
        fn(rec)
        op = Op("dve", rec.calls)
        for e in self.ENGS:
            for o in reversed(self.ops[e]):
                if o.stream is None:
                    if o is not op:
                        op.deps.append(o)
                        o.signal = True
                    break
        for so in self.last_stream_op.values():
            op.deps.append(so)
        b = self.bPH
        b.w = op
        b.r = []
        self.ops["dve"].append(op)
        return op

    def op(self, eng, fns, reads=(), writes=()):
        return self._add(eng, fns, reads, writes)

    def dma(self, eng, out, in_, reads=(), writes=(), stream=None, **kw):
        assert stream is not None
        out = out._t() if isinstance(out, PX) else out
        in_ = in_._t() if isinstance(in_, PX) else in_
        return self._add(eng, [lambda e, o=out, i=in_, k=kw: e.dma_start(out=o, in_=i, **k)],
                         reads, writes, stream=stream)

    def emit(self, final_wait_eng="sp"):
        nc = self.nc
        for e in self.ENGS:
            c = 0
            for op in self.ops[e]:
                if op.stream is None and op.signal:
                    c += 1
                    op.semval = c
        with contextlib.ExitStack() as st:
            esem = {e: st.enter_context(nc.semaphore("s_" + e)) for e in self.ENGS}
            dsem = {s: st.enter_context(nc.semaphore("d_" + s)) for s in self.streams}
            block = st.enter_context(nc.Block())
            engobj = {"pe": block.tensor, "act": block.scalar, "dve": block.vector,
                      "pool": block.gpsimd, "sp": block.sync}

            def run(ename):
                def body(e):
                    seen = {}
                    for op in self.ops[ename]:
                        need = {}
                        for d in op.deps:
                            if d.stream is not None:
                                key, val, sem = "d_" + d.stream, d.dval, dsem[d.stream]
                            else:
                                key, val, sem = d.eng, d.semval, esem[d.eng]
                            if val > need.get(key, (0, None))[0]:
                                need[key] = (val, sem)
                        for key, (val, sem) in need.items():
                            if seen.get(key, 0) >= val:
                                continue
                            seen[key] = val
                            e.wait_ge(sem, val)
                        ins = None
                        for (name, a, k) in op.fns:
                            ins = getattr(e, name)(*a, **k)
                        if op.stream is not None:
                            ins.then_inc(dsem[op.stream], 16)
                        elif op.signal:
                            ins.then_inc(esem[ename], 1)
                    if ename == final_wait_eng:
                        for s, n in self.streams.items():
                            e.wait_ge(dsem[s], 16 * n)
                        for en in self.ENGS:
                            c = max([o.semval for o in self.ops[en]] + [0])
                            if c:
                                e.wait_ge(esem[en], c)
                return body

            for ename in self.ENGS:
                engobj[ename](run(ename))


def _tile_blocks(kind):
    if kind == "P":
        i = np.arange(128)
        return 128, i // 64, i % 64
    i = np.arange(TS)
    return TS, i % SPP, i // SPP


def host_consts():
    c = {}
    c["ident"] = np.eye(128, dtype=np.float32)
    c["ones"] = np.ones((128, 128), np.float32)
    rsw = np.zeros((128, 128), np.float32)
    for p in range(64):
        rsw[p + 64, p] = 1.0
        rsw[p, p + 64] = 1.0
    c["rsw"] = rsw
    lg = np.log(1.0 - 2.0 ** (-5.0 - np.arange(8, dtype=np.float32))).astype(np.float32)
    for kind in ("P", "S"):
        n, blk, pos = _tile_blocks(kind)
        same = blk[:, None] == blk[None, :]
        m = np.zeros((4, 128, 128), np.float32)
        m[0, :n, :n] = np.where(same & (pos[None, :] >= pos[:, None]), 0.0, NEG)
        m[1, :n, :n] = np.where(same & (pos[:, None] > pos[None, :]), 0.0, NEG)
        m[2, :n, :n] = np.where(same & (pos[:, None] <= pos[None, :]), 1.0, 0.0)
        m[3, :n, :n] = np.where(same, 1.0, 0.0)
        c["mask" + kind] = m if kind == "P" else np.ascontiguousarray(m[:, :, :TS])
        C = 64 if kind == "P" else 4
        rel = (pos[None, :] - pos[:, None]).astype(np.float32)
        rdm = np.zeros((8, 128, 128), np.float32)
        rqd = np.zeros((8, 128, 128), np.float32)
        rkd = np.zeros((128, 8), np.float32)
        rgc = np.zeros((128, 8), np.float32)
        for h in range(8):
            dm = np.where(same & (rel >= 0), np.exp(np.maximum(rel, 0.0) * lg[h]), 0.0)
            rdm[h, :n, :n] = dm * (128.0 ** -0.5)
            rqd[h, :, :n] = np.exp((pos.astype(np.float32) + 1.0) * lg[h])[None, :]
            rkd[:n, h] = np.exp((C - 1.0 - pos.astype(np.float32)) * lg[h]) * (128.0 ** -0.5)
            rgc[:, h] = np.exp(C * lg[h])
        c["rdm" + kind] = rdm.astype(np.float32)
        c["rqd" + kind] = rqd.astype(np.float32)
        c["rkd" + kind] = rkd.astype(np.float32)
        c["rgc" + kind] = rgc.astype(np.float32)
    oh = np.zeros((128, SPP), np.float32)
    for i in range(TS):
        oh[i, i % SPP] = 1.0
    c["onehot"] = oh
    half = 64
    inv = (10000.0 ** (-np.arange(half, dtype=np.float32) / np.float32(half))).astype(np.float32)
    rot = np.zeros((NPASS, 2, 128, T), np.float32)
    for p in range(NPASS):
        pos = np.concatenate([p * TPP + np.arange(TPP), PAST + (np.arange(TS) // SPP)]).astype(np.float32)
        ang = (pos[None, :] * inv[:, None]).astype(np.float32).astype(np.float64)
        cs, sn = np.cos(ang), np.sin(ang)
        rot[p, 0, :64] = cs
        rot[p, 0, 64:] = cs
        rot[p, 1, :64] = -sn
        rot[p, 1, 64:] = sn
    c["rot"] = rot
    cst = np.zeros((128, 8), np.float32)
    cst[:, 0] = EPS
    cst[:, 1] = 128.0 * EPS
    cst[:, 2] = 1.0
    c["cst"] = cst
    return c


def tile_w(W, cu=CU):
    K, M = W.shape
    return np.ascontiguousarray(W.reshape(K // 128, 128, M // cu, cu).transpose(2, 1, 0, 3))


def build_program():
    nc = bass.Bass("TRN2", target_bir_lowering=False)
    ins_spec = {}
    outs_spec = {}

    def din(name, shape):
        ins_spec[name] = tuple(shape)
        return nc.dram_tensor(name, list(shape), F32, kind="ExternalInput").ap()

    def dout(name, shape):
        outs_spec[name] = tuple(shape)
        return nc.dram_tensor(name, list(shape), F32, kind="ExternalOutput").ap()

    xin = din("xin", [NPASS, 128, KC, T])
    cv = din("cv", [128, KC, 17])
    sgdn = din("sgdn", [NPASS, 8, 128, SPP, 128])
    sret = din("sret", [NPASS, 8, 128, SPP, 128])
    sconv = din("sconv", [NPASS, 24, 128, 3, SPP])
    w_ada = din("w_ada", [176, 128, KC, CU])
    b_ada = din("b_ada", [128, 176])
    normw = din("normw", [128, 4, KC])
    wg = [din("w1g", [NFC, 128, KC, CU]), din("w2g", [NFC, 128, KC, CU])]
    wu = [din("w1u", [NFC, 128, KC, CU]), din("w2u", [NFC, 128, KC, CU])]
    wd = [din("w1d", [NFC, 128, D]), din("w2d", [NFC, 128, D])]
    w_in = din("w_in", [64, 128, KC, CU])
    w_ba = din("w_ba", [128, KC, 16])
    w_out = din("w_out", [16, 128, KC, CU])
    convw = din("convw", [128, 24, 4])
    smallv = din("smallv", [128, 32])
    gnb = din("gnb", [128, 8])
    cident = din("ident", [128, 128])
    cones = din("ones", [128, 128])
    crsw = din("rsw", [128, 128])
    cmask = {"P": din("maskP", [4, 128, 128]), "S": din("maskS", [4, 128, TS])}
    crdm = {"P": din("rdmP", [8, 128, 128]), "S": din("rdmS", [8, 128, 128])}
    crqd = {"P": din("rqdP", [8, 128, 128]), "S": din("rqdS", [8, 128, 128])}
    crkd = {"P": din("rkdP", [128, 8]), "S": din("rkdS", [128, 8])}
    crgc = {"P": din("rgcP", [128, 8]), "S": din("rgcS", [128, 8])}
    conehot = din("onehot", [128, SPP])
    crot = din("rot", [NPASS, 2, 128, T])
    ccst = din("cst", [128, 8])

    yout = dout("yout", [NPASS, 128, KC, T])
    gdn_p = dout("gdn_p", [8, 128, 128])
    ret_p = dout("ret_p", [8, 128, 128])
    conv_p = dout("conv_p", [3, 3072])
    gdn_s = dout("gdn_s", [NPASS, 8, 128, SPP, 128])
    ret_s = dout("ret_s", [NPASS, 8, 128, SPP, 128])
    conv_s = dout("conv_s", [NPASS, 3, SPP, 3072])
    dbg_outs = {k: dout("dbg_" + k, v) for k, v in DEBUG.items()}
    sscr = nc.dram_tensor("sscr", [16, 128, 128], F32).ap()
    xscr = nc.dram_tensor("xscr", [128, KC, T], F32).ap()

    st = contextlib.ExitStack()
    with st:
        def sb(name, shape, dt=F32):
            return st.enter_context(nc.sbuf_tensor("s_" + name, list(shape), dt))[:]

        P = Prog(nc)
        bPH = P.bPH

        def stage(k):
            if STAGE is not None and k >= STAGE:
                raise _Stop()

        xT = sb("xT", [128, KC, T])
        hT = sb("hT", [128, KC, T], BF16)
        WU = [sb(f"WU{i}", [128, KC, CU], BF16) for i in range(NWU)]
        adaT = sb("adaT", [128, 176, 17])
        Amod = sb("Amod", [128, 4, KC, 17])
        ident = sb("ident", [128, 128])
        ones = sb("ones", [128, 128])
        onesb = sb("onesb", [128, 128], BF16)
        rsw = sb("rsw", [128, 128])
        masks = {"P": sb("maskP", [128, 4, 128]), "S": sb("maskS", [128, 4, TS])}
        onehot = sb("onehot", [128, SPP])
        cst = sb("cst", [128, 8])
        normT = sb("normT", [128, 4, KC])
        badaT = sb("badaT", [128, 176])
        cwT = sb("cwT", [128, 24, 4])
        smv = sb("smv", [128, 32])
        gnbT = sb("gnbT", [128, 8])
        rkd = {"P": sb("rkdP", [128, 8]), "S": sb("rkdS", [128, 8])}
        rgc = {"P": sb("rgcP", [128, 8]), "S": sb("rgcS", [128, 8])}
        wba = sb("wba", [128, KC, 16], BF16)
        negA = sb("negA", [128, 8])
        TM = sb("TM", [128, NPT + 1, 64])
        Sw_ = [sb(f"Sw{i}", [128, 128]) for i in range(2)]
        ctail = sb("ctail", [128, 24, 3])
        sq0_ = [sb(f"sq{i}", [128, SEG]) for i in range(2)]
        tmpf0_ = [sb(f"tmpf{i}", [128, SEG]) for i in range(2)]

        OVB = 51840
        OV = sb("OV", [128, OVB // 4])

        XV = xT.rearrange("p a b -> p (a b)")
        XVB = KC * T * 4

        def carve(off, shape, dt=F32, base=None, limit=None):
            base = OV if base is None else base
            limit = OVB if limit is None else limit
            esz = 2 if dt == BF16 else 4
            nel = int(np.prod(shape))
            assert off % 4 == 0 and (nel * esz) % 4 == 0 and off + nel * esz <= limit, (off, shape)
            ap = base[:, off // 4:(off + nel * esz) // 4]
            if dt == BF16:
                ap = ap.bitcast(BF16)
            if len(shape) == 2:
                ap = ap.rearrange("p (a b) -> p a b", a=shape[0])
            return ap

        ctmp = carve(0, [KC, 17])
        scT = sb("scT", [128, KC, 17], BF16)
        act = [carve(0, [4, T], BF16), carve(8448, [4, T], BF16)]
        WD = [carve(16896 + i * 4096, [D], BF16) for i in range(NWD)]
        sg = [carve(33280 + j * 704, [SEG], BF16) for j in range(2)]
        oTall = carve(0, [16, T], BF16)
        rstd = carve(0, [T])
        Sall = carve(33792, [SPP, 128])
        wsel = carve(37888, [SPP, 128])
        tlnames = ["DTi", "Dst", "Egc", "a1", "L", "LT", "QKT", "M", "P", "PT", "kbe", "kdec", "vb", "u", "wT",
                   "qdec", "vnew"]
        lanes = []
        for li in range(2):
            L = Lane()
            L.id = li
            xo = 34688 * li
            cx = lambda off, shape, dt=F32, xo=xo: carve(xo + off, shape, dt, base=XV, limit=XVB)
            L.FM = [cx(i * 4224, [T]) for i in range(3)]
            L.szT = cx(12672, [T], BF16)
            L.cb = cx(14784, [3 + TPP])
            L.cbs = cx(18944, [7 * SPP])
            L.cvt = cx(19200, [T])
            if li == 0:
                L.rdm = cx(23424, [128])
                L.rqd = cx(23936, [128])
                L.crow = cx(24448, [384])
                L.tl = {k: cx(25984 + i * 512, [128]) for i, k in enumerate(tlnames)}
                L.sq = sq0_
                L.tmpf = tmpf0_
            else:
                L.tl = {k: cx(23424 + i * 512, [128]) for i, k in enumerate(tlnames)}
                L.sq = [carve(41984 + j * 1408, [SEG]) for j in range(2)]
                L.tmpf = [carve(44800 + j * 1408, [SEG]) for j in range(2)]
                L.crow = carve(47616, [384])
                L.rdm = carve(49152, [128])
                L.rqd = carve(49664, [128])
            L.Sw = Sw_[li]
            lanes.append(L)
        rot = carve(14784, [2, T], base=XV, limit=XVB)
        CL[0] = lanes[0]
        REC = [False]
        tl, FM, szT, cb, cbs, cvt, rdm, rqd, crow, sq, tmpf = (PX(n) for n in (
            "tl", "FM", "szT", "cb", "cbs", "cvt", "rdm", "rqd", "crow", "sq", "tmpf"))

        pst = [st.enter_context(nc.psum_tensor(f"ps{i}", [128, 512], F32))[:] for i in range(8)]
        bps = P.bufs("ps", 8)
        for b_ in bps:
            P.pstok[id(b_)] = Buf("tok_" + b_.name)
        psi = [0]

        def bank():
            if REC[0]:
                L = CL[0]
                i = 4 * L.id + 1 + (L.bk_i % 3)
                L.bk_i += 1
                return pst[i], bps[i]
            i = psi[0] % 8
            psi[0] += 1
            return pst[i], bps[i]

        def bank_o():
            if REC[0]:
                i = 4 * CL[0].id
                return pst[i], bps[i]
            return bank()

        def ovb(b):
            if isinstance(b, list):
                for x in b:
                    ovb(x)
            else:
                P.ovset.add(id(b))
            return b

        bx = [[P.buf(f"x{k}_{s}") for s in range(NSEG)] for k in range(KC)]
        bh = [[P.buf(f"h{k}_{s}") for s in range(NSEG)] for k in range(KC)]
        bWU = P.bufs("WU", NWU)
        bWD = ovb(P.bufs("WD", NWD))
        bC = P.buf("consts")
        bada = P.bufs("adaT", 11)
        bAmod = P.bufs("Amod", 4)
        bact = [ovb(P.bufs(f"act{i}_", 4)) for i in range(2)]
        bsg = ovb(P.bufs("sg", 2))
        brstd = ovb(P.bufs("rstd", NSEG))
        bTM = P.bufs("TM", NPT + 1)
        bsscr = P.bufs("sscr", 16)
        bxscr = P.buf("xscr")
        bctail = P.bufs("ctail", 24)
        boT = [ovb(P.bufs(f"oT{i}_", NSEG)) for i in range(16)]
        brot = ovb(P.buf("rot"))
        bSall = ovb(P.buf("Sall"))
        bwsel = ovb(P.buf("wsel"))
        for L in lanes:
            li = L.id
            L.bk_i = 0
            L.wu_i = 0
            L.bsq = P.bufs(f"sq_{li}_", 2) if li == 0 else ovb(P.bufs(f"sq_{li}_", 2))
            L.btmpf = P.bufs(f"tmpf_{li}_", 2) if li == 0 else ovb(P.bufs(f"tmpf_{li}_", 2))
            L.bSw = P.buf(f"Sw{li}")
            L.bFM = [ovb(P.bufs(f"FM{li}_{i}_", NSEG)) for i in range(3)]
            L.bszT = ovb(P.bufs(f"szT{li}_", NSEG))
            for nm in ("cb", "cbs", "cvt", "rdm", "rqd", "crow"):
                setattr(L, "b" + nm, ovb(P.buf(f"{nm}{li}")))
            L.btl = {k: ovb(P.buf(f"tl{li}_" + k)) for k in tlnames}
            for new, old in (("osb", "L"), ("osq", "LT"), ("rn", "P"), ("on", "PT"), ("mu", "a1"), ("msq", "DTi"),
                             ("var", "Dst"), ("cen", "M"), ("vTM", "vb")):
                L.tl[new] = L.tl[old]
                L.btl[new] = L.btl[old]
        btl, bFM, bszT = PX("btl"), PX("bFM"), PX("bszT")
        bcb, bcbs, bcvt, brdm, brqd, bcrow = (BX("b" + n) for n in ("cb", "cbs", "cvt", "rdm", "rqd", "crow"))
        bsq, btmpf = PX("bsq"), PX("btmpf")
        vexp, bvexp = wsel, bwsel
        bwba = P.buf("wba")
        bnegA = P.buf("negA")
        bscT = P.buf("scT")
        bx_all = [b for r in bx for b in r]

        segs = [(s * SEG, SEG) for s in range(NSEG)]

        def segs_of(c0, n):
            return sorted({c // SEG for c in (c0, c0 + n - 1)})

        wu_i = [0]
        wd_i = [0]

        def load_wu(src_ap, view=None):
            if REC[0]:
                L = CL[0]
                i = 3 * L.id + (L.wu_i % 3)
                L.wu_i += 1
            else:
                i = wu_i[0] % NWU
                wu_i[0] += 1
            dst = WU[i] if view is None else view(WU[i])
            P.dma("pool", dst, src_ap, writes=[bWU[i]], stream=f"WU{i}")
            return i

        def phase_barrier():
            P.barrier(lambda e: e.memset(cst[:, 7:8], 0.0))

        dbg_n = [0]

        def dbg(name, ap, reads):
            if name in dbg_outs:
                dbg_n[0] += 1
                P.dma("sp", dbg_outs[name], ap, reads=reads, stream=f"dbg{dbg_n[0]}")

        ci = [0]

        def cload(dst, src):
            ci[0] += 1
            P.dma("sp", dst, src, writes=[bC], stream=f"c{ci[0]}")

        cload(ident, cident)
        cload(ones, cones)
        cload(rsw, crsw)
        for k in ("P", "S"):
            cload(masks[k], cmask[k].rearrange("m p n -> p m n"))
            cload(rkd[k], crkd[k])
            cload(rgc[k], crgc[k])
        cload(onehot, conehot)
        cload(cst, ccst)
        cload(normT, normw)
        cload(badaT, b_ada)
        cload(cwT, convw)
        cload(smv, smallv)
        cload(gnbT, gnb)
        cload(ctmp, cv)
        P.dma("pool", wba, w_ba, writes=[bwba], stream="wba")
        P.op("act", lambda e: e.activation(out=negA, in_=smv[:, 8:16], func=AF.Exp), reads=[bC], writes=[bnegA])
        P.op("dve", lambda e: e.tensor_scalar(out=negA, in0=negA, scalar1=-1.0, scalar2=None, op0=ALU.mult),
             reads=[bnegA], writes=[bnegA])
        P.op("act", lambda e: e.activation(out=scT, in_=ctmp, func=AF.Silu), reads=[bC], writes=[bscT])
        P.op("dve", lambda e: e.memset(ctail, 0.0), writes=bctail)
        bonesb = P.buf("onesb")
        P.op("dve", lambda e: e.tensor_copy(out=onesb, in_=ones), reads=[bC], writes=[bonesb])

        sc_off = [16, 64, 112, 160]
        sh_off = [0, 48, 96, 144]
        g_off = [32, 80, 128]
        ada_done = set()

        def ada_group(gi):
            if gi in ada_done:
                return
            ada_done.add(gi)
            for u in range(16 * gi, 16 * gi + 16):
                wi = load_wu(w_ada[u])
                pt, pb = bank()
                P.op("pe", [lambda e, kc=kc, wi=wi, pt=pt: e.matmul(
                    pt[:, 0:17], WU[wi][:, kc, :], scT[:, kc, :], start=(kc == 0), stop=(kc == KC - 1))
                    for kc in range(KC)], reads=[bWU[wi], bscT], writes=[pb])
                P.op("dve", lambda e, u=u, pt=pt: e.tensor_scalar(
                    out=adaT[:, u, :], in0=pt[:, 0:17], scalar1=badaT[:, u:u + 1], scalar2=None, op0=ALU.add),
                    reads=[pb, bC] + ([bada[gi]] if u % 16 else []), writes=[bada[gi]])
            if gi * 16 in sc_off:
                s_ = sc_off.index(gi * 16)
                P.op("dve", lambda e: e.scalar_tensor_tensor(
                    out=Amod[:, s_, :, :], in0=adaT[:, sc_off[s_]:sc_off[s_] + 16, :], scalar=1.0,
                    in1=normT[:, s_, :].unsqueeze(2).to_broadcast([128, KC, 17]), op0=ALU.add, op1=ALU.mult),
                    reads=[bada[gi], bC], writes=[bAmod[s_]])
            if gi * 16 in (g_off[0], g_off[2]):
                P.op("dve", lambda e: e.tensor_scalar(
                    out=adaT[:, gi * 16:gi * 16 + 16, :], in0=adaT[:, gi * 16:gi * 16 + 16, :],
                    scalar1=0.5, scalar2=None, op0=ALU.mult), reads=[bada[gi]], writes=[bada[gi]])

        ada_pending = []
        for gi in (1, 0, 2, 3, 4, 5, 6, 7, 8, 9, 10):
            ada_group(gi)

        def samp_bc(ap2d, p):
            return ap2d[:, 1 + p * SPP:1 + (p + 1) * SPP].unsqueeze(1).to_broadcast([128, 4, SPP])

        def ts(ap):
            return ap.rearrange("p (t s) -> p t s", t=4)

        def norm_mod(p, site, final=False):
            ada_group(sc_off[site] // 16)
            ada_group(sh_off[site] // 16)
            phase_barrier()
            for s, (c0, n) in enumerate(segs):
                pt, pb = bank()
                for kc in range(KC):
                    j = kc % 2
                    P.op("act", lambda e, kc=kc, j=j, c0=c0, n=n: e.activation(
                        out=sq[j].bitcast(BF16)[:, 0:n], in_=xT[:, kc, c0:c0 + n], func=AF.Square),
                        reads=[bx[kc][s]], writes=[bsq[j]])
                    P.op("pe", lambda e, kc=kc, j=j, n=n, pt=pt: e.matmul(
                        pt[:, 0:n], onesb, sq[j].bitcast(BF16)[:, 0:n], start=(kc == 0), stop=(kc == KC - 1)),
                        reads=[bsq[j], bonesb], writes=[pb])
                P.op("act", lambda e, c0=c0, n=n, pt=pt: e.activation(
                    out=rstd[:, c0:c0 + n], in_=pt[:, 0:n], func=AF.Sqrt, bias=cst[:, 0:1], scale=1.0 / D),
                    reads=[pb, bC], writes=[brstd[s]])
                P.op("dve", lambda e, c0=c0, n=n: e.reciprocal(out=rstd[:, c0:c0 + n], in_=rstd[:, c0:c0 + n]),
                     reads=[brstd[s]], writes=[brstd[s]])
                npm = min(c0 + n, TPP) - c0
                for kc in range(KC):
                    j = kc % 2
                    P.op("dve", lambda e, kc=kc, j=j, c0=c0, n=n: e.tensor_tensor(
                        out=tmpf[j][:, 0:n], in0=xT[:, kc, c0:c0 + n], in1=rstd[:, c0:c0 + n], op=ALU.mult),
                        reads=[bx[kc][s], brstd[s]], writes=[btmpf[j]])
                    if not final:
                        wr = [bh[kc][s]]
                        o_p = lambda a, b, kc=kc: hT[:, kc, a:b]
                    else:
                        wr = [bx[kc][s]]
                        o_p = lambda a, b, kc=kc: xT[:, kc, a:b]
                    if npm > 0:
                        P.op("act", lambda e, kc=kc, j=j, c0=c0, npm=npm, o_p=o_p: e.activation(
                            out=o_p(c0, c0 + npm), in_=tmpf[j][:, 0:npm], func=AF.Identity,
                            scale=Amod[:, site, kc, 0:1], bias=adaT[:, sh_off[site] + kc, 0:1]),
                            reads=[btmpf[j], bAmod[site], bada[sh_off[site] // 16]], writes=wr)
                    if npm < n:
                        P.op("dve", lambda e, kc=kc, j=j, npm=npm, n=n: e.tensor_tensor(
                            out=ts(tmpf[j][:, npm:n]), in0=ts(tmpf[j][:, npm:n]),
                            in1=samp_bc(Amod[:, site, kc, :], p), op=ALU.mult),
                            reads=[btmpf[j], bAmod[site]], writes=[btmpf[j]])
                        P.op("dve", lambda e, kc=kc, j=j, npm=npm, n=n, c0=c0, o_p=o_p: e.tensor_tensor(
                            out=ts(o_p(c0 + npm, c0 + n)), in0=ts(tmpf[j][:, npm:n]),
                            in1=samp_bc(adaT[:, sh_off[site] + kc, :], p), op=ALU.add),
                            reads=[btmpf[j], bada[sh_off[site] // 16]], writes=wr)
            if final:
                P.dma("sp", yout[p], xT, reads=[b for r in bx for b in r], stream="yout")

        def gate_evac(pt, pb, p, dc, s, goff):
            c0, n = segs[s]
            npm = min(c0 + n, TPP) - c0
            if npm > 0:
                P.op("dve", lambda e: e.scalar_tensor_tensor(
                    out=xT[:, dc, c0:c0 + npm], in0=pt[:, 0:npm], scalar=adaT[:, goff + dc, 0:1],
                    in1=xT[:, dc, c0:c0 + npm], op0=ALU.mult, op1=ALU.add),
                    reads=[pb, bada[goff // 16], bx[dc][s]], writes=[bx[dc][s]])
            if npm < n:
                j = dc % 2
                P.op("dve", lambda e: e.tensor_tensor(
                    out=ts(tmpf[j][:, npm:n]), in0=ts(pt[:, npm:n]),
                    in1=samp_bc(adaT[:, goff + dc, :], p), op=ALU.mult),
                    reads=[pb, bada[goff // 16]], writes=[btmpf[j]])
                P.op("dve", lambda e: e.tensor_tensor(
                    out=xT[:, dc, c0 + npm:c0 + n], in0=xT[:, dc, c0 + npm:c0 + n], in1=tmpf[j][:, npm:n],
                    op=ALU.add), reads=[btmpf[j], bx[dc][s]], writes=[bx[dc][s]])

        def ffn(p, which):
            phase_barrier()
            goff = g_off[0] if which == 0 else g_off[2]
            NG = NFC // 4

            def gate_up(g):
                a = g % 2
                for f4 in range(4):
                    f = 4 * g + f4
                    wgi = load_wu(wg[which][f])
                    wui = load_wu(wu[which][f])
                    for s, (c0, n) in enumerate(segs):
                        ptg, pbg = bank()
                        ptu, pbu = bank()
                        P.op("pe", [lambda e, kc=kc, ptg=ptg, wgi=wgi, c0=c0, n=n: e.matmul(
                            ptg[:, 0:n], WU[wgi][:, kc, :], hT[:, kc, c0:c0 + n],
                            start=(kc == 0), stop=(kc == KC - 1)) for kc in range(KC)],
                            reads=[bWU[wgi]] + [bh[kc][s] for kc in range(KC)], writes=[pbg])
                        P.op("pe", [lambda e, kc=kc, ptu=ptu, wui=wui, c0=c0, n=n: e.matmul(
                            ptu[:, 0:n], WU[wui][:, kc, :], hT[:, kc, c0:c0 + n],
                            start=(kc == 0), stop=(kc == KC - 1)) for kc in range(KC)],
                            reads=[bWU[wui]] + [bh[kc][s] for kc in range(KC)], writes=[pbu])
                        j = (f4 * NSEG + s) % 2
                        P.op("act", lambda e, j=j, n=n, ptg=ptg: e.activation(
                            out=sg[j][:, 0:n], in_=ptg[:, 0:n], func=AF.Silu), reads=[pbg], writes=[bsg[j]])
                        P.op("dve", lambda e, j=j, n=n, ptu=ptu, a=a, f4=f4, c0=c0: e.tensor_tensor(
                            out=act[a][:, f4, c0:c0 + n], in0=ptu[:, 0:n], in1=sg[j][:, 0:n], op=ALU.mult),
                            reads=[pbu, bsg[j]], writes=[bact[a][f4]])

            def down_mm(g):
                a = g % 2
                wis = []
                for f4 in range(4):
                    i = wd_i[0] % NWD
                    wd_i[0] += 1
                    P.dma("pool", WD[i], wd[which][4 * g + f4], writes=[bWD[i]], stream=f"WD{i}")
                    wis.append(i)
                for dc in range(KC):
                    for s, (c0, n) in enumerate(segs):
                        pt, pb = bank()
                        P.op("pe", [lambda e, f4=f4, pt=pt, dc=dc, c0=c0, n=n, wis=wis, a=a: e.matmul(
                            pt[:, 0:n], WD[wis[f4]][:, dc * 128:(dc + 1) * 128], act[a][:, f4, c0:c0 + n],
                            start=(f4 == 0), stop=(f4 == 3)) for f4 in range(4)],
                            reads=[bWD[i] for i in wis] + bact[a], writes=[pb])
                        gate_evac(pt, pb, p, dc, s, goff)

            ada_group(goff // 16)
            gate_up(0)
            for g in range(NG):
                if g + 1 < NG:
                    gate_up(g + 1)
                if ada_pending:
                    ada_group(ada_pending.pop(0))
                down_mm(g)

        def tile_list():
            tiles = [("P", 128, i * 128) for i in range(NPT)]
            tiles.append(("S", TS, TPP))
            return tiles

        def tm_scalars(p):
            for ti, (kind, n, c0) in enumerate(tile_list()):
                ss = segs_of(c0, n)
                pt, pb = bank()
                P.op("pe", [lambda e, kc=kc, pt=pt, c0=c0, n=n: e.matmul(
                    pt[0:n, 0:16], hT[:, kc, c0:c0 + n], wba[:, kc, :], start=(kc == 0), stop=(kc == KC - 1))
                    for kc in range(KC)],
                    reads=[bwba] + [bh[kc][s] for kc in range(KC) for s in ss], writes=[pb])
                tm = TM[0:n, ti, :]
                b = bTM[ti]
                P.op("act", lambda e, pt=pt, n=n, tm=tm: e.activation(out=tm[:, 0:8], in_=pt[0:n, 0:8], func=AF.Sigmoid),
                     reads=[pb], writes=[b])
                P.op("dve", lambda e, pt=pt, n=n, tm=tm: e.tensor_tensor(
                    out=tm[:, 8:16], in0=pt[0:n, 8:16], in1=smv[0:n, 0:8], op=ALU.add),
                    reads=[pb, bC, b], writes=[b])
                P.op("act", lambda e, tm=tm: e.activation(out=tm[:, 8:16], in_=tm[:, 8:16], func=AF.Exp),
                     reads=[b], writes=[b])
                P.op("act", lambda e, tm=tm, n=n: e.activation(out=tm[:, 8:16], in_=tm[:, 8:16], func=AF.Ln,
                                                               bias=cst[0:n, 2:3], scale=1.0),
                     reads=[b, bC], writes=[b])
                P.op("dve", lambda e, tm=tm, n=n: e.tensor_tensor(out=tm[:, 8:16], in0=tm[:, 8:16], in1=negA[0:n, :],
                                                                  op=ALU.mult), reads=[b, bnegA], writes=[b])
                pt2, pb2 = bank()
                mk = masks[kind]
                P.op("pe", lambda e, pt2=pt2, n=n, tm=tm, mk=mk: e.matmul(
                    pt2[0:n, 0:8], mk[0:n, 2, 0:n], tm[:, 8:16], start=True, stop=True), reads=[b, bC], writes=[pb2])
                pt3, pb3 = bank()
                P.op("pe", lambda e, pt3=pt3, n=n, tm=tm, mk=mk: e.matmul(
                    pt3[0:n, 0:8], mk[0:n, 3, 0:n], tm[:, 8:16], start=True, stop=True), reads=[b, bC], writes=[pb3])
                P.op("dve", lambda e, pt2=pt2, n=n, tm=tm: e.tensor_copy(out=tm[:, 16:24], in_=pt2[0:n, 0:8]),
                     reads=[pb2, b], writes=[b])
                P.op("dve", lambda e, tm=tm: e.tensor_scalar(out=tm[:, 56:64], in0=tm[:, 16:24], scalar1=-1.0,
                                                             scalar2=None, op0=ALU.mult), reads=[b], writes=[b])
                P.op("act", lambda e, tm=tm: e.activation(out=tm[:, 32:40], in_=tm[:, 16:24], func=AF.Exp),
                     reads=[b], writes=[b])
                P.op("dve", lambda e, tm=tm: e.tensor_tensor(out=tm[:, 40:48], in0=tm[:, 32:40], in1=tm[:, 0:8],
                                                             op=ALU.mult), reads=[b], writes=[b])
                P.op("dve", lambda e, pt3=pt3, n=n, tm=tm: e.tensor_tensor(
                    out=tm[:, 48:56], in0=pt3[0:n, 0:8], in1=tm[:, 16:24], op=ALU.subtract),
                    reads=[pb3, b], writes=[b])
                P.op("act", lambda e, tm=tm: e.activation(out=tm[:, 48:56], in_=tm[:, 48:56], func=AF.Exp),
                     reads=[b], writes=[b])

        def fm_proj(wi, s):
            c0, n = segs[s]
            pt, pb = bank()
            P.op("pe", [lambda e, kc=kc, pt=pt, c0=c0, n=n: e.matmul(
                pt[:, 0:n], WU[wi][:, kc, :], hT[:, kc, c0:c0 + n],
                start=(kc == 0), stop=(kc == KC - 1)) for kc in range(KC)],
                reads=[bWU[wi]] + [bh[kc][s] for kc in range(KC)], writes=[pb])
            return pt, pb

        def cp(eng, out, in_, reads, writes):
            if eng == "act":
                P.op("act", lambda e: e.copy(out=out, in_=in_), reads=reads, writes=writes)
            else:
                P.op("dve", lambda e: e.tensor_copy(out=out, in_=in_), reads=reads, writes=writes)

        def mm1(out_ap, lhsT, rhs, reads, pb, **kw):
            P.op("pe", lambda e: e.matmul(out_ap, lhsT, rhs, start=True, stop=True, **kw), reads=reads, writes=[pb])

        def o_finish_gdn(hd, hl, c0, n, pt, pb):
            hl = hd
            ss = segs_of(c0, n)
            cp("dve", tl["osb"][:, 0:n], pt[:, 0:n], [pb], [btl["osb"]])
            P.op("act", lambda e: e.activation(out=tl["osq"].bitcast(BF16)[:, 0:n], in_=pt[:, 0:n], func=AF.Square),
                 reads=[pb], writes=[btl["osq"]])
            pt2, pb2 = bank()
            mm1(pt2[:, 0:n], onesb, tl["osq"].bitcast(BF16)[:, 0:n], [btl["osq"], bonesb], pb2)
            P.op("act", lambda e: e.activation(out=tl["rn"][:, 0:n], in_=pt2[:, 0:n], func=AF.Sqrt,
                                               bias=cst[:, 0:1], scale=1.0 / 128), reads=[pb2, bC],
                 writes=[btl["rn"]])
            P.op("dve", lambda e: e.reciprocal(out=tl["rn"][:, 0:n], in_=tl["rn"][:, 0:n]),
                 reads=[btl["rn"]], writes=[btl["rn"]])
            P.op("dve", lambda e: e.tensor_tensor(out=tl["on"][:, 0:n], in0=tl["osb"][:, 0:n], in1=tl["rn"][:, 0:n],
                                                  op=ALU.mult), reads=[btl["osb"], btl["rn"]], writes=[btl["on"]])
            P.op("dve", lambda e: e.scalar_tensor_tensor(
                out=oTall[:, hl, c0:c0 + n], in0=tl["on"][:, 0:n], scalar=smv[:, 16:17], in1=szT[:, c0:c0 + n],
                op0=ALU.mult, op1=ALU.mult), reads=[btl["on"], bC] + [bszT[s] for s in ss],
                writes=[boT[hl][s] for s in ss])

        def o_finish_ret(hd, hl, c0, n, pt, pb):
            hl = 8 + hd
            ss = segs_of(c0, n)
            cp("dve", tl["osb"][:, 0:n], pt[:, 0:n], [pb], [btl["osb"]])
            P.op("act", lambda e: e.activation(out=tl["osq"].bitcast(BF16)[:, 0:n], in_=pt[:, 0:n], func=AF.Square),
                 reads=[pb], writes=[btl["osq"]])
            pt2, pb2 = bank()
            mm1(pt2[:, 0:n], ones, tl["osb"][:, 0:n], [btl["osb"], bC], pb2)
            mm1(pt2[:, 128:128 + n], onesb, tl["osq"].bitcast(BF16)[:, 0:n], [btl["osq"], bonesb], pb2)
            P.op("act", lambda e: e.activation(out=tl["mu"][:, 0:n], in_=pt2[:, 0:n], func=AF.Identity,
                                               scale=1.0 / 128), reads=[pb2], writes=[btl["mu"]])
            P.op("dve", lambda e: e.tensor_tensor(out=tl["msq"][:, 0:n], in0=tl["mu"][:, 0:n], in1=tl["mu"][:, 0:n],
                                                  op=ALU.mult), reads=[btl["mu"]], writes=[btl["msq"]])
            P.op("dve", lambda e: e.scalar_tensor_tensor(
                out=tl["var"][:, 0:n], in0=pt2[:, 128:128 + n], scalar=1.0 / 128, in1=tl["msq"][:, 0:n],
                op0=ALU.mult, op1=ALU.subtract), reads=[pb2, btl["msq"]], writes=[btl["var"]])
            P.op("dve", lambda e: e.tensor_scalar(out=tl["var"][:, 0:n], in0=tl["var"][:, 0:n], scalar1=0.0,
                                                  scalar2=None, op0=ALU.max), reads=[btl["var"]], writes=[btl["var"]])
            P.op("act", lambda e: e.activation(out=tl["rn"][:, 0:n], in_=tl["var"][:, 0:n], func=AF.Sqrt,
                                               bias=cst[:, 0:1], scale=1.0), reads=[btl["var"], bC],
                 writes=[btl["rn"]])
            P.op("dve", lambda e: e.reciprocal(out=tl["rn"][:, 0:n], in_=tl["rn"][:, 0:n]),
                 reads=[btl["rn"]], writes=[btl["rn"]])
            P.op("dve", lambda e: e.tensor_tensor(out=tl["cen"][:, 0:n], in0=tl["osb"][:, 0:n], in1=tl["mu"][:, 0:n],
                                                  op=ALU.subtract), reads=[btl["osb"], btl["mu"]], writes=[btl["cen"]])
            P.op("dve", lambda e: e.tensor_tensor(out=tl["on"][:, 0:n], in0=tl["cen"][:, 0:n], in1=tl["rn"][:, 0:n],
                                                  op=ALU.mult), reads=[btl["cen"], btl["rn"]], writes=[btl["on"]])
            P.op("act", lambda e: e.activation(out=tl["on"][:, 0:n], in_=tl["on"][:, 0:n], func=AF.Identity,
                                               scale=smv[:, 17 + hd:18 + hd], bias=gnbT[:, hd:hd + 1]),
                 reads=[btl["on"], bC], writes=[btl["on"]])
            P.op("dve", lambda e: e.tensor_tensor(out=oTall[:, hl, c0:c0 + n], in0=tl["on"][:, 0:n],
                                                  in1=szT[:, c0:c0 + n], op=ALU.mult),
                 reads=[btl["on"]] + [bszT[s] for s in ss], writes=[boT[hl][s] for s in ss])

        def recur_prompt(S, bS, qdec, bqdec, QKT, bQKT, kdec, bkdec, vsrc, bvsrc, glast_ap_fn, greads,
                         wT=None, bwT=None, u=None, bu=None):
            pto, pbo = bank_o()
            for c in range(2):
                r0 = 64 * c
                if wT is not None:
                    ptw, pbw = bank()
                    mm1(ptw[:, 0:128], wT[:, 0:128], S, [bwT, bS], pbw)
                    P.op("dve", lambda e, r0=r0, ptw=ptw: e.tensor_tensor(
                        out=tl["vnew"][r0:r0 + 64, :], in0=u[r0:r0 + 64, :], in1=ptw[r0:r0 + 64, 0:128],
                        op=ALU.subtract), reads=[pbw, bu], writes=[btl["vnew"]])
                    v_, bv_ = tl["vnew"], btl["vnew"]
                else:
                    v_, bv_ = vsrc, bvsrc
                P.op("pe", [lambda e, r0=r0: e.matmul(pto[:, r0:r0 + 64], S, qdec[:, r0:r0 + 64],
                                                      start=True, stop=False, skip_group_check=True),
                            lambda e, r0=r0, v_=v_: e.matmul(pto[:, r0:r0 + 64], v_[r0:r0 + 64, :],
                                                             QKT[r0:r0 + 64, r0:r0 + 64],
                                                             start=False, stop=True, skip_group_check=True)],
                     reads=[bS, bqdec, bv_, bQKT], writes=[pbo])
                pts, pbs = bank()
                mm1(pts[:, 0:128], kdec[r0:r0 + 64, :], v_[r0:r0 + 64, :], [bkdec, bv_], pbs)
                P.op("dve", lambda e, r0=r0, pts=pts: e.scalar_tensor_tensor(
                    out=S, in0=S, scalar=glast_ap_fn(r0 + 63), in1=pts[:, 0:128], op0=ALU.mult, op1=ALU.add),
                    reads=[pbs, bS] + greads, writes=[bS])
            return pto, pbo

        def recur_sample(p, hd, sdram_in, sdram_out, qdec, bqdec, QKT, bQKT, kdec, bkdec, vsrc, bvsrc,
                         glast3, greads, wT=None, bwT=None, u=None, bu=None):
            n = TS
            P.cur_atomic = ("samp", CL[0].id, hd, id(sdram_in))
            P.dma("sp", Sall, sdram_in[p, hd], writes=[bSall], stream="Sall")
            nh = (SPP * 128) // 512
            Sflat = Sall.rearrange("p s d -> p (s d)")
            if wT is not None:
                for hh in range(nh):
                    ptw, pbw = bank()
                    mm1(ptw[0:n, :], wT[:, 0:n], Sflat[:, hh * 512:(hh + 1) * 512], [bwT, bSall], pbw)
                    P.op("dve", lambda e, ptw=ptw, hh=hh: e.tensor_tensor(
                        out=wsel[0:n, hh * 4:(hh + 1) * 4, :], in0=ptw[0:n, :].rearrange("p (s d) -> p s d", s=4),
                        in1=onehot[0:n, hh * 4:(hh + 1) * 4].unsqueeze(2).to_broadcast([n, 4, 128]), op=ALU.mult),
                        reads=[pbw, bC] + ([bwsel] if hh else []), writes=[bwsel])
                P.op("dve", lambda e: e.tensor_reduce(
                    out=tl["on"][0:n, :], in_=wsel[0:n, :, :].rearrange("p s d -> p d s"), axis=AX.X, op=ALU.add),
                    reads=[bwsel], writes=[btl["on"]])
                P.op("dve", lambda e: e.tensor_tensor(out=tl["vnew"][0:n, :], in0=u[0:n, :], in1=tl["on"][0:n, :],
                                                      op=ALU.subtract), reads=[btl["on"], bu], writes=[btl["vnew"]])
                v_, bv_ = tl["vnew"], btl["vnew"]
            else:
                v_, bv_ = vsrc, bvsrc
            pto, pbo = bank_o()
            fns = []
            for s in range(SPP):
                fns.append(lambda e, s=s: e.matmul(
                    ts(pto[:, 0:n])[:, :, s], Sall[:, s, :], ts(qdec[:, 0:n])[:, :, s],
                    start=(s == 0), stop=False, skip_group_check=True))
            fns.append(lambda e: e.matmul(pto[:, 0:n], v_[0:n, :], QKT[0:n, 0:n], start=False, stop=True,
                                          skip_group_check=True))
            P.op("pe", fns, reads=[bSall, bqdec, bv_, bQKT], writes=[pbo])
            P.op("dve", lambda e: e.tensor_tensor(
                out=vexp[0:n, :, :], in0=v_[0:n, :].unsqueeze(1).to_broadcast([n, SPP, 128]),
                in1=onehot[0:n, :].unsqueeze(2).to_broadcast([n, SPP, 128]), op=ALU.mult),
                reads=[bv_, bC], writes=[bvexp])
            P.op("dve", lambda e: e.tensor_tensor(out=Sall, in0=Sall, in1=glast3, op=ALU.mult),
                 reads=[bSall] + greads, writes=[bSall])
            vflat = vexp.rearrange("p s d -> p (s d)")
            for hh in range(nh):
                pts, pbs = bank()
                mm1(pts[:, :], kdec[0:n, :], vflat[0:n, hh * 512:(hh + 1) * 512], [bkdec, bvexp], pbs)
                P.op("dve", lambda e, pts=pts, hh=hh: e.tensor_tensor(
                    out=Sflat[:, hh * 512:(hh + 1) * 512], in0=Sflat[:, hh * 512:(hh + 1) * 512], in1=pts[:, :],
                    op=ALU.add), reads=[pbs, bSall], writes=[bSall])
            P.dma("sp", sdram_out[p, hd], Sall, reads=[bSall], stream="Sall_o")
            P.cur_atomic = None
            return pto, pbo

        swi = [0]

        def state_in(p, idx):
            L = CL[0]
            if p == 0:
                P.op("dve", lambda e: e.memset(L.Sw, 0.0), writes=[L.bSw])
            else:
                P.dma("sp", L.Sw, sscr[idx], reads=[bsscr[idx]], writes=[L.bSw], stream="Sw")
            return L.Sw, L.bSw, L.id

        def state_out(p, idx, i, out_ap):
            L = CL[0]
            if p == NPASS - 1:
                P.dma("sp", out_ap, L.Sw, reads=[L.bSw], stream="Sw_o")
            else:
                P.dma("sp", sscr[idx], L.Sw, reads=[L.bSw], writes=[bsscr[idx]], stream="Sw_o")

        def gdn_head(p, hd):
            hl = hd % 4
            wq = load_wu(w_in[hd * 4 + 0])
            wk = load_wu(w_in[hd * 4 + 1])
            wv = load_wu(w_in[hd * 4 + 2])
            wqkv = [wq, wk, wv]
            S, bS, si = state_in(p, hd)
            r0c = TPP - 3
            nr = 3 + TS
            ptc, pbc = bank()
            for comp in range(3):
                P.op("pe", [lambda e, kc=kc, comp=comp: e.matmul(
                    ptc[0:nr, comp * 128:(comp + 1) * 128], hT[:, kc, r0c:T], WU[wqkv[comp]][:, kc, :],
                    start=(kc == 0), stop=(kc == KC - 1), skip_group_check=True) for kc in range(KC)],
                    reads=[bWU[wqkv[comp]]] + [bh[kc][NSEG - 1] for kc in range(KC)], writes=[pbc])
            cp("act", crow[0:nr, :], ptc[0:nr, 0:384], [pbc], [bcrow])
            if p == NPASS - 1:
                P.dma("sp", conv_p.rearrange("i (c h x) -> i c h x", c=3, h=8)[:, :, hd, :],
                      crow[0:3, :].rearrange("i (c x) -> i c x", c=3), reads=[bcrow], stream="crow_o")
            for j in range(3):
                P.dma("sp", conv_s[p, j].rearrange("s (c h x) -> s c h x", c=3, h=8)[:, :, hd, :],
                      crow[3 + (1 + j) * SPP:3 + (2 + j) * SPP, :].rearrange("s (c x) -> s c x", c=3),
                      reads=[bcrow], stream="crow_o")
            for comp in range(3):
                wi = wqkv[comp]
                ch = comp * 8 + hd
                P.op("dve", lambda e, ch=ch: e.tensor_copy(out=cb[:, 0:3], in_=ctail[:, ch, :]),
                     reads=[bctail[ch]], writes=[bcb])
                P.dma("sp", cbs[:, 0:3 * SPP], sconv[p, ch].rearrange("c i s -> c (i s)"), writes=[bcbs],
                      stream="cbs")
                for s, (c0, n) in enumerate(segs):
                    pt, pb = fm_proj(wi, s)
                    npm = min(c0 + n, TPP) - c0
                    if npm > 0:
                        cp("act", cb[:, 3 + c0:3 + c0 + npm], pt[:, 0:npm], [pb, bcb], [bcb])
                    if npm < n:
                        cp("dve", cbs[:, 3 * SPP:7 * SPP], pt[:, npm:n], [pb, bcbs], [bcbs])
                P.op("dve", lambda e, ch=ch: e.tensor_copy(out=ctail[:, ch, :], in_=cb[:, TPP:TPP + 3]),
                     reads=[bcb], writes=[bctail[ch]])
                for (dst, src, width, bsrc, step) in ((cvt[:, 0:TPP], cb, TPP, bcb, 1),
                                                      (cvt[:, TPP:T], cbs, TS, bcbs, SPP)):
                    P.op("dve", lambda e, dst=dst, src=src, width=width, ch=ch: e.tensor_scalar(
                        out=dst, in0=src[:, 0:width], scalar1=cwT[:, ch, 0:1], scalar2=None, op0=ALU.mult),
                        reads=[bsrc, bC, bcvt], writes=[bcvt])
                    for i in range(1, 4):
                        P.op("dve", lambda e, dst=dst, src=src, width=width, ch=ch, i=i, step=step:
                             e.scalar_tensor_tensor(out=dst, in0=src[:, i * step:i * step + width],
                                                    scalar=cwT[:, ch, i:i + 1], in1=dst, op0=ALU.mult, op1=ALU.add),
                             reads=[bsrc, bC, bcvt], writes=[bcvt])
                P.op("act", lambda e, comp=comp: e.activation(out=FM[comp], in_=cvt, func=AF.Silu),
                     reads=[bcvt], writes=bFM[comp])
            wz = load_wu(w_in[hd * 4 + 3])
            for s, (c0, n) in enumerate(segs):
                pt, pb = fm_proj(wz, s)
                P.op("act", lambda e, pt=pt, c0=c0, n=n: e.activation(out=szT[:, c0:c0 + n], in_=pt[:, 0:n],
                                                                      func=AF.Silu), reads=[pb], writes=[bszT[s]])
            for comp in range(2):
                for s, (c0, n) in enumerate(segs):
                    j = s % 2
                    P.op("act", lambda e, comp=comp, j=j, c0=c0, n=n: e.activation(
                        out=sq[j].bitcast(BF16)[:, 0:n], in_=FM[comp][:, c0:c0 + n], func=AF.Square),
                        reads=[bFM[comp][s]], writes=[bsq[j]])
                    pt, pb = bank()
                    mm1(pt[:, 0:n], onesb, sq[j].bitcast(BF16)[:, 0:n], [bsq[j], bonesb], pb)
                    P.op("act", lambda e, comp=comp, j=j, n=n, pt=pt: e.activation(
                        out=tmpf[j][:, 0:n], in_=pt[:, 0:n], func=AF.Sqrt,
                        bias=(cst[:, 1:2] if comp == 0 else cst[:, 0:1]), scale=(128.0 if comp == 0 else 1.0)),
                        reads=[pb, bC], writes=[btmpf[j]])
                    P.op("dve", lambda e, j=j, n=n: e.reciprocal(out=tmpf[j][:, 0:n], in_=tmpf[j][:, 0:n]),
                         reads=[btmpf[j]], writes=[btmpf[j]])
                    P.op("dve", lambda e, comp=comp, j=j, c0=c0, n=n: e.tensor_tensor(
                        out=FM[comp][:, c0:c0 + n], in0=FM[comp][:, c0:c0 + n], in1=tmpf[j][:, 0:n], op=ALU.mult),
                        reads=[btmpf[j], bFM[comp][s]], writes=[bFM[comp][s]])
            qT, kT, vT = FM
            for ti, (kind, n, c0) in enumerate(tile_list()):
                ss = segs_of(c0, n)
                rq = [bFM[0][s] for s in ss]
                rk = [bFM[1][s] for s in ss]
                rv = [bFM[2][s] for s in ss]
                mk = masks[kind]
                tm = TM[0:n, ti, :]
                btm = bTM[ti]
                ptr, pbr = bank()
                mm1(ptr[:, 0:n], tm[:, 8 + hd:9 + hd].to_broadcast([n, 128]), mk[0:n, 2, 0:n], [btm, bC], pbr)
                P.op("dve", lambda e, ptr=ptr, n=n, mk=mk: e.tensor_tensor(
                    out=tl["a1"][0:n, 0:n], in0=ptr[0:n, 0:n], in1=mk[0:n, 0, 0:n], op=ALU.add),
                    reads=[pbr, bC], writes=[btl["a1"]])
                P.op("act", lambda e, n=n, tm=tm: e.activation(out=tl["DTi"][0:n, 0:n], in_=tl["a1"][0:n, 0:n],
                                                               func=AF.Exp, bias=tm[:, 56 + hd:57 + hd], scale=1.0),
                     reads=[btl["a1"], btm], writes=[btl["DTi"]])
                P.op("dve", lambda e, ptr=ptr, n=n, mk=mk: e.scalar_tensor_tensor(
                    out=tl["a1"][0:n, 0:n], in0=ptr[0:n, 0:n], scalar=-1.0, in1=mk[0:n, 1, 0:n],
                    op0=ALU.mult, op1=ALU.add), reads=[pbr, bC, btl["a1"]], writes=[btl["a1"]])
                P.op("act", lambda e, n=n, tm=tm: e.activation(out=tl["Dst"][0:n, 0:n], in_=tl["a1"][0:n, 0:n],
                                                               func=AF.Exp, bias=tm[:, 16 + hd:17 + hd], scale=1.0),
                     reads=[btl["a1"], btm], writes=[btl["Dst"]])
                P.op("act", lambda e, ptr=ptr, n=n: e.activation(out=tl["Egc"][:, 0:n], in_=ptr[:, 0:n], func=AF.Exp),
                     reads=[pbr], writes=[btl["Egc"]])
                ptg, pbg = bank()
                mm1(ptg[0:n, 0:n], kT[:, c0:c0 + n], kT[:, c0:c0 + n], rk, pbg)
                mm1(ptg[0:n, 128:128 + n], kT[:, c0:c0 + n], qT[:, c0:c0 + n], rk + rq, pbg)
                P.op("dve", lambda e, ptg=ptg, n=n, tm=tm: e.scalar_tensor_tensor(
                    out=tl["L"][0:n, 0:n], in0=ptg[0:n, 0:n], scalar=tm[:, hd:hd + 1], in1=tl["Dst"][0:n, 0:n],
                    op0=ALU.mult, op1=ALU.mult), reads=[pbg, btm, btl["Dst"]], writes=[btl["L"]])
                P.op("dve", lambda e, ptg=ptg, n=n: e.tensor_tensor(
                    out=tl["QKT"][0:n, 0:n], in0=ptg[0:n, 128:128 + n], in1=tl["DTi"][0:n, 0:n], op=ALU.mult),
                    reads=[pbg, btl["DTi"]], writes=[btl["QKT"]])
                ptt, pbt = bank()
                P.op("pe", lambda e, ptt=ptt, n=n: e.transpose(ptt[0:n, 0:n], tl["L"][0:n, 0:n], ident[0:n, 0:n]),
                     reads=[btl["L"], bC], writes=[pbt])
                cp("act", tl["LT"][0:n, 0:n], ptt[0:n, 0:n], [pbt], [btl["LT"]])
                P.op("dve", lambda e, n=n: e.tensor_tensor(out=tl["M"][0:n, 0:n], in0=ident[0:n, 0:n],
                                                           in1=tl["LT"][0:n, 0:n], op=ALU.subtract),
                     reads=[btl["LT"], bC], writes=[btl["M"]])
                nlev = 5 if kind == "P" else 1
                Pc, PTc = "L", "LT"
                for lv in range(nlev):
                    last = lv == nlev - 1
                    pa, pba = bank()
                    mm1(pa[0:n, 0:n], tl[PTc][0:n, 0:n], tl[Pc][0:n, 0:n], [btl[Pc], btl[PTc]], pba)
                    if not last:
                        mm1(pa[0:n, 128:128 + n], tl[Pc][0:n, 0:n], tl[PTc][0:n, 0:n], [btl[Pc], btl[PTc]], pba)
                    nP = "P" if Pc == "L" else "L"
                    nPT = "PT" if PTc == "LT" else "LT"
                    cp("act", tl[nP][0:n, 0:n], pa[0:n, 0:n], [pba], [btl[nP]])
                    if not last:
                        cp("dve", tl[nPT][0:n, 0:n], pa[0:n, 128:128 + n], [pba], [btl[nPT]])
                        PTc = nPT
                    Pc = nP
                    pm, pbm = bank()
                    mm1(pm[0:n, 0:n], tl[Pc][0:n, 0:n], tl["M"][0:n, 0:n], [btl[Pc], btl["M"]], pbm)
                    P.op("dve", lambda e, pm=pm, n=n: e.tensor_tensor(
                        out=tl["M"][0:n, 0:n], in0=tl["M"][0:n, 0:n], in1=pm[0:n, 0:n], op=ALU.add),
                        reads=[pbm, btl["M"]], writes=[btl["M"]])
                pk, pbk = bank()
                P.op("pe", lambda e, pk=pk, n=n, c0=c0: e.transpose(pk[0:n, 0:128], kT[:, c0:c0 + n], ident),
                     reads=rk + [bC], writes=[pbk])
                P.op("pe", lambda e, pk=pk, n=n, c0=c0: e.transpose(pk[0:n, 128:256], vT[:, c0:c0 + n], ident),
                     reads=rv + [bC], writes=[pbk])
                P.op("act", lambda e, pk=pk, n=n, tm=tm: e.activation(
                    out=tl["kbe"][0:n, :], in_=pk[0:n, 0:128], func=AF.Identity, scale=tm[:, 40 + hd:41 + hd]),
                    reads=[pbk, btm], writes=[btl["kbe"]])
                P.op("dve", lambda e, pk=pk, n=n, tm=tm: e.tensor_scalar(
                    out=tl["kdec"][0:n, :], in0=pk[0:n, 0:128], scalar1=tm[:, 48 + hd:49 + hd], scalar2=None,
                    op0=ALU.mult), reads=[pbk, btm], writes=[btl["kdec"]])
                P.op("dve", lambda e, pk=pk, n=n, tm=tm: e.tensor_scalar(
                    out=tl["vb"][0:n, :], in0=pk[0:n, 128:256], scalar1=tm[:, hd:hd + 1], scalar2=None,
                    op0=ALU.mult), reads=[pbk, btm], writes=[btl["vb"]])
                pu, pbu = bank()
                mm1(pu[0:n, 0:128], tl["M"][0:n, 0:n], tl["vb"][0:n, :], [btl["M"], btl["vb"]], pbu)
                mm1(pu[:, 128:128 + n], tl["kbe"][0:n, :], tl["M"][0:n, 0:n], [btl["M"], btl["kbe"]], pbu)
                cp("act", tl["u"][0:n, :], pu[0:n, 0:128], [pbu], [btl["u"]])
                cp("dve", tl["wT"][:, 0:n], pu[:, 128:128 + n], [pbu], [btl["wT"]])
                P.op("dve", lambda e, n=n, c0=c0: e.tensor_tensor(out=tl["qdec"][:, 0:n], in0=qT[:, c0:c0 + n],
                                                                  in1=tl["Egc"][:, 0:n], op=ALU.mult),
                     reads=rq + [btl["Egc"]], writes=[btl["qdec"]])
                if kind == "P":
                    pto, pbo = recur_prompt(S, bS, tl["qdec"], btl["qdec"], tl["QKT"], btl["QKT"],
                                            tl["kdec"], btl["kdec"], None, None,
                                            lambda col: tl["Egc"][:, col:col + 1], [btl["Egc"]],
                                            wT=tl["wT"], bwT=btl["wT"], u=tl["u"], bu=btl["u"])
                else:
                    pto, pbo = recur_sample(p, hd, sgdn, gdn_s, tl["qdec"], btl["qdec"], tl["QKT"], btl["QKT"],
                                            tl["kdec"], btl["kdec"], None, None,
                                            tl["Egc"][:, 3 * SPP:4 * SPP].unsqueeze(2).to_broadcast([128, SPP, 128]),
                                            [btl["Egc"]], wT=tl["wT"], bwT=btl["wT"], u=tl["u"], bu=btl["u"])
                o_finish_gdn(hd, hl, c0, n, pto, pbo)
            state_out(p, hd, si, gdn_p[hd])

        def ret_head(p, hd):
            hl = hd % 4
            wq = load_wu(w_in[32 + hd * 4 + 0])
            wk = load_wu(w_in[32 + hd * 4 + 1])
            wv = load_wu(w_in[32 + hd * 4 + 2])
            S, bS, si = state_in(p, 8 + hd)
            P.dma("sp", rdm, crdm["P"][hd], writes=[brdm], stream="rdm")
            P.dma("sp", rqd, crqd["P"][hd], writes=[brqd], stream="rqd")
            for comp, wi in ((0, wq), (1, wk)):
                for s, (c0, n) in enumerate(segs):
                    pt, pb = fm_proj(wi, s)
                    j = s % 2
                    cp("act", tmpf[j][:, 0:n], pt[:, 0:n], [pb], [btmpf[j]])
                    pt2, pb2 = bank()
                    mm1(pt2[:, 0:n], rsw, tmpf[j][:, 0:n], [btmpf[j], bC], pb2)
                    P.op("dve", lambda e, j=j, c0=c0, n=n, pt2=pt2: e.tensor_tensor(
                        out=sq[j][:, 0:n], in0=pt2[:, 0:n], in1=rot[:, 1, c0:c0 + n], op=ALU.mult),
                        reads=[pb2, brot], writes=[bsq[j]])
                    P.op("dve", lambda e, j=j, c0=c0, n=n: e.tensor_tensor(
                        out=tmpf[j][:, 0:n], in0=tmpf[j][:, 0:n], in1=rot[:, 0, c0:c0 + n], op=ALU.mult),
                        reads=[btmpf[j], brot], writes=[btmpf[j]])
                    P.op("dve", lambda e, j=j, c0=c0, n=n, comp=comp: e.tensor_tensor(
                        out=FM[comp][:, c0:c0 + n], in0=tmpf[j][:, 0:n], in1=sq[j][:, 0:n], op=ALU.add),
                        reads=[btmpf[j], bsq[j]], writes=[bFM[comp][s]])
            wgt = load_wu(w_in[32 + hd * 4 + 3])
            for s, (c0, n) in enumerate(segs):
                pt, pb = fm_proj(wgt, s)
                P.op("act", lambda e, pt=pt, c0=c0, n=n: e.activation(out=szT[:, c0:c0 + n], in_=pt[:, 0:n],
                                                                      func=AF.Silu), reads=[pb], writes=[bszT[s]])
            qT, kT = FM[0], FM[1]
            for ti, (kind, n, c0) in enumerate(tile_list()):
                ss = segs_of(c0, n)
                rq = [bFM[0][s] for s in ss]
                rk = [bFM[1][s] for s in ss]
                if kind == "S":
                    P.dma("sp", rdm, crdm["S"][hd], writes=[brdm], stream="rdm")
                    P.dma("sp", rqd, crqd["S"][hd], writes=[brqd], stream="rqd")
                pv, pbv = bank()
                P.op("pe", [lambda e, kc=kc, pv=pv, c0=c0, n=n: e.matmul(
                    pv[0:n, 0:128], hT[:, kc, c0:c0 + n], WU[wv][:, kc, :], start=(kc == 0), stop=(kc == KC - 1))
                    for kc in range(KC)],
                    reads=[bWU[wv]] + [bh[kc][s] for kc in range(KC) for s in ss], writes=[pbv])
                cp("act", tl["vTM"][0:n, :], pv[0:n, 0:128], [pbv], [btl["vTM"]])
                ptg, pbg = bank()
                mm1(ptg[0:n, 0:n], kT[:, c0:c0 + n], qT[:, c0:c0 + n], rk + rq, pbg)
                P.op("dve", lambda e, ptg=ptg, n=n: e.tensor_tensor(
                    out=tl["QKT"][0:n, 0:n], in0=ptg[0:n, 0:n], in1=rdm[0:n, 0:n], op=ALU.mult),
                    reads=[pbg, brdm], writes=[btl["QKT"]])
                pk, pbk = bank()
                P.op("pe", lambda e, pk=pk, n=n, c0=c0: e.transpose(pk[0:n, 0:128], kT[:, c0:c0 + n], ident),
                     reads=rk + [bC], writes=[pbk])
                P.op("dve", lambda e, pk=pk, n=n, kind=kind: e.tensor_scalar(
                    out=tl["kdec"][0:n, :], in0=pk[0:n, 0:128], scalar1=rkd[kind][0:n, hd:hd + 1], scalar2=None,
                    op0=ALU.mult), reads=[pbk, bC], writes=[btl["kdec"]])
                P.op("dve", lambda e, n=n, c0=c0: e.tensor_tensor(out=tl["qdec"][:, 0:n], in0=qT[:, c0:c0 + n],
                                                                  in1=rqd[:, 0:n], op=ALU.mult),
                     reads=rq + [brqd], writes=[btl["qdec"]])
                if kind == "P":
                    pto, pbo = recur_prompt(S, bS, tl["qdec"], btl["qdec"], tl["QKT"], btl["QKT"],
                                            tl["kdec"], btl["kdec"], tl["vTM"], btl["vTM"],
                                            lambda col: rgc["P"][:, hd:hd + 1], [bC])
                else:
                    pto, pbo = recur_sample(p, hd, sret, ret_s, tl["qdec"], btl["qdec"], tl["QKT"], btl["QKT"],
                                            tl["kdec"], btl["kdec"], tl["vTM"], btl["vTM"],
                                            rgc["S"][:, hd:hd + 1].unsqueeze(2).to_broadcast([128, SPP, 128]), [bC])
                o_finish_ret(hd, hl, c0, n, pto, pbo)
            state_out(p, 8 + hd, si, ret_p[hd])

        def out_proj16(p):
            for dc in range(KC):
                wi = load_wu(w_out[dc])
                for s, (c0, n) in enumerate(segs):
                    pt, pb = bank()
                    P.op("pe", [lambda e, k=k, pt=pt, c0=c0, n=n, wi=wi: e.matmul(
                        pt[:, 0:n], WU[wi][:, k, :], oTall[:, k, c0:c0 + n],
                        start=(k == 0), stop=(k == 15)) for k in range(16)],
                        reads=[bWU[wi]] + [boT[k][s] for k in range(16)], writes=[pb])
                    gate_evac(pt, pb, p, dc, s, g_off[1])

        def run_pair(f0, f1, stagger=0):
            recs = []
            for li, f in enumerate((f0, f1)):
                CL[0] = lanes[li]
                REC[0] = True
                P.rec_target = []
                P.rec_suffix = f"_L{li}"
                f()
                recs.append(P.rec_target)
                P.rec_target = None
                REC[0] = False
            CL[0] = lanes[0]
            merged = []
            idx = [0, 0]

            def take(li):
                r = recs[li]
                if idx[li] >= len(r):
                    return
                a = r[idx[li]][5]
                merged.append(r[idx[li]])
                idx[li] += 1
                if a is not None:
                    while idx[li] < len(r) and r[idx[li]][5] == a:
                        merged.append(r[idx[li]])
                        idx[li] += 1

            while idx[0] < min(stagger, len(recs[0])):
                take(0)
            while idx[0] < len(recs[0]) or idx[1] < len(recs[1]):
                take(0)
                take(1)
            P.replay(merged)

        def mixer(p):
            phase_barrier()
            P.dma("sp", xscr, xT, reads=bx_all, writes=[bxscr], stream="xscr_o")
            tm_scalars(p)
            stage(3.1)
            phase_barrier()
            nops = []

            def lane_heads(fn, hs):
                def f():
                    n0 = len(P.rec_target)
                    for h in hs:
                        fn(p, h)
                    nops.append(len(P.rec_target) - n0)
                return f

            if STAGGER:
                run_pair(lane_heads(gdn_head, (0, 2, 4, 6)), lane_heads(gdn_head, (1, 3, 5, 7)), stagger=STAGGER_G)
            else:
                for pr in range(4):
                    run_pair(lambda: gdn_head(p, 2 * pr), lambda: gdn_head(p, 2 * pr + 1))
            stage(3.8)
            phase_barrier()
            stage(3.9)
            P.dma("sp", rot, crot[p].rearrange("c p t -> p c t"), writes=[brot], stream="rot")
            if STAGGER:
                run_pair(lane_heads(ret_head, (0, 2, 4, 6)), lane_heads(ret_head, (1, 3, 5, 7)), stagger=STAGGER_R)
            else:
                for pr in range(4):
                    run_pair(lambda: ret_head(p, 2 * pr), lambda: ret_head(p, 2 * pr + 1))
            if p == 0 and STAGE is None and False:
                print("lane op counts", nops)
            phase_barrier()
            P.dma("sp", xT, xscr, reads=[bxscr, bPH], writes=bx_all, stream="xin")
            ada_group(g_off[1] // 16)
            out_proj16(p)

        try:
            stage(0)
            for p in range(NPASS):
                P.dma("sp", xT, xin[p], writes=[b for r in bx for b in r], stream="xin")
                norm_mod(p, 0)
                stage(1)
                ffn(p, 0)
                dbg("x1", xT, [b for r in bx for b in r])
                stage(2)
                norm_mod(p, 1)
                stage(3)
                mixer(p)
                dbg("x2", xT, [b for r in bx for b in r])
                stage(4)
                norm_mod(p, 2)
                ffn(p, 1)
                stage(5)
                norm_mod(p, 3, final=True)
                stage(6 + p)
        except _Stop:
            pass
        dbg("ada", adaT, bada)

        P.emit()
    return nc, ins_spec, outs_spec


_CACHE = {}


def _get_program():
    if "nc" not in _CACHE:
        _CACHE["nc"] = build_program()
    return _CACHE["nc"]


def kernel(x_prompt, x_sample, state_gdn, state_conv, state_ret, c_prompt, c_sample,
           w_ada, b_ada, norm_ffn1, w1_gate, w1_up, w1_down, norm_mix, w_in, conv_w, a_log, dt_bias,
           gdn_norm_w, ret_gn_w, ret_gn_b, w_out, norm_ffn2, w2_gate, w2_up, w2_down,
           w_ada_final, b_ada_final, norm_final):
    f = lambda a: np.asarray(a, dtype=np.float32)
    x_prompt, x_sample = f(x_prompt), f(x_sample)
    state_gdn, state_conv, state_ret = f(state_gdn)[0], f(state_conv)[0], f(state_ret)[0]
    c_prompt, c_sample = f(c_prompt), f(c_sample)
    nc, ins_spec, outs_spec = _get_program()
    consts = host_consts()

    shared = {}
    shared["w_ada"] = np.concatenate([tile_w(f(w_ada)[0]), tile_w(f(w_ada_final))], 0)
    ball = np.concatenate([f(b_ada)[0], f(b_ada_final)])
    shared["b_ada"] = np.ascontiguousarray(ball.reshape(176, 128).T)
    nrm = np.stack([f(norm_ffn1)[0], f(norm_mix)[0], f(norm_ffn2)[0], f(norm_final)], 0)
    shared["normw"] = np.ascontiguousarray(nrm.reshape(4, KC, 128).transpose(2, 0, 1))
    shared["w1g"], shared["w1u"] = tile_w(f(w1_gate)[0]), tile_w(f(w1_up)[0])
    shared["w2g"], shared["w2u"] = tile_w(f(w2_gate)[0]), tile_w(f(w2_up)[0])
    shared["w1d"] = np.ascontiguousarray(f(w1_down)[0].reshape(NFC, 128, D))
    shared["w2d"] = np.ascontiguousarray(f(w2_down)[0].reshape(NFC, 128, D))
    win = f(w_in)[0]
    cols = []
    for hd in range(8):
        for blk in (0, 1, 2, 3):
            o = blk * 1024 + hd * 128
            cols.append(np.arange(o, o + 128))
    o5 = 3072 + 1024 + 16
    for hd in range(8):
        for blk in (0, 1, 2, 3):
            o = o5 + blk * 1024 + hd * 128
            cols.append(np.arange(o, o + 128))
    cols = np.concatenate(cols)
    shared["w_in"] = tile_w(win[:, cols])
    wba_ = win[:, 4096:4112]
    shared["w_ba"] = np.ascontiguousarray(wba_.reshape(KC, 128, 16).transpose(1, 0, 2))
    wo = f(w_out)[0]
    shared["w_out"] = tile_w(wo)
    shared["convw"] = np.ascontiguousarray(f(conv_w)[0].reshape(4, 24, 128).transpose(2, 1, 0))
    smallv = np.zeros((128, 32), np.float32)
    smallv[:, 0:8] = f(dt_bias)[0][None, :]
    smallv[:, 8:16] = f(a_log)[0][None, :]
    smallv[:, 16] = f(gdn_norm_w)[0]
    smallv[:, 17:25] = f(ret_gn_w)[0].reshape(8, 128).T
    shared["smallv"] = smallv
    shared["gnb"] = np.ascontiguousarray(f(ret_gn_b)[0].reshape(8, 128).T)
    for k in ("ident", "ones", "rsw", "maskP", "maskS", "rdmP", "rdmS", "rqdP", "rqdS", "rkdP", "rkdS",
              "rgcP", "rgcS", "onehot", "rot", "cst"):
        shared[k] = consts[k]

    in_maps = []
    for c in range(NCORES):
        b = c % 4
        m = dict(shared)
        xin = np.zeros((NPASS, T, D), np.float32)
        sq0 = c * SPC
        for p in range(NPASS):
            xin[p, :TPP] = x_prompt[b, p * TPP:(p + 1) * TPP]
            xs = x_sample[sq0 + p * SPP: sq0 + (p + 1) * SPP]
            xin[p, TPP:] = xs.transpose(1, 0, 2).reshape(TS, D)
        m["xin"] = np.ascontiguousarray(xin.reshape(NPASS, T, KC, 128).transpose(0, 3, 2, 1))
        cvec = np.concatenate([c_prompt[b:b + 1], c_sample[sq0:sq0 + SPC]], 0)
        m["cv"] = np.ascontiguousarray(cvec.reshape(17, KC, 128).transpose(2, 1, 0))
        sg_ = state_gdn[sq0:sq0 + SPC].reshape(NPASS, SPP, 8, 128, 128)
        m["sgdn"] = np.ascontiguousarray(sg_.transpose(0, 2, 3, 1, 4))
        sr_ = state_ret[sq0:sq0 + SPC].reshape(NPASS, SPP, 8, 128, 128)
        m["sret"] = np.ascontiguousarray(sr_.transpose(0, 2, 3, 1, 4))
        sc_ = state_conv[sq0:sq0 + SPC].reshape(NPASS, SPP, 3, 24, 128)
        m["sconv"] = np.ascontiguousarray(sc_.transpose(0, 3, 4, 2, 1))
        for k, shp in ins_spec.items():
            assert m[k].shape == shp, (k, m[k].shape, shp)
            assert m[k].dtype == np.float32
        in_maps.append({k: m[k] for k in ins_spec})

    res = run_bass_kernel_spmd(nc, in_maps, core_ids=list(range(NCORES)))
    R = res.results
    _CACHE["last"] = R

    y_prompt = np.zeros((4, 2048, D), np.float32)
    y_sample = np.zeros((128, 4, D), np.float32)
    gdn_p = np.zeros((1, 4, 8, 128, 128), np.float32)
    conv_p = np.zeros((1, 4, 3, 3072), np.float32)
    ret_p = np.zeros((1, 4, 8, 128, 128), np.float32)
    gdn_s = np.zeros((1, 128, 8, 128, 128), np.float32)
    conv_s = np.zeros((1, 128, 3, 3072), np.float32)
    ret_s = np.zeros((1, 128, 8, 128, 128), np.float32)
    for c in range(NCORES):
        r = R[c]
        sq0 = c * SPC
        y = np.asarray(r["yout"]).transpose(0, 3, 2, 1).reshape(NPASS, T, D)
        for p in range(NPASS):
            if c < 4:
                y_prompt[c, p * TPP:(p + 1) * TPP] = y[p, :TPP]
            y_sample[sq0 + p * SPP: sq0 + (p + 1) * SPP] = y[p, TPP:].reshape(4, SPP, D).transpose(1, 0, 2)
        if c < 4:
            gdn_p[0, c] = r["gdn_p"]
            ret_p[0, c] = r["ret_p"]
            conv_p[0, c] = r["conv_p"]
        gs = np.asarray(r["gdn_s"]).transpose(0, 3, 1, 2, 4).reshape(SPC, 8, 128, 128)
        rs = np.asarray(r["ret_s"]).transpose(0, 3, 1, 2, 4).reshape(SPC, 8, 128, 128)
        cs = np.asarray(r["conv_s"]).transpose(0, 2, 1, 3).reshape(SPC, 3, 3072)
        gdn_s[0, sq0:sq0 + SPC] = gs
        ret_s[0, sq0:sq0 + SPC] = rs
        conv_s[0, sq0:sq0 + SPC] = cs
    return (y_prompt, y_sample, gdn_p, conv_p, ret_p, gdn_s, conv_s, ret_s)
```

```python
import contextlib
import numpy as np
import concourse.bass as bass
import concourse.mybir as mybir
from concourse.bass_utils import run_bass_kernel_spmd

F32 = mybir.dt.float32
BF16 = mybir.dt.bfloat16
ALU = mybir.AluOpType
AF = mybir.ActivationFunctionType
AX = mybir.AxisListType

NCORES = 8
D = 2048
KC = 16
DFF = 5632
NFC = 44
NPASS = 2
TPP = 2048 // NPASS
NPT = TPP // 128
SPC = 16
SPP = SPC // NPASS
TS = 4 * SPP
T = TPP + TS
NSEG = 3
SEG = T // NSEG
assert SEG * NSEG == T and SEG <= 512
PAST = 16384
EPS = 1e-6
NEG = -30000.0
CU = 128
NWU = 6
NWD = 4

DEBUG = {}
STAGE = None


class _Stop(Exception):
    pass


class Buf:
    __slots__ = ("name", "w", "r")

    def __init__(self, name):
        self.name = name
        self.w = None
        self.r = []


class Op:
    __slots__ = ("eng", "fns", "deps", "signal", "semval", "stream", "dval")

    def __init__(self, eng, fns):
        self.eng = eng
        self.fns = fns
        self.deps = []
        self.signal = False
        self.semval = 0
        self.stream = None
        self.dval = 0


CL = [None]


class PX:
    def __init__(self, name):
        self._n = name

    def _t(self):
        return getattr(CL[0], self._n)

    def __getitem__(self, k):
        return self._t()[k]

    def __iter__(self):
        return iter(self._t())

    def __getattr__(self, a):
        return getattr(self._t(), a)


class BX:
    def __init__(self, name):
        self._n = name

    def get(self):
        return getattr(CL[0], self._n)


class _Rec:
    def __init__(self):
        self.calls = []

    def __getattr__(self, name):
        def f(*a, **k):
            a = tuple(x._t() if isinstance(x, PX) else x for x in a)
            k = {kk: (v._t() if isinstance(v, PX) else v) for kk, v in k.items()}
            self.calls.append((name, a, k))
            return self
        return f


class Lane:
    pass


class Prog:
    ENGS = ("pe", "act", "dve", "pool", "sp")

    def __init__(self, nc):
        self.nc = nc
        self.ops = {e: [] for e in self.ENGS}
        self.streams = {}
        self.last_stream_op = {}
        self.ovset = set()
        self.pstok = {}
        self.bPH = Buf("phase")
        self.rec_target = None
        self.rec_suffix = ""
        self.cur_atomic = None

    def buf(self, name):
        return Buf(name)

    def bufs(self, name, n):
        return [Buf(f"{name}{i}") for i in range(n)]

    def _add(self, eng, fns, reads, writes, stream=None):
        if not isinstance(fns, (list, tuple)):
            fns = [fns]
        rec = _Rec()
        for f in fns:
            f(rec)
        calls = rec.calls
        reads = [b.get() if isinstance(b, BX) else b for b in reads]
        writes = [b.get() if isinstance(b, BX) else b for b in writes]
        if self.rec_target is not None:
            if stream is not None and not stream.startswith("WU"):
                stream = stream + self.rec_suffix
            self.rec_target.append((eng, calls, reads, writes, stream, self.cur_atomic))
            return None
        return self._add_real(eng, calls, reads, writes, stream)

    def replay(self, entries):
        for (eng, calls, reads, writes, stream, _a) in entries:
            self._add_real(eng, calls, reads, writes, stream)

    def _add_real(self, eng, calls, reads, writes, stream=None):
        op = Op(eng, calls)
        deps = {}
        if any(id(b) in self.ovset for b in writes) and self.bPH not in reads:
            reads = list(reads) + [self.bPH]
        if eng != "pe":
            toks = [self.pstok[id(b)] for b in reads if id(b) in self.pstok]
            if toks:
                writes = list(writes) + toks
        for b in reads:
            if b.w is not None:
                deps[id(b.w)] = b.w
        for b in writes:
            if b.w is not None:
                deps[id(b.w)] = b.w
            for r in b.r:
                deps[id(r)] = r
        for d in deps.values():
            if d is op:
                continue
            if d.eng == "pe" and eng == "pe" and d.stream is None and stream is None:
                continue
            op.deps.append(d)
            d.signal = True
        for b in reads:
            b.r.append(op)
        for b in writes:
            b.w = op
            b.r = []
        if stream is not None:
            op.stream = stream
            self.streams[stream] = self.streams.get(stream, 0) + 1
            op.dval = 16 * self.streams[stream]
            self.last_stream_op[stream] = op
        self.ops[eng].append(op)
        return op

    def barrier(self, fn):
        rec = _Rec()
        fn(rec)
        op = Op("dve", rec.calls)
        for e in self.ENGS:
            for o in reversed(self.ops[e]):
                if o.stream is None:
                    if o is not op:
                        op.deps.append(o)
                        o.signal = True
                    break
        for so in self.last_stream_op.values():
            op.deps.append(so)
        b = self.bPH
        b.w = op
        b.r = []
        self.ops["dve"].append(op)
        return op

    def op(self, eng, fns, reads=(), writes=()):
        return self._add(eng, fns, reads, writes)

    def dma(self, eng, out, in_, reads=(), writes=(), stream=None, **kw):
        assert stream is not None
        out = out._t() if isinstance(out, PX) else out
        in_ = in_._t() if isinstance(in_, PX) else in_
        return self._add(eng, [lambda e, o=out, i=in_, k=kw: e.dma_start(out=o, in_=i, **k)],
                         reads, writes, stream=stream)

    def emit(self, final_wait_eng="sp"):
        nc = self.nc
        for e in self.ENGS:
            c = 0
            for op in self.ops[e]:
                if op.stream is None and op.signal:
                    c += 1
                    op.semval = c
        with contextlib.ExitStack() as st:
            esem = {e: st.enter_context(nc.semaphore("s_" + e)) for e in self.ENGS}
            dsem = {s: st.enter_context(nc.semaphore("d_" + s)) for s in self.streams}
            block = st.enter_context(nc.Block())
            engobj = {"pe": block.tensor, "act": block.scalar, "dve": block.vector,
                      "pool": block.gpsimd, "sp": block.sync}

            def run(ename):
                def body(e):
                    seen = {}
                    for op in self.ops[ename]:
                        need = {}
                        for d in op.deps:
                            if d.stream is not None:
                                key, val, sem = "d_" + d.stream, d.dval, dsem[d.stream]
                            else:
                                key, val, sem = d.eng, d.semval, esem[d.eng]
                            if val > need.get(key, (0, None))[0]:
                                need[key] = (val, sem)
                        for key, (val, sem) in need.items():
                            if seen.get(key, 0) >= val:
                                continue
                            seen[key] = val
                            e.wait_ge(sem, val)
                        ins = None
                        for (name, a, k) in op.fns:
                            ins = getattr(e, name)(*a, **k)
                        if op.stream is not None:
                            ins.then_inc(dsem[op.stream], 16)
                        elif op.signal:
                            ins.then_inc(esem[ename], 1)
                    if ename == final_wait_eng:
                        for s, n in self.streams.items():
                            e.wait_ge(dsem[s], 16 * n)
                        for en in self.ENGS:
                            c = max([o.semval for o in self.ops[en]] + [0])
                            if c:
                                e.wait_ge(esem[en], c)
                return body

            for ename in self.ENGS:
                engobj[ename](run(ename))


def _tile_blocks(kind):
    if kind == "P":
        i = np.arange(128)
        return 128, i // 64, i % 64
    i = np.arange(TS)
    return TS, i % SPP, i // SPP


def host_consts():
    c = {}
    c["ident"] = np.eye(128, dtype=np.float32)
    c["ones"] = np.ones((128, 128), np.float32)
    rsw = np.zeros((128, 128), np.float32)
    for p in range(64):
        rsw[p + 64, p] = 1.0
        rsw[p, p + 64] = 1.0
    c["rsw"] = rsw
    lg = np.log(1.0 - 2.0 ** (-5.0 - np.arange(8, dtype=np.float32))).astype(np.float32)
    for kind in ("P", "S"):
        n, blk, pos = _tile_blocks(kind)
        same = blk[:, None] == blk[None, :]
        m = np.zeros((4, 128, 128), np.float32)
        m[0, :n, :n] = np.where(same & (pos[None, :] >= pos[:, None]), 0.0, NEG)
        m[1, :n, :n] = np.where(same & (pos[:, None] > pos[None, :]), 0.0, NEG)
        m[2, :n, :n] = np.where(same & (pos[:, None] <= pos[None, :]), 1.0, 0.0)
        m[3, :n, :n] = np.where(same, 1.0, 0.0)
        c["mask" + kind] = m if kind == "P" else np.ascontiguousarray(m[:, :, :TS])
        C = 64 if kind == "P" else 4
        rel = (pos[None, :] - pos[:, None]).astype(np.float32)
        rdm = np.zeros((8, 128, 128), np.float32)
        rqd = np.zeros((8, 128, 128), np.float32)
        rkd = np.zeros((128, 8), np.float32)
        rgc = np.zeros((128, 8), np.float32)
        for h in range(8):
            dm = np.where(same & (rel >= 0), np.exp(np.maximum(rel, 0.0) * lg[h]), 0.0)
            rdm[h, :n, :n] = dm * (128.0 ** -0.5)
            rqd[h, :, :n] = np.exp((pos.astype(np.float32) + 1.0) * lg[h])[None, :]
            rkd[:n, h] = np.exp((C - 1.0 - pos.astype(np.float32)) * lg[h]) * (128.0 ** -0.5)
            rgc[:, h] = np.exp(C * lg[h])
        c["rdm" + kind] = rdm.astype(np.float32)
        c["rqd" + kind] = rqd.astype(np.float32)
        c["rkd" + kind] = rkd.astype(np.float32)
        c["rgc" + kind] = rgc.astype(np.float32)
    oh = np.zeros((128, SPP), np.float32)
    for i in range(TS):
        oh[i, i % SPP] = 1.0
    c["onehot"] = oh
    half = 64
    inv = (10000.0 ** (-np.arange(half, dtype=np.float32) / np.float32(half))).astype(np.float32)
    rot = np.zeros((NPASS, 2, 128, T), np.float32)
    for p in range(NPASS):
        pos = np.concatenate([p * TPP + np.arange(TPP), PAST + (np.arange(TS) // SPP)]).astype(np.float32)
        ang = (pos[None, :] * inv[:, None]).astype(np.float32).astype(np.float64)
        cs, sn = np.cos(ang), np.sin(ang)
        rot[p, 0, :64] = cs
        rot[p, 0, 64:] = cs
        rot[p, 1, :64] = -sn
        rot[p, 1, 64:] = sn
    c["rot"] = rot
    cst = np.zeros((128, 8), np.float32)
    cst[:, 0] = EPS
    cst[:, 1] = 128.0 * EPS
    cst[:, 2] = 1.0
    c["cst"] = cst
    return c


def tile_w(W, cu=CU):
    K, M = W.shape
    return np.ascontiguousarray(W.reshape(K // 128, 128, M // cu, cu).transpose(2, 1, 0, 3))


def build_program():
    nc = bass.Bass("TRN2", target_bir_lowering=False)
    ins_spec = {}
    outs_spec = {}

    def din(name, shape):
        ins_spec[name] = tuple(shape)
        return nc.dram_tensor(name, list(shape), F32, kind="ExternalInput").ap()

    def dout(name, shape):
        outs_spec[name] = tuple(shape)
        return nc.dram_tensor(name, list(shape), F32, kind="ExternalOutput").ap()

    xin = din("xin", [NPASS, 128, KC, T])
    cv = din("cv", [128, KC, 17])
    sgdn = din("sgdn", [NPASS, 8, 128, SPP, 128])
    sret = din("sret", [NPASS, 8, 128, SPP, 128])
    sconv = din("sconv", [NPASS, 24, 128, 3, SPP])
    w_ada = din("w_ada", [176, 128, KC, CU])
    b_ada = din("b_ada", [128, 176])
    normw = din("normw", [128, 4, KC])
    wg = [din("w1g", [NFC, 128, KC, CU]), din("w2g", [NFC, 128, KC, CU])]
    wu = [din("w1u", [NFC, 128, KC, CU]), din("w2u", [NFC, 128, KC, CU])]
    wd = [din("w1d", [NFC, 128, D]), din("w2d", [NFC, 128, D])]
    w_in = din("w_in", [64, 128, KC, CU])
    w_ba = din("w_ba", [128, KC, 16])
    w_out = din("w_out", [16, 128, KC, CU])
    convw = din("convw", [128, 24, 4])
    smallv = din("smallv", [128, 32])
    gnb = din("gnb", [128, 8])
    cident = din("ident", [128, 128])
    cones = din("ones", [128, 128])
    crsw = din("rsw", [128, 128])
    cmask = {"P": din("maskP", [4, 128, 128]), "S": din("maskS", [4, 128, TS])}
    crdm = {"P": din("rdmP", [8, 128, 128]), "S": din("rdmS", [8, 128, 128])}
    crqd = {"P": din("rqdP", [8, 128, 128]), "S": din("rqdS", [8, 128, 128])}
    crkd = {"P": din("rkdP", [128, 8]), "S": din("rkdS", [128, 8])}
    crgc = {"P": din("rgcP", [128, 8]), "S": din("rgcS", [128, 8])}
    conehot = din("onehot", [128, SPP])
    crot = din("rot", [NPASS, 2, 128, T])
    ccst = din("cst", [128, 8])

    yout = dout("yout", [NPASS, 128, KC, T])
    gdn_p = dout("gdn_p", [8, 128, 128])
    ret_p = dout("ret_p", [8, 128, 128])
    conv_p = dout("conv_p", [3, 3072])
    gdn_s = dout("gdn_s", [NPASS, 8, 128, SPP, 128])
    ret_s = dout("ret_s", [NPASS, 8, 128, SPP, 128])
    conv_s = dout("conv_s", [NPASS, 3, SPP, 3072])
    dbg_outs = {k: dout("dbg_" + k, v) for k, v in DEBUG.items()}
    sscr = nc.dram_tensor("sscr", [16, 128, 128], F32).ap()
    xscr = nc.dram_tensor("xscr", [128, KC, T], F32).ap()

    st = contextlib.ExitStack()
    with st:
        def sb(name, shape, dt=F32):
            return st.enter_context(nc.sbuf_tensor("s_" + name, list(shape), dt))[:]

        P = Prog(nc)
        bPH = P.bPH

        def stage(k):
            if STAGE is not None and k >= STAGE:
                raise _Stop()

        xT = sb("xT", [128, KC, T])
        hT = sb("hT", [128, KC, T], BF16)
        WU = [sb(f"WU{i}", [128, KC, CU], BF16) for i in range(NWU)]
        adaT = sb("adaT", [128, 176, 17])
        Amod = sb("Amod", [128, 4, KC, 17])
        ident = sb("ident", [128, 128])
        ones = sb("ones", [128, 128])
        onesb = sb("onesb", [128, 128], BF16)
        rsw = sb("rsw", [128, 128])
        masks = {"P": sb("maskP", [128, 4, 128]), "S": sb("maskS", [128, 4, TS])}
        onehot = sb("onehot", [128, SPP])
        cst = sb("cst", [128, 8])
        normT = sb("normT", [128, 4, KC])
        badaT = sb("badaT", [128, 176])
        cwT = sb("cwT", [128, 24, 4])
        smv = sb("smv", [128, 32])
        gnbT = sb("gnbT", [128, 8])
        rkd = {"P": sb("rkdP", [128, 8]), "S": sb("rkdS", [128, 8])}
        rgc = {"P": sb("rgcP", [128, 8]), "S": sb("rgcS", [128, 8])}
        wba = sb("wba", [128, KC, 16], BF16)
        negA = sb("negA", [128, 8])
        TM = sb("TM", [128, NPT + 1, 64])
        Sw_ = [sb(f"Sw{i}", [128, 128]) for i in range(2)]
        ctail = sb("ctail", [128, 24, 3])
        sq0_ = [sb(f"sq{i}", [128, SEG]) for i in range(2)]
        tmpf0_ = [sb(f"tmpf{i}", [128, SEG]) for i in range(2)]

        OVB = 51840
        OV = sb("OV", [128, OVB // 4])

        XV = xT.rearrange("p a b -> p (a b)")
        XVB = KC * T * 4

        def carve(off, shape, dt=F32, base=None, limit=None):
            base = OV if base is None else base
            limit = OVB if limit is None else limit
            esz = 2 if dt == BF16 else 4
            nel = int(np.prod(shape))
            assert off % 4 == 0 and (nel * esz) % 4 == 0 and off + nel * esz <= limit, (off, shape)
            ap = base[:, off // 4:(off + nel * esz) // 4]
            if dt == BF16:
                ap = ap.bitcast(BF16)
            if len(shape) == 2:
                ap = ap.rearrange("p (a b) -> p a b", a=shape[0])
            return ap

        ctmp = carve(0, [KC, 17])
        scT = sb("scT", [128, KC, 17], BF16)
        act = [carve(0, [4, T], BF16), carve(8448, [4, T], BF16)]
        WD = [carve(16896 + i * 4096, [D], BF16) for i in range(NWD)]
        sg = [carve(33280 + j * 704, [SEG], BF16) for j in range(2)]
        oTall = carve(0, [16, T], BF16)
        rstd = carve(0, [T])
        Sall = carve(33792, [SPP, 128])
        wsel = carve(37888, [SPP, 128])
        tlnames = ["DTi", "Dst", "Egc", "a1", "L", "LT", "QKT", "M", "P", "PT", "kbe", "kdec", "vb", "u", "wT",
                   "qdec", "vnew"]
        lanes = []
        for li in range(2):
            L = Lane()
            L.id = li
            xo = 34688 * li
            cx = lambda off, shape, dt=F32, xo=xo: carve(xo + off, shape, dt, base=XV, limit=XVB)
            L.FM = [cx(i * 4224, [T]) for i in range(3)]
            L.szT = cx(12672, [T], BF16)
            L.cb = cx(14784, [3 + TPP])
            L.cbs = cx(18944, [7 * SPP])
            L.cvt = cx(19200, [T])
            if li == 0:
                L.rdm = cx(23424, [128])
                L.rqd = cx(23936, [128])
                L.crow = cx(24448, [384])
                L.tl = {k: cx(25984 + i * 512, [128]) for i, k in enumerate(tlnames)}
                L.sq = sq0_
                L.tmpf = tmpf0_
            else:
                L.tl = {k: cx(23424 + i * 512, [128]) for i, k in enumerate(tlnames)}
                L.sq = [carve(41984 + j * 1408, [SEG]) for j in range(2)]
                L.tmpf = [carve(44800 + j * 1408, [SEG]) for j in range(2)]
                L.crow = carve(47616, [384])
                L.rdm = carve(49152, [128])
                L.rqd = carve(49664, [128])
            L.Sw = Sw_[li]
            L.Sb = carve(66816 + 256 * li, [128], BF16, base=XV, limit=XVB)
            lanes.append(L)
        rot = carve(14784, [2, T], base=XV, limit=XVB)
        CL[0] = lanes[0]
        REC = [False]
        tl, FM, szT, cb, cbs, cvt, rdm, rqd, crow, sq, tmpf = (PX(n) for n in (
            "tl", "FM", "szT", "cb", "cbs", "cvt", "rdm", "rqd", "crow", "sq", "tmpf"))

        pst = [st.enter_context(nc.psum_tensor(f"ps{i}", [128, 512], F32))[:] for i in range(8)]
        bps = P.bufs("ps", 8)
        for b_ in bps:
            P.pstok[id(b_)] = Buf("tok_" + b_.name)
        psi = [0]

        def bank():
            if REC[0]:
                L = CL[0]
                i = 4 * L.id + 1 + (L.bk_i % 3)
                L.bk_i += 1
                return pst[i], bps[i]
            i = psi[0] % 8
            psi[0] += 1
            return pst[i], bps[i]

        def bank_o():
            if REC[0]:
                i = 4 * CL[0].id
                return pst[i], bps[i]
            return bank()

        def ovb(b):
            if isinstance(b, list):
                for x in b:
                    ovb(x)
            else:
                P.ovset.add(id(b))
            return b

        bx = [[P.buf(f"x{k}_{s}") for s in range(NSEG)] for k in range(KC)]
        bh = [[P.buf(f"h{k}_{s}") for s in range(NSEG)] for k in range(KC)]
        bWU = P.bufs("WU", NWU)
        bWD = ovb(P.bufs("WD", NWD))
        bC = P.buf("consts")
        bada = P.bufs("adaT", 11)
        bAmod = P.bufs("Amod", 4)
        bact = [ovb(P.bufs(f"act{i}_", 4)) for i in range(2)]
        bsg = ovb(P.bufs("sg", 2))
        brstd = ovb(P.bufs("rstd", NSEG))
        bTM = P.bufs("TM", NPT + 1)
        bsscr = P.bufs("sscr", 16)
        bxscr = P.buf("xscr")
        bctail = P.bufs("ctail", 24)
        boT = [ovb(P.bufs(f"oT{i}_", NSEG)) for i in range(16)]
        brot = ovb(P.buf("rot"))
        bSall = ovb(P.buf("Sall"))
        bwsel = ovb(P.buf("wsel"))
        for L in lanes:
            li = L.id
            L.bk_i = 0
            L.wu_i = 0
            L.bsq = P.bufs(f"sq_{li}_", 2) if li == 0 else ovb(P.bufs(f"sq_{li}_", 2))
            L.btmpf = P.bufs(f"tmpf_{li}_", 2) if li == 0 else ovb(P.bufs(f"tmpf_{li}_", 2))
            L.bSw = P.buf(f"Sw{li}")
            L.bSb = ovb(P.buf(f"Sb{li}"))
            L.bFM = [ovb(P.bufs(f"FM{li}_{i}_", NSEG)) for i in range(3)]
            L.bszT = ovb(P.bufs(f"szT{li}_", NSEG))
            for nm in ("cb", "cbs", "cvt", "rdm", "rqd", "crow"):
                setattr(L, "b" + nm, ovb(P.buf(f"{nm}{li}")))
            L.btl = {k: ovb(P.buf(f"tl{li}_" + k)) for k in tlnames}
            for new, old in (("osb", "L"), ("osq", "LT"), ("rn", "P"), ("on", "PT"), ("mu", "a1"), ("msq", "DTi"),
                             ("var", "Dst"), ("cen", "M"), ("vTM", "vb")):
                L.tl[new] = L.tl[old]
                L.btl[new] = L.btl[old]
        btl, bFM, bszT = PX("btl"), PX("bFM"), PX("bszT")
        bcb, bcbs, bcvt, brdm, brqd, bcrow = (BX("b" + n) for n in ("cb", "cbs", "cvt", "rdm", "rqd", "crow"))
        bsq, btmpf = PX("bsq"), PX("btmpf")
        vexp, bvexp = wsel, bwsel
        bwba = P.buf("wba")
        bnegA = P.buf("negA")
        bscT = P.buf("scT")
        bx_all = [b for r in bx for b in r]

        segs = [(s * SEG, SEG) for s in range(NSEG)]

        def segs_of(c0, n):
            return sorted({c // SEG for c in (c0, c0 + n - 1)})

        wu_i = [0]
        wd_i = [0]

        def load_wu(src_ap, view=None):
            if REC[0]:
                L = CL[0]
                i = 3 * L.id + (L.wu_i % 3)
                L.wu_i += 1
            else:
                i = wu_i[0] % NWU
                wu_i[0] += 1
            dst = WU[i] if view is None else view(WU[i])
            P.dma("pool", dst, src_ap, writes=[bWU[i]], stream=f"WU{i}")
            return i

        def phase_barrier():
            P.barrier(lambda e: e.memset(cst[:, 7:8], 0.0))

        dbg_n = [0]

        def dbg(name, ap, reads):
            if name in dbg_outs:
                dbg_n[0] += 1
                P.dma("sp", dbg_outs[name], ap, reads=reads, stream=f"dbg{dbg_n[0]}")

        ci = [0]

        def cload(dst, src):
            ci[0] += 1
            P.dma("sp", dst, src, writes=[bC], stream=f"c{ci[0]}")

        cload(ident, cident)
        cload(ones, cones)
        cload(rsw, crsw)
        for k in ("P", "S"):
            cload(masks[k], cmask[k].rearrange("m p n -> p m n"))
            cload(rkd[k], crkd[k])
            cload(rgc[k], crgc[k])
        cload(onehot, conehot)
        cload(cst, ccst)
        cload(normT, normw)
        cload(badaT, b_ada)
        cload(cwT, convw)
        cload(smv, smallv)
        cload(gnbT, gnb)
        cload(ctmp, cv)
        P.dma("pool", wba, w_ba, writes=[bwba], stream="wba")
        P.op("act", lambda e: e.activation(out=negA, in_=smv[:, 8:16], func=AF.Exp), reads=[bC], writes=[bnegA])
        P.op("dve", lambda e: e.tensor_scalar(out=negA, in0=negA, scalar1=-1.0, scalar2=None, op0=ALU.mult),
             reads=[bnegA], writes=[bnegA])
        P.op("act", lambda e: e.activation(out=scT, in_=ctmp, func=AF.Silu), reads=[bC], writes=[bscT])
        P.op("dve", lambda e: e.memset(ctail, 0.0), writes=bctail)
        bonesb = P.buf("onesb")
        P.op("dve", lambda e: e.tensor_copy(out=onesb, in_=ones), reads=[bC], writes=[bonesb])

        sc_off = [16, 64, 112, 160]
        sh_off = [0, 48, 96, 144]
        g_off = [32, 80, 128]
        ada_done = set()

        def ada_group(gi):
            if gi in ada_done:
                return
            ada_done.add(gi)
            for u in range(16 * gi, 16 * gi + 16):
                wi = load_wu(w_ada[u])
                pt, pb = bank()
                P.op("pe", [lambda e, kc=kc, wi=wi, pt=pt: e.matmul(
                    pt[:, 0:17], WU[wi][:, kc, :], scT[:, kc, :], start=(kc == 0), stop=(kc == KC - 1))
                    for kc in range(KC)], reads=[bWU[wi], bscT], writes=[pb])
                P.op("dve", lambda e, u=u, pt=pt: e.tensor_scalar(
                    out=adaT[:, u, :], in0=pt[:, 0:17], scalar1=badaT[:, u:u + 1], scalar2=None, op0=ALU.add),
                    reads=[pb, bC] + ([bada[gi]] if u % 16 else []), writes=[bada[gi]])
            if gi * 16 in sc_off:
                s_ = sc_off.index(gi * 16)
                P.op("dve", lambda e: e.scalar_tensor_tensor(
                    out=Amod[:, s_, :, :], in0=adaT[:, sc_off[s_]:sc_off[s_] + 16, :], scalar=1.0,
                    in1=normT[:, s_, :].unsqueeze(2).to_broadcast([128, KC, 17]), op0=ALU.add, op1=ALU.mult),
                    reads=[bada[gi], bC], writes=[bAmod[s_]])
            if gi * 16 in (g_off[0], g_off[2]):
                P.op("dve", lambda e: e.tensor_scalar(
                    out=adaT[:, gi * 16:gi * 16 + 16, :], in0=adaT[:, gi * 16:gi * 16 + 16, :],
                    scalar1=0.5, scalar2=None, op0=ALU.mult), reads=[bada[gi]], writes=[bada[gi]])

        ada_pending = []
        for gi in (1, 0, 2, 3, 4, 5, 6, 7, 8, 9, 10):
            ada_group(gi)

        def samp_bc(ap2d, p):
            return ap2d[:, 1 + p * SPP:1 + (p + 1) * SPP].unsqueeze(1).to_broadcast([128, 4, SPP])

        def ts(ap):
            return ap.rearrange("p (t s) -> p t s", t=4)

        def norm_mod(p, site, final=False):
            ada_group(sc_off[site] // 16)
            ada_group(sh_off[site] // 16)
            phase_barrier()
            for s, (c0, n) in enumerate(segs):
                pt, pb = bank()
                for kc in range(KC):
                    j = kc % 2
                    P.op("act", lambda e, kc=kc, j=j, c0=c0, n=n: e.activation(
                        out=sq[j].bitcast(BF16)[:, 0:n], in_=xT[:, kc, c0:c0 + n], func=AF.Square),
                        reads=[bx[kc][s]], writes=[bsq[j]])
                    P.op("pe", lambda e, kc=kc, j=j, n=n, pt=pt: e.matmul(
                        pt[:, 0:n], onesb, sq[j].bitcast(BF16)[:, 0:n], start=(kc == 0), stop=(kc == KC - 1)),
                        reads=[bsq[j], bonesb], writes=[pb])
                P.op("act", lambda e, c0=c0, n=n, pt=pt: e.activation(
                    out=rstd[:, c0:c0 + n], in_=pt[:, 0:n], func=AF.Sqrt, bias=cst[:, 0:1], scale=1.0 / D),
                    reads=[pb, bC], writes=[brstd[s]])
                P.op("dve", lambda e, c0=c0, n=n: e.reciprocal(out=rstd[:, c0:c0 + n], in_=rstd[:, c0:c0 + n]),
                     reads=[brstd[s]], writes=[brstd[s]])
                npm = min(c0 + n, TPP) - c0
                for kc in range(KC):
                    j = kc % 2
                    P.op("dve", lambda e, kc=kc, j=j, c0=c0, n=n: e.tensor_tensor(
                        out=tmpf[j][:, 0:n], in0=xT[:, kc, c0:c0 + n], in1=rstd[:, c0:c0 + n], op=ALU.mult),
                        reads=[bx[kc][s], brstd[s]], writes=[btmpf[j]])
                    if not final:
                        wr = [bh[kc][s]]
                        o_p = lambda a, b, kc=kc: hT[:, kc, a:b]
                    else:
                        wr = [bx[kc][s]]
                        o_p = lambda a, b, kc=kc: xT[:, kc, a:b]
                    if npm > 0:
                        P.op("act", lambda e, kc=kc, j=j, c0=c0, npm=npm, o_p=o_p: e.activation(
                            out=o_p(c0, c0 + npm), in_=tmpf[j][:, 0:npm], func=AF.Identity,
                            scale=Amod[:, site, kc, 0:1], bias=adaT[:, sh_off[site] + kc, 0:1]),
                            reads=[btmpf[j], bAmod[site], bada[sh_off[site] // 16]], writes=wr)
                    if npm < n:
                        P.op("dve", lambda e, kc=kc, j=j, npm=npm, n=n: e.tensor_tensor(
                            out=ts(tmpf[j][:, npm:n]), in0=ts(tmpf[j][:, npm:n]),
                            in1=samp_bc(Amod[:, site, kc, :], p), op=ALU.mult),
                            reads=[btmpf[j], bAmod[site]], writes=[btmpf[j]])
                        P.op("dve", lambda e, kc=kc, j=j, npm=npm, n=n, c0=c0, o_p=o_p: e.tensor_tensor(
                            out=ts(o_p(c0 + npm, c0 + n)), in0=ts(tmpf[j][:, npm:n]),
                            in1=samp_bc(adaT[:, sh_off[site] + kc, :], p), op=ALU.add),
                            reads=[btmpf[j], bada[sh_off[site] // 16]], writes=wr)
            if final:
                P.dma("sp", yout[p], xT, reads=[b for r in bx for b in r], stream="yout")

        def gate_evac(pt, pb, p, dc, s, goff):
            c0, n = segs[s]
            npm = min(c0 + n, TPP) - c0
            if npm > 0:
                P.op("dve", lambda e: e.scalar_tensor_tensor(
                    out=xT[:, dc, c0:c0 + npm], in0=pt[:, 0:npm], scalar=adaT[:, goff + dc, 0:1],
                    in1=xT[:, dc, c0:c0 + npm], op0=ALU.mult, op1=ALU.add),
                    reads=[pb, bada[goff // 16], bx[dc][s]], writes=[bx[dc][s]])
            if npm < n:
                j = dc % 2
                P.op("dve", lambda e: e.tensor_tensor(
                    out=ts(tmpf[j][:, npm:n]), in0=ts(pt[:, npm:n]),
                    in1=samp_bc(adaT[:, goff + dc, :], p), op=ALU.mult),
                    reads=[pb, bada[goff // 16]], writes=[btmpf[j]])
                P.op("dve", lambda e: e.tensor_tensor(
                    out=xT[:, dc, c0 + npm:c0 + n], in0=xT[:, dc, c0 + npm:c0 + n], in1=tmpf[j][:, npm:n],
                    op=ALU.add), reads=[btmpf[j], bx[dc][s]], writes=[bx[dc][s]])

        def ffn(p, which):
            phase_barrier()
            goff = g_off[0] if which == 0 else g_off[2]
            NG = NFC // 4

            def gate_up(g):
                a = g % 2
                for f4 in range(4):
                    f = 4 * g + f4
                    wgi = load_wu(wg[which][f])
                    wui = load_wu(wu[which][f])
                    for s, (c0, n) in enumerate(segs):
                        ptg, pbg = bank()
                        ptu, pbu = bank()
                        P.op("pe", [lambda e, kc=kc, ptg=ptg, wgi=wgi, c0=c0, n=n: e.matmul(
                            ptg[:, 0:n], WU[wgi][:, kc, :], hT[:, kc, c0:c0 + n],
                            start=(kc == 0), stop=(kc == KC - 1)) for kc in range(KC)],
                            reads=[bWU[wgi]] + [bh[kc][s] for kc in range(KC)], writes=[pbg])
                        P.op("pe", [lambda e, kc=kc, ptu=ptu, wui=wui, c0=c0, n=n: e.matmul(
                            ptu[:, 0:n], WU[wui][:, kc, :], hT[:, kc, c0:c0 + n],
                            start=(kc == 0), stop=(kc == KC - 1)) for kc in range(KC)],
                            reads=[bWU[wui]] + [bh[kc][s] for kc in range(KC)], writes=[pbu])
                        j = (f4 * NSEG + s) % 2
                        P.op("act", lambda e, j=j, n=n, ptg=ptg: e.activation(
                            out=sg[j][:, 0:n], in_=ptg[:, 0:n], func=AF.Silu), reads=[pbg], writes=[bsg[j]])
                        P.op("dve", lambda e, j=j, n=n, ptu=ptu, a=a, f4=f4, c0=c0: e.tensor_tensor(
                            out=act[a][:, f4, c0:c0 + n], in0=ptu[:, 0:n], in1=sg[j][:, 0:n], op=ALU.mult),
                            reads=[pbu, bsg[j]], writes=[bact[a][f4]])

            def down_mm(g):
                a = g % 2
                wis = []
                for f4 in range(4):
                    i = wd_i[0] % NWD
                    wd_i[0] += 1
                    P.dma("pool", WD[i], wd[which][4 * g + f4], writes=[bWD[i]], stream=f"WD{i}")
                    wis.append(i)
                for dc in range(KC):
                    for s, (c0, n) in enumerate(segs):
                        pt, pb = bank()
                        P.op("pe", [lambda e, f4=f4, pt=pt, dc=dc, c0=c0, n=n, wis=wis, a=a: e.matmul(
                            pt[:, 0:n], WD[wis[f4]][:, dc * 128:(dc + 1) * 128], act[a][:, f4, c0:c0 + n],
                            start=(f4 == 0), stop=(f4 == 3)) for f4 in range(4)],
                            reads=[bWD[i] for i in wis] + bact[a], writes=[pb])
                        gate_evac(pt, pb, p, dc, s, goff)

            ada_group(goff // 16)
            gate_up(0)
            for g in range(NG):
                if g + 1 < NG:
                    gate_up(g + 1)
                if ada_pending:
                    ada_group(ada_pending.pop(0))
                down_mm(g)

        def tile_list():
            tiles = [("P", 128, i * 128) for i in range(NPT)]
            tiles.append(("S", TS, TPP))
            return tiles

        def tm_scalars(p):
            for ti, (kind, n, c0) in enumerate(tile_list()):
                ss = segs_of(c0, n)
                pt, pb = bank()
                P.op("pe", [lambda e, kc=kc, pt=pt, c0=c0, n=n: e.matmul(
                    pt[0:n, 0:16], hT[:, kc, c0:c0 + n], wba[:, kc, :], start=(kc == 0), stop=(kc == KC - 1))
                    for kc in range(KC)],
                    reads=[bwba] + [bh[kc][s] for kc in range(KC) for s in ss], writes=[pb])
                tm = TM[0:n, ti, :]
                b = bTM[ti]
                P.op("act", lambda e, pt=pt, n=n, tm=tm: e.activation(out=tm[:, 0:8], in_=pt[0:n, 0:8], func=AF.Sigmoid),
                     reads=[pb], writes=[b])
                P.op("dve", lambda e, pt=pt, n=n, tm=tm: e.tensor_tensor(
                    out=tm[:, 8:16], in0=pt[0:n, 8:16], in1=smv[0:n, 0:8], op=ALU.add),
                    reads=[pb, bC, b], writes=[b])
                P.op("act", lambda e, tm=tm: e.activation(out=tm[:, 8:16], in_=tm[:, 8:16], func=AF.Exp),
                     reads=[b], writes=[b])
                P.op("act", lambda e, tm=tm, n=n: e.activation(out=tm[:, 8:16], in_=tm[:, 8:16], func=AF.Ln,
                                                               bias=cst[0:n, 2:3], scale=1.0),
                     reads=[b, bC], writes=[b])
                P.op("dve", lambda e, tm=tm, n=n: e.tensor_tensor(out=tm[:, 8:16], in0=tm[:, 8:16], in1=negA[0:n, :],
                                                                  op=ALU.mult), reads=[b, bnegA], writes=[b])
                pt2, pb2 = bank()
                mk = masks[kind]
                P.op("pe", lambda e, pt2=pt2, n=n, tm=tm, mk=mk: e.matmul(
                    pt2[0:n, 0:8], mk[0:n, 2, 0:n], tm[:, 8:16], start=True, stop=True), reads=[b, bC], writes=[pb2])
                pt3, pb3 = bank()
                P.op("pe", lambda e, pt3=pt3, n=n, tm=tm, mk=mk: e.matmul(
                    pt3[0:n, 0:8], mk[0:n, 3, 0:n], tm[:, 8:16], start=True, stop=True), reads=[b, bC], writes=[pb3])
                P.op("dve", lambda e, pt2=pt2, n=n, tm=tm: e.tensor_copy(out=tm[:, 16:24], in_=pt2[0:n, 0:8]),
                     reads=[pb2, b], writes=[b])
                P.op("dve", lambda e, tm=tm: e.tensor_scalar(out=tm[:, 56:64], in0=tm[:, 16:24], scalar1=-1.0,
                                                             scalar2=None, op0=ALU.mult), reads=[b], writes=[b])
                P.op("act", lambda e, tm=tm: e.activation(out=tm[:, 32:40], in_=tm[:, 16:24], func=AF.Exp),
                     reads=[b], writes=[b])
                P.op("dve", lambda e, tm=tm: e.tensor_tensor(out=tm[:, 40:48], in0=tm[:, 32:40], in1=tm[:, 0:8],
                                                             op=ALU.mult), reads=[b], writes=[b])
                P.op("dve", lambda e, pt3=pt3, n=n, tm=tm: e.tensor_tensor(
                    out=tm[:, 48:56], in0=pt3[0:n, 0:8], in1=tm[:, 16:24], op=ALU.subtract),
                    reads=[pb3, b], writes=[b])
                P.op("act", lambda e, tm=tm: e.activation(out=tm[:, 48:56], in_=tm[:, 48:56], func=AF.Exp),
                     reads=[b], writes=[b])

        def fm_proj(wi, s):
            c0, n = segs[s]
            pt, pb = bank()
            P.op("pe", [lambda e, kc=kc, pt=pt, c0=c0, n=n: e.matmul(
                pt[:, 0:n], WU[wi][:, kc, :], hT[:, kc, c0:c0 + n],
                start=(kc == 0), stop=(kc == KC - 1)) for kc in range(KC)],
                reads=[bWU[wi]] + [bh[kc][s] for kc in range(KC)], writes=[pb])
            return pt, pb

        def cp(eng, out, in_, reads, writes):
            if eng == "act":
                P.op("act", lambda e: e.copy(out=out, in_=in_), reads=reads, writes=writes)
            else:
                P.op("dve", lambda e: e.tensor_copy(out=out, in_=in_), reads=reads, writes=writes)

        def mm1(out_ap, lhsT, rhs, reads, pb, **kw):
            P.op("pe", lambda e: e.matmul(out_ap, lhsT, rhs, start=True, stop=True, **kw), reads=reads, writes=[pb])

        def o_finish_gdn(hd, hl, c0, n, pt, pb):
            hl = hd
            ss = segs_of(c0, n)
            cp("dve", tl["osb"][:, 0:n], pt[:, 0:n], [pb], [btl["osb"]])
            P.op("act", lambda e: e.activation(out=tl["osq"].bitcast(BF16)[:, 0:n], in_=pt[:, 0:n], func=AF.Square),
                 reads=[pb], writes=[btl["osq"]])
            pt2, pb2 = bank()
            mm1(pt2[:, 0:n], onesb, tl["osq"].bitcast(BF16)[:, 0:n], [btl["osq"], bonesb], pb2)
            P.op("act", lambda e: e.activation(out=tl["rn"][:, 0:n], in_=pt2[:, 0:n], func=AF.Sqrt,
                                               bias=cst[:, 0:1], scale=1.0 / 128), reads=[pb2, bC],
                 writes=[btl["rn"]])
            P.op("dve", lambda e: e.reciprocal(out=tl["rn"][:, 0:n], in_=tl["rn"][:, 0:n]),
                 reads=[btl["rn"]], writes=[btl["rn"]])
            P.op("dve", lambda e: e.tensor_tensor(out=tl["on"][:, 0:n], in0=tl["osb"][:, 0:n], in1=tl["rn"][:, 0:n],
                                                  op=ALU.mult), reads=[btl["osb"], btl["rn"]], writes=[btl["on"]])
            P.op("dve", lambda e: e.scalar_tensor_tensor(
                out=oTall[:, hl, c0:c0 + n], in0=tl["on"][:, 0:n], scalar=smv[:, 16:17], in1=szT[:, c0:c0 + n],
                op0=ALU.mult, op1=ALU.mult), reads=[btl["on"], bC] + [bszT[s] for s in ss],
                writes=[boT[hl][s] for s in ss])

        def o_finish_ret(hd, hl, c0, n, pt, pb):
            hl = 8 + hd
            ss = segs_of(c0, n)
            cp("dve", tl["osb"][:, 0:n], pt[:, 0:n], [pb], [btl["osb"]])
            P.op("act", lambda e: e.activation(out=tl["osq"].bitcast(BF16)[:, 0:n], in_=pt[:, 0:n], func=AF.Square),
                 reads=[pb], writes=[btl["osq"]])
            pt2, pb2 = bank()
            mm1(pt2[:, 0:n], ones, tl["osb"][:, 0:n], [btl["osb"], bC], pb2)
            mm1(pt2[:, 128:128 + n], onesb, tl["osq"].bitcast(BF16)[:, 0:n], [btl["osq"], bonesb], pb2)
            P.op("act", lambda e: e.activation(out=tl["mu"][:, 0:n], in_=pt2[:, 0:n], func=AF.Identity,
                                               scale=1.0 / 128), reads=[pb2], writes=[btl["mu"]])
            P.op("dve", lambda e: e.tensor_tensor(out=tl["msq"][:, 0:n], in0=tl["mu"][:, 0:n], in1=tl["mu"][:, 0:n],
                                                  op=ALU.mult), reads=[btl["mu"]], writes=[btl["msq"]])
            P.op("dve", lambda e: e.scalar_tensor_tensor(
                out=tl["var"][:, 0:n], in0=pt2[:, 128:128 + n], scalar=1.0 / 128, in1=tl["msq"][:, 0:n],
                op0=ALU.mult, op1=ALU.subtract), reads=[pb2, btl["msq"]], writes=[btl["var"]])
            P.op("dve", lambda e: e.tensor_scalar(out=tl["var"][:, 0:n], in0=tl["var"][:, 0:n], scalar1=0.0,
                                                  scalar2=None, op0=ALU.max), reads=[btl["var"]], writes=[btl["var"]])
            P.op("act", lambda e: e.activation(out=tl["rn"][:, 0:n], in_=tl["var"][:, 0:n], func=AF.Sqrt,
                                               bias=cst[:, 0:1], scale=1.0), reads=[btl["var"], bC],
                 writes=[btl["rn"]])
            P.op("dve", lambda e: e.reciprocal(out=tl["rn"][:, 0:n], in_=tl["rn"][:, 0:n]),
                 reads=[btl["rn"]], writes=[btl["rn"]])
            P.op("dve", lambda e: e.tensor_tensor(out=tl["cen"][:, 0:n], in0=tl["osb"][:, 0:n], in1=tl["mu"][:, 0:n],
                                                  op=ALU.subtract), reads=[btl["osb"], btl["mu"]], writes=[btl["cen"]])
            P.op("dve", lambda e: e.tensor_tensor(out=tl["on"][:, 0:n], in0=tl["cen"][:, 0:n], in1=tl["rn"][:, 0:n],
                                                  op=ALU.mult), reads=[btl["cen"], btl["rn"]], writes=[btl["on"]])
            P.op("act", lambda e: e.activation(out=tl["on"][:, 0:n], in_=tl["on"][:, 0:n], func=AF.Identity,
                                               scale=smv[:, 17 + hd:18 + hd], bias=gnbT[:, hd:hd + 1]),
                 reads=[btl["on"], bC], writes=[btl["on"]])
            P.op("dve", lambda e: e.tensor_tensor(out=oTall[:, hl, c0:c0 + n], in0=tl["on"][:, 0:n],
                                                  in1=szT[:, c0:c0 + n], op=ALU.mult),
                 reads=[btl["on"]] + [bszT[s] for s in ss], writes=[boT[hl][s] for s in ss])

        def recur_prompt(S, bS, qdec, bqdec, QKT, bQKT, kdec, bkdec, vsrc, bvsrc, glast_ap_fn, greads,
                         wT=None, bwT=None, u=None, bu=None):
            Sb, bSb = CL[0].Sb, CL[0].bSb
            vn = tl["vnew"].bitcast(BF16)
            pto, pbo = bank_o()
            for c in range(2):
                r0 = 64 * c
                if wT is not None:
                    ptw, pbw = bank()
                    mm1(ptw[:, 0:128], wT[:, 0:128], Sb, [bwT, bSb], pbw)
                    P.op("dve", lambda e, r0=r0, ptw=ptw: e.tensor_tensor(
                        out=vn[r0:r0 + 64, 0:128], in0=u[r0:r0 + 64, :], in1=ptw[r0:r0 + 64, 0:128],
                        op=ALU.subtract), reads=[pbw, bu], writes=[btl["vnew"]])
                    v_, bv_ = vn, btl["vnew"]
                else:
                    v_, bv_ = vsrc, bvsrc
                P.op("pe", [lambda e, r0=r0: e.matmul(pto[:, r0:r0 + 64], Sb, qdec[:, r0:r0 + 64],
                                                      start=True, stop=False, skip_group_check=True),
                            lambda e, r0=r0, v_=v_: e.matmul(pto[:, r0:r0 + 64], v_[r0:r0 + 64, 0:128],
                                                             QKT[r0:r0 + 64, r0:r0 + 64],
                                                             start=False, stop=True, skip_group_check=True)],
                     reads=[bSb, bqdec, bv_, bQKT], writes=[pbo])
                pts, pbs = bank()
                mm1(pts[:, 0:128], kdec[r0:r0 + 64, 0:128], v_[r0:r0 + 64, 0:128], [bkdec, bv_], pbs)
                P.op("dve", lambda e, r0=r0, pts=pts: e.scalar_tensor_tensor(
                    out=Sb, in0=S, scalar=glast_ap_fn(r0 + 63), in1=pts[:, 0:128], op0=ALU.mult, op1=ALU.add),
                    reads=[pbs, bS] + greads, writes=[bSb])
                P.op("dve", lambda e, r0=r0, pts=pts: e.scalar_tensor_tensor(
                    out=S, in0=S, scalar=glast_ap_fn(r0 + 63), in1=pts[:, 0:128], op0=ALU.mult, op1=ALU.add),
                    reads=[pbs, bS] + greads, writes=[bS])
            return pto, pbo

        def recur_sample(p, hd, sdram_in, sdram_out, qdec, bqdec, QKT, bQKT, kdec, bkdec, vsrc, bvsrc,
                         glast3, greads, wT=None, bwT=None, u=None, bu=None):
            n = TS
            P.cur_atomic = ("samp", CL[0].id, hd, id(sdram_in))
            P.dma("sp", Sall, sdram_in[p, hd], writes=[bSall], stream="Sall")
            nh = (SPP * 128) // 512
            Sflat = Sall.rearrange("p s d -> p (s d)")
            if wT is not None:
                for hh in range(nh):
                    ptw, pbw = bank()
                    mm1(ptw[0:n, :], wT[:, 0:n], Sflat[:, hh * 512:(hh + 1) * 512], [bwT, bSall], pbw)
                    P.op("dve", lambda e, ptw=ptw, hh=hh: e.tensor_tensor(
                        out=wsel[0:n, hh * 4:(hh + 1) * 4, :], in0=ptw[0:n, :].rearrange("p (s d) -> p s d", s=4),
                        in1=onehot[0:n, hh * 4:(hh + 1) * 4].unsqueeze(2).to_broadcast([n, 4, 128]), op=ALU.mult),
                        reads=[pbw, bC] + ([bwsel] if hh else []), writes=[bwsel])
                P.op("dve", lambda e: e.tensor_reduce(
                    out=tl["on"][0:n, :], in_=wsel[0:n, :, :].rearrange("p s d -> p d s"), axis=AX.X, op=ALU.add),
                    reads=[bwsel], writes=[btl["on"]])
                P.op("dve", lambda e: e.tensor_tensor(out=tl["vnew"][0:n, :], in0=u[0:n, :], in1=tl["on"][0:n, :],
                                                      op=ALU.subtract), reads=[btl["on"], bu], writes=[btl["vnew"]])
                v_, bv_ = tl["vnew"], btl["vnew"]
            else:
                v_, bv_ = vsrc, bvsrc
            pto, pbo = bank_o()
            fns = []
            for s in range(SPP):
                fns.append(lambda e, s=s: e.matmul(
                    ts(pto[:, 0:n])[:, :, s], Sall[:, s, :], ts(qdec[:, 0:n])[:, :, s],
                    start=(s == 0), stop=False, skip_group_check=True))
            fns.append(lambda e: e.matmul(pto[:, 0:n], v_[0:n, :], QKT[0:n, 0:n], start=False, stop=True,
                                          skip_group_check=True))
            P.op("pe", fns, reads=[bSall, bqdec, bv_, bQKT], writes=[pbo])
            P.op("dve", lambda e: e.tensor_tensor(
                out=vexp[0:n, :, :], in0=v_[0:n, :].unsqueeze(1).to_broadcast([n, SPP, 128]),
                in1=onehot[0:n, :].unsqueeze(2).to_broadcast([n, SPP, 128]), op=ALU.mult),
                reads=[bv_, bC], writes=[bvexp])
            P.op("dve", lambda e: e.tensor_tensor(out=Sall, in0=Sall, in1=glast3, op=ALU.mult),
                 reads=[bSall] + greads, writes=[bSall])
            vflat = vexp.rearrange("p s d -> p (s d)")
            for hh in range(nh):
                pts, pbs = bank()
                mm1(pts[:, :], kdec[0:n, :], vflat[0:n, hh * 512:(hh + 1) * 512], [bkdec, bvexp], pbs)
                P.op("dve", lambda e, pts=pts, hh=hh: e.tensor_tensor(
                    out=Sflat[:, hh * 512:(hh + 1) * 512], in0=Sflat[:, hh * 512:(hh + 1) * 512], in1=pts[:, :],
                    op=ALU.add), reads=[pbs, bSall], writes=[bSall])
            P.dma("sp", sdram_out[p, hd], Sall, reads=[bSall], stream="Sall_o")
            P.cur_atomic = None
            return pto, pbo

        swi = [0]

        def state_in(p, idx):
            L = CL[0]
            if p == 0:
                P.op("dve", lambda e: e.memset(L.Sw, 0.0), writes=[L.bSw])
            else:
                P.dma("sp", L.Sw, sscr[idx], reads=[bsscr[idx]], writes=[L.bSw], stream="Sw")
            P.op("act", lambda e: e.copy(out=L.Sb, in_=L.Sw), reads=[L.bSw], writes=[L.bSb])
            return L.Sw, L.bSw, L.id

        def state_out(p, idx, i, out_ap):
            L = CL[0]
            if p == NPASS - 1:
                P.dma("sp", out_ap, L.Sw, reads=[L.bSw], stream="Sw_o")
            else:
                P.dma("sp", sscr[idx], L.Sw, reads=[L.bSw], writes=[bsscr[idx]], stream="Sw_o")

        def gdn_head(p, hd):
            hl = hd % 4
            wq = load_wu(w_in[hd * 4 + 0])
            wk = load_wu(w_in[hd * 4 + 1])
            wv = load_wu(w_in[hd * 4 + 2])
            wqkv = [wq, wk, wv]
            S, bS, si = state_in(p, hd)
            r0c = TPP - 3
            nr = 3 + TS
            ptc, pbc = bank()
            for comp in range(3):
                P.op("pe", [lambda e, kc=kc, comp=comp: e.matmul(
                    ptc[0:nr, comp * 128:(comp + 1) * 128], hT[:, kc, r0c:T], WU[wqkv[comp]][:, kc, :],
                    start=(kc == 0), stop=(kc == KC - 1), skip_group_check=True) for kc in range(KC)],
                    reads=[bWU[wqkv[comp]]] + [bh[kc][NSEG - 1] for kc in range(KC)], writes=[pbc])
            cp("act", crow[0:nr, :], ptc[0:nr, 0:384], [pbc], [bcrow])
            if p == NPASS - 1:
                P.dma("sp", conv_p.rearrange("i (c h x) -> i c h x", c=3, h=8)[:, :, hd, :],
                      crow[0:3, :].rearrange("i (c x) -> i c x", c=3), reads=[bcrow], stream="crow_o")
            for j in range(3):
                P.dma("sp", conv_s[p, j].rearrange("s (c h x) -> s c h x", c=3, h=8)[:, :, hd, :],
                      crow[3 + (1 + j) * SPP:3 + (2 + j) * SPP, :].rearrange("s (c x) -> s c x", c=3),
                      reads=[bcrow], stream="crow_o")
            for comp in range(3):
                wi = wqkv[comp]
                ch = comp * 8 + hd
                P.op("dve", lambda e, ch=ch: e.tensor_copy(out=cb[:, 0:3], in_=ctail[:, ch, :]),
                     reads=[bctail[ch]], writes=[bcb])
                P.dma("sp", cbs[:, 0:3 * SPP], sconv[p, ch].rearrange("c i s -> c (i s)"), writes=[bcbs],
                      stream="cbs")
                for s, (c0, n) in enumerate(segs):
                    pt, pb = fm_proj(wi, s)
                    npm = min(c0 + n, TPP) - c0
                    if npm > 0:
                        cp("act", cb[:, 3 + c0:3 + c0 + npm], pt[:, 0:npm], [pb, bcb], [bcb])
                    if npm < n:
                        cp("dve", cbs[:, 3 * SPP:7 * SPP], pt[:, npm:n], [pb, bcbs], [bcbs])
                P.op("dve", lambda e, ch=ch: e.tensor_copy(out=ctail[:, ch, :], in_=cb[:, TPP:TPP + 3]),
                     reads=[bcb], writes=[bctail[ch]])
                for (dst, src, width, bsrc, step) in ((cvt[:, 0:TPP], cb, TPP, bcb, 1),
                                                      (cvt[:, TPP:T], cbs, TS, bcbs, SPP)):
                    P.op("dve", lambda e, dst=dst, src=src, width=width, ch=ch: e.tensor_scalar(
                        out=dst, in0=src[:, 0:width], scalar1=cwT[:, ch, 0:1], scalar2=None, op0=ALU.mult),
                        reads=[bsrc, bC, bcvt], writes=[bcvt])
                    for i in range(1, 4):
                        P.op("dve", lambda e, dst=dst, src=src, width=width, ch=ch, i=i, step=step:
                             e.scalar_tensor_tensor(out=dst, in0=src[:, i * step:i * step + width],
                                                    scalar=cwT[:, ch, i:i + 1], in1=dst, op0=ALU.mult, op1=ALU.add),
                             reads=[bsrc, bC, bcvt], writes=[bcvt])
                P.op("act", lambda e, comp=comp: e.activation(out=FM[comp], in_=cvt, func=AF.Silu),
                     reads=[bcvt], writes=bFM[comp])
            wz = load_wu(w_in[hd * 4 + 3])
            for s, (c0, n) in enumerate(segs):
                pt, pb = fm_proj(wz, s)
                P.op("act", lambda e, pt=pt, c0=c0, n=n: e.activation(out=szT[:, c0:c0 + n], in_=pt[:, 0:n],
                                                                      func=AF.Silu), reads=[pb], writes=[bszT[s]])
            for comp in range(2):
                for s, (c0, n) in enumerate(segs):
                    j = s % 2
                    P.op("act", lambda e, comp=comp, j=j, c0=c0, n=n: e.activation(
                        out=sq[j].bitcast(BF16)[:, 0:n], in_=FM[comp][:, c0:c0 + n], func=AF.Square),
                        reads=[bFM[comp][s]], writes=[bsq[j]])
                    pt, pb = bank()
                    mm1(pt[:, 0:n], onesb, sq[j].bitcast(BF16)[:, 0:n], [bsq[j], bonesb], pb)
                    P.op("act", lambda e, comp=comp, j=j, n=n, pt=pt: e.activation(
                        out=tmpf[j][:, 0:n], in_=pt[:, 0:n], func=AF.Sqrt,
                        bias=(cst[:, 1:2] if comp == 0 else cst[:, 0:1]), scale=(128.0 if comp == 0 else 1.0)),
                        reads=[pb, bC], writes=[btmpf[j]])
                    P.op("dve", lambda e, j=j, n=n: e.reciprocal(out=tmpf[j][:, 0:n], in_=tmpf[j][:, 0:n]),
                         reads=[btmpf[j]], writes=[btmpf[j]])
                    P.op("dve", lambda e, comp=comp, j=j, c0=c0, n=n: e.tensor_tensor(
                        out=FM[comp][:, c0:c0 + n], in0=FM[comp][:, c0:c0 + n], in1=tmpf[j][:, 0:n], op=ALU.mult),
                        reads=[btmpf[j], bFM[comp][s]], writes=[bFM[comp][s]])
            qT, kT, vT = FM
            for ti, (kind, n, c0) in enumerate(tile_list()):
                ss = segs_of(c0, n)
                rq = [bFM[0][s] for s in ss]
                rk = [bFM[1][s] for s in ss]
                rv = [bFM[2][s] for s in ss]
                mk = masks[kind]
                tm = TM[0:n, ti, :]
                btm = bTM[ti]
                bfk = (kind == "P")
                bv = (lambda nm: tl[nm].bitcast(BF16)) if bfk else (lambda nm: tl[nm])
                ptr, pbr = bank()
                mm1(ptr[:, 0:n], tm[:, 8 + hd:9 + hd].to_broadcast([n, 128]), mk[0:n, 2, 0:n], [btm, bC], pbr)
                P.op("dve", lambda e, ptr=ptr, n=n, mk=mk: e.tensor_tensor(
                    out=tl["a1"][0:n, 0:n], in0=ptr[0:n, 0:n], in1=mk[0:n, 0, 0:n], op=ALU.add),
                    reads=[pbr, bC], writes=[btl["a1"]])
                P.op("act", lambda e, n=n, tm=tm: e.activation(out=tl["DTi"][0:n, 0:n], in_=tl["a1"][0:n, 0:n],
                                                               func=AF.Exp, bias=tm[:, 56 + hd:57 + hd], scale=1.0),
                     reads=[btl["a1"], btm], writes=[btl["DTi"]])
                P.op("dve", lambda e, ptr=ptr, n=n, mk=mk: e.scalar_tensor_tensor(
                    out=tl["a1"][0:n, 0:n], in0=ptr[0:n, 0:n], scalar=-1.0, in1=mk[0:n, 1, 0:n],
                    op0=ALU.mult, op1=ALU.add), reads=[pbr, bC, btl["a1"]], writes=[btl["a1"]])
                P.op("act", lambda e, n=n, tm=tm: e.activation(out=tl["Dst"][0:n, 0:n], in_=tl["a1"][0:n, 0:n],
                                                               func=AF.Exp, bias=tm[:, 16 + hd:17 + hd], scale=1.0),
                     reads=[btl["a1"], btm], writes=[btl["Dst"]])
                P.op("act", lambda e, ptr=ptr, n=n: e.activation(out=tl["Egc"][:, 0:n], in_=ptr[:, 0:n], func=AF.Exp),
                     reads=[pbr], writes=[btl["Egc"]])
                ptg, pbg = bank()
                mm1(ptg[0:n, 0:n], kT[:, c0:c0 + n], kT[:, c0:c0 + n], rk, pbg)
                mm1(ptg[0:n, 128:128 + n], kT[:, c0:c0 + n], qT[:, c0:c0 + n], rk + rq, pbg)
                P.op("dve", lambda e, ptg=ptg, n=n, tm=tm: e.scalar_tensor_tensor(
                    out=tl["L"][0:n, 0:n], in0=ptg[0:n, 0:n], scalar=tm[:, hd:hd + 1], in1=tl["Dst"][0:n, 0:n],
                    op0=ALU.mult, op1=ALU.mult), reads=[pbg, btm, btl["Dst"]], writes=[btl["L"]])
                P.op("dve", lambda e, ptg=ptg, n=n: e.tensor_tensor(
                    out=bv("QKT")[0:n, 0:n], in0=ptg[0:n, 128:128 + n], in1=tl["DTi"][0:n, 0:n], op=ALU.mult),
                    reads=[pbg, btl["DTi"]], writes=[btl["QKT"]])
                ptt, pbt = bank()
                P.op("pe", lambda e, ptt=ptt, n=n: e.transpose(ptt[0:n, 0:n], tl["L"][0:n, 0:n], ident[0:n, 0:n]),
                     reads=[btl["L"], bC], writes=[pbt])
                cp("act", tl["LT"][0:n, 0:n], ptt[0:n, 0:n], [pbt], [btl["LT"]])
                P.op("dve", lambda e, n=n: e.tensor_tensor(out=tl["M"][0:n, 0:n], in0=ident[0:n, 0:n],
                                                           in1=tl["LT"][0:n, 0:n], op=ALU.subtract),
                     reads=[btl["LT"], bC], writes=[btl["M"]])
                nlev = 5 if kind == "P" else 1
                Pc, PTc = "L", "LT"
                for lv in range(nlev):
                    last = lv == nlev - 1
                    pa, pba = bank()
                    mm1(pa[0:n, 0:n], tl[PTc][0:n, 0:n], tl[Pc][0:n, 0:n], [btl[Pc], btl[PTc]], pba)
                    if not last:
                        mm1(pa[0:n, 128:128 + n], tl[Pc][0:n, 0:n], tl[PTc][0:n, 0:n], [btl[Pc], btl[PTc]], pba)
                    nP = "P" if Pc == "L" else "L"
                    nPT = "PT" if PTc == "LT" else "LT"
                    cp("act", tl[nP][0:n, 0:n], pa[0:n, 0:n], [pba], [btl[nP]])
                    if not last:
                        cp("dve", tl[nPT][0:n, 0:n], pa[0:n, 128:128 + n], [pba], [btl[nPT]])
                        PTc = nPT
                    Pc = nP
                    pm, pbm = bank()
                    mm1(pm[0:n, 0:n], tl[Pc][0:n, 0:n], tl["M"][0:n, 0:n], [btl[Pc], btl["M"]], pbm)
                    P.op("dve", lambda e, pm=pm, n=n: e.tensor_tensor(
                        out=tl["M"][0:n, 0:n], in0=tl["M"][0:n, 0:n], in1=pm[0:n, 0:n], op=ALU.add),
                        reads=[pbm, btl["M"]], writes=[btl["M"]])
                pk, pbk = bank()
                P.op("pe", lambda e, pk=pk, n=n, c0=c0: e.transpose(pk[0:n, 0:128], kT[:, c0:c0 + n], ident),
                     reads=rk + [bC], writes=[pbk])
                P.op("pe", lambda e, pk=pk, n=n, c0=c0: e.transpose(pk[0:n, 128:256], vT[:, c0:c0 + n], ident),
                     reads=rv + [bC], writes=[pbk])
                P.op("act", lambda e, pk=pk, n=n, tm=tm: e.activation(
                    out=tl["kbe"][0:n, :], in_=pk[0:n, 0:128], func=AF.Identity, scale=tm[:, 40 + hd:41 + hd]),
                    reads=[pbk, btm], writes=[btl["kbe"]])
                P.op("dve", lambda e, pk=pk, n=n, tm=tm: e.tensor_scalar(
                    out=bv("kdec")[0:n, 0:128], in0=pk[0:n, 0:128], scalar1=tm[:, 48 + hd:49 + hd], scalar2=None,
                    op0=ALU.mult), reads=[pbk, btm], writes=[btl["kdec"]])
                P.op("dve", lambda e, pk=pk, n=n, tm=tm: e.tensor_scalar(
                    out=tl["vb"][0:n, :], in0=pk[0:n, 128:256], scalar1=tm[:, hd:hd + 1], scalar2=None,
                    op0=ALU.mult), reads=[pbk, btm], writes=[btl["vb"]])
                pu, pbu = bank()
                mm1(pu[0:n, 0:128], tl["M"][0:n, 0:n], tl["vb"][0:n, :], [btl["M"], btl["vb"]], pbu)
                mm1(pu[:, 128:128 + n], tl["kbe"][0:n, :], tl["M"][0:n, 0:n], [btl["M"], btl["kbe"]], pbu)
                cp("act", tl["u"][0:n, :], pu[0:n, 0:128], [pbu], [btl["u"]])
                cp("dve", bv("wT")[:, 0:n], pu[:, 128:128 + n], [pbu], [btl["wT"]])
                P.op("dve", lambda e, n=n, c0=c0: e.tensor_tensor(out=bv("qdec")[:, 0:n], in0=qT[:, c0:c0 + n],
                                                                  in1=tl["Egc"][:, 0:n], op=ALU.mult),
                     reads=rq + [btl["Egc"]], writes=[btl["qdec"]])
                if kind == "P":
                    pto, pbo = recur_prompt(S, bS, bv("qdec"), btl["qdec"], bv("QKT"), btl["QKT"],
                                            bv("kdec"), btl["kdec"], None, None,
                                            lambda col: tl["Egc"][:, col:col + 1], [btl["Egc"]],
                                            wT=bv("wT"), bwT=btl["wT"], u=tl["u"], bu=btl["u"])
                else:
                    pto, pbo = recur_sample(p, hd, sgdn, gdn_s, tl["qdec"], btl["qdec"], tl["QKT"], btl["QKT"],
                                            tl["kdec"], btl["kdec"], None, None,
                                            tl["Egc"][:, 3 * SPP:4 * SPP].unsqueeze(2).to_broadcast([128, SPP, 128]),
                                            [btl["Egc"]], wT=tl["wT"], bwT=btl["wT"], u=tl["u"], bu=btl["u"])
                o_finish_gdn(hd, hl, c0, n, pto, pbo)
            state_out(p, hd, si, gdn_p[hd])

        def ret_head(p, hd):
            hl = hd % 4
            wq = load_wu(w_in[32 + hd * 4 + 0])
            wk = load_wu(w_in[32 + hd * 4 + 1])
            wv = load_wu(w_in[32 + hd * 4 + 2])
            S, bS, si = state_in(p, 8 + hd)
            P.dma("sp", rdm, crdm["P"][hd], writes=[brdm], stream="rdm")
            P.dma("sp", rqd, crqd["P"][hd], writes=[brqd], stream="rqd")
            for comp, wi in ((0, wq), (1, wk)):
                for s, (c0, n) in enumerate(segs):
                    pt, pb = fm_proj(wi, s)
                    j = s % 2
                    cp("act", tmpf[j][:, 0:n], pt[:, 0:n], [pb], [btmpf[j]])
                    pt2, pb2 = bank()
                    mm1(pt2[:, 0:n], rsw, tmpf[j][:, 0:n], [btmpf[j], bC], pb2)
                    P.op("dve", lambda e, j=j, c0=c0, n=n, pt2=pt2: e.tensor_tensor(
                        out=sq[j][:, 0:n], in0=pt2[:, 0:n], in1=rot[:, 1, c0:c0 + n], op=ALU.mult),
                        reads=[pb2, brot], writes=[bsq[j]])
                    P.op("dve", lambda e, j=j, c0=c0, n=n: e.tensor_tensor(
                        out=tmpf[j][:, 0:n], in0=tmpf[j][:, 0:n], in1=rot[:, 0, c0:c0 + n], op=ALU.mult),
                        reads=[btmpf[j], brot], writes=[btmpf[j]])
                    P.op("dve", lambda e, j=j, c0=c0, n=n, comp=comp: e.tensor_tensor(
                        out=FM[comp][:, c0:c0 + n], in0=tmpf[j][:, 0:n], in1=sq[j][:, 0:n], op=ALU.add),
                        reads=[btmpf[j], bsq[j]], writes=[bFM[comp][s]])
            wgt = load_wu(w_in[32 + hd * 4 + 3])
            for s, (c0, n) in enumerate(segs):
                pt, pb = fm_proj(wgt, s)
                P.op("act", lambda e, pt=pt, c0=c0, n=n: e.activation(out=szT[:, c0:c0 + n], in_=pt[:, 0:n],
                                                                      func=AF.Silu), reads=[pb], writes=[bszT[s]])
            qT, kT = FM[0], FM[1]
            for ti, (kind, n, c0) in enumerate(tile_list()):
                ss = segs_of(c0, n)
                rq = [bFM[0][s] for s in ss]
                rk = [bFM[1][s] for s in ss]
                bfk = (kind == "P")
                bv = (lambda nm: tl[nm].bitcast(BF16)) if bfk else (lambda nm: tl[nm])
                if kind == "S":
                    P.dma("sp", rdm, crdm["S"][hd], writes=[brdm], stream="rdm")
                    P.dma("sp", rqd, crqd["S"][hd], writes=[brqd], stream="rqd")
                pv, pbv = bank()
                P.op("pe", [lambda e, kc=kc, pv=pv, c0=c0, n=n: e.matmul(
                    pv[0:n, 0:128], hT[:, kc, c0:c0 + n], WU[wv][:, kc, :], start=(kc == 0), stop=(kc == KC - 1))
                    for kc in range(KC)],
                    reads=[bWU[wv]] + [bh[kc][s] for kc in range(KC) for s in ss], writes=[pbv])
                cp("act", bv("vTM")[0:n, 0:128], pv[0:n, 0:128], [pbv], [btl["vTM"]])
                ptg, pbg = bank()
                mm1(ptg[0:n, 0:n], kT[:, c0:c0 + n], qT[:, c0:c0 + n], rk + rq, pbg)
                P.op("dve", lambda e, ptg=ptg, n=n: e.tensor_tensor(
                    out=bv("QKT")[0:n, 0:n], in0=ptg[0:n, 0:n], in1=rdm[0:n, 0:n], op=ALU.mult),
                    reads=[pbg, brdm], writes=[btl["QKT"]])
                pk, pbk = bank()
                P.op("pe", lambda e, pk=pk, n=n, c0=c0: e.transpose(pk[0:n, 0:128], kT[:, c0:c0 + n], ident),
                     reads=rk + [bC], writes=[pbk])
                P.op("dve", lambda e, pk=pk, n=n, kind=kind: e.tensor_scalar(
                    out=bv("kdec")[0:n, 0:128], in0=pk[0:n, 0:128], scalar1=rkd[kind][0:n, hd:hd + 1], scalar2=None,
                    op0=ALU.mult), reads=[pbk, bC], writes=[btl["kdec"]])
                P.op("dve", lambda e, n=n, c0=c0: e.tensor_tensor(out=bv("qdec")[:, 0:n], in0=qT[:, c0:c0 + n],
                                                                  in1=rqd[:, 0:n], op=ALU.mult),
                     reads=rq + [brqd], writes=[btl["qdec"]])
                if kind == "P":
                    pto, pbo = recur_prompt(S, bS, bv("qdec"), btl["qdec"], bv("QKT"), btl["QKT"],
                                            bv("kdec"), btl["kdec"], bv("vTM"), btl["vTM"],
                                            lambda col: rgc["P"][:, hd:hd + 1], [bC])
                else:
                    pto, pbo = recur_sample(p, hd, sret, ret_s, tl["qdec"], btl["qdec"], tl["QKT"], btl["QKT"],
                                            tl["kdec"], btl["kdec"], tl["vTM"], btl["vTM"],
                                            rgc["S"][:, hd:hd + 1].unsqueeze(2).to_broadcast([128, SPP, 128]), [bC])
                o_finish_ret(hd, hl, c0, n, pto, pbo)
            state_out(p, 8 + hd, si, ret_p[hd])

        def out_proj16(p):
            for dc in range(KC):
                wi = load_wu(w_out[dc])
                for s, (c0, n) in enumerate(segs):
                    pt, pb = bank()
                    P.op("pe", [lambda e, k=k, pt=pt, c0=c0, n=n, wi=wi: e.matmul(
                        pt[:, 0:n], WU[wi][:, k, :], oTall[:, k, c0:c0 + n],
                        start=(k == 0), stop=(k == 15)) for k in range(16)],
                        reads=[bWU[wi]] + [boT[k][s] for k in range(16)], writes=[pb])
                    gate_evac(pt, pb, p, dc, s, g_off[1])

        def run_pair(f0, f1):
            recs = []
            for li, f in enumerate((f0, f1)):
                CL[0] = lanes[li]
                REC[0] = True
                P.rec_target = []
                P.rec_suffix = f"_L{li}"
                f()
                recs.append(P.rec_target)
                P.rec_target = None
                REC[0] = False
            CL[0] = lanes[0]
            merged = []
            idx = [0, 0]
            while idx[0] < len(recs[0]) or idx[1] < len(recs[1]):
                for li in range(2):
                    r = recs[li]
                    if idx[li] >= len(r):
                        continue
                    a = r[idx[li]][5]
                    merged.append(r[idx[li]])
                    idx[li] += 1
                    if a is not None:
                        while idx[li] < len(r) and r[idx[li]][5] == a:
                            merged.append(r[idx[li]])
                            idx[li] += 1
            P.replay(merged)

        def mixer(p):
            phase_barrier()
            P.dma("sp", xscr, xT, reads=bx_all, writes=[bxscr], stream="xscr_o")
            tm_scalars(p)
            stage(3.1)
            phase_barrier()
            for pr in range(4):
                run_pair(lambda: gdn_head(p, 2 * pr), lambda: gdn_head(p, 2 * pr + 1))
                stage(3.8)
            phase_barrier()
            stage(3.9)
            P.dma("sp", rot, crot[p].rearrange("c p t -> p c t"), writes=[brot], stream="rot")
            for pr in range(4):
                run_pair(lambda: ret_head(p, 2 * pr), lambda: ret_head(p, 2 * pr + 1))
            phase_barrier()
            P.dma("sp", xT, xscr, reads=[bxscr, bPH], writes=bx_all, stream="xin")
            ada_group(g_off[1] // 16)
            out_proj16(p)

        try:
            stage(0)
            for p in range(NPASS):
                P.dma("sp", xT, xin[p], writes=[b for r in bx for b in r], stream="xin")
                norm_mod(p, 0)
                stage(1)
                ffn(p, 0)
                dbg("x1", xT, [b for r in bx for b in r])
                stage(2)
                norm_mod(p, 1)
                stage(3)
                mixer(p)
                dbg("x2", xT, [b for r in bx for b in r])
                stage(4)
                norm_mod(p, 2)
                ffn(p, 1)
                stage(5)
                norm_mod(p, 3, final=True)
                stage(6 + p)
        except _Stop:
            pass
        dbg("ada", adaT, bada)

        P.emit()
    return nc, ins_spec, outs_spec


_CACHE = {}


def _get_program():
    if "nc" not in _CACHE:
        _CACHE["nc"] = build_program()
    return _CACHE["nc"]


def kernel(x_prompt, x_sample, state_gdn, state_conv, state_ret, c_prompt, c_sample,
           w_ada, b_ada, norm_ffn1, w1_gate, w1_up, w1_down, norm_mix, w_in, conv_w, a_log, dt_bias,
           gdn_norm_w, ret_gn_w, ret_gn_b, w_out, norm_ffn2, w2_gate, w2_up, w2_down,
           w_ada_final, b_ada_final, norm_final):
    f = lambda a: np.asarray(a, dtype=np.float32)
    x_prompt, x_sample = f(x_prompt), f(x_sample)
    state_gdn, state_conv, state_ret = f(state_gdn)[0], f(state_conv)[0], f(state_ret)[0]
    c_prompt, c_sample = f(c_prompt), f(c_sample)
    nc, ins_spec, outs_spec = _get_program()
    consts = host_consts()

    shared = {}
    shared["w_ada"] = np.concatenate([tile_w(f(w_ada)[0]), tile_w(f(w_ada_final))], 0)
    ball = np.concatenate([f(b_ada)[0], f(b_ada_final)])
    shared["b_ada"] = np.ascontiguousarray(ball.reshape(176, 128).T)
    nrm = np.stack([f(norm_ffn1)[0], f(norm_mix)[0], f(norm_ffn2)[0], f(norm_final)], 0)
    shared["normw"] = np.ascontiguousarray(nrm.reshape(4, KC, 128).transpose(2, 0, 1))
    shared["w1g"], shared["w1u"] = tile_w(f(w1_gate)[0]), tile_w(f(w1_up)[0])
    shared["w2g"], shared["w2u"] = tile_w(f(w2_gate)[0]), tile_w(f(w2_up)[0])
    shared["w1d"] = np.ascontiguousarray(f(w1_down)[0].reshape(NFC, 128, D))
    shared["w2d"] = np.ascontiguousarray(f(w2_down)[0].reshape(NFC, 128, D))
    win = f(w_in)[0]
    cols = []
    for hd in range(8):
        for blk in (0, 1, 2, 3):
            o = blk * 1024 + hd * 128
            cols.append(np.arange(o, o + 128))
    o5 = 3072 + 1024 + 16
    for hd in range(8):
        for blk in (0, 1, 2, 3):
            o = o5 + blk * 1024 + hd * 128
            cols.append(np.arange(o, o + 128))
    cols = np.concatenate(cols)
    shared["w_in"] = tile_w(win[:, cols])
    wba_ = win[:, 4096:4112]
    shared["w_ba"] = np.ascontiguousarray(wba_.reshape(KC, 128, 16).transpose(1, 0, 2))
    wo = f(w_out)[0]
    shared["w_out"] = tile_w(wo)
    shared["convw"] = np.ascontiguousarray(f(conv_w)[0].reshape(4, 24, 128).transpose(2, 1, 0))
    smallv = np.zeros((128, 32), np.float32)
    smallv[:, 0:8] = f(dt_bias)[0][None, :]
    smallv[:, 8:16] = f(a_log)[0][None, :]
    smallv[:, 16] = f(gdn_norm_w)[0]
    smallv[:, 17:25] = f(ret_gn_w)[0].reshape(8, 128).T
    shared["smallv"] = smallv
    shared["gnb"] = np.ascontiguousarray(f(ret_gn_b)[0].reshape(8, 128).T)
    for k in ("ident", "ones", "rsw", "maskP", "maskS", "rdmP", "rdmS", "rqdP", "rqdS", "rkdP", "rkdS",
              "rgcP", "rgcS", "onehot", "rot", "cst"):
        shared[k] = consts[k]

    in_maps = []
    for c in range(NCORES):
        b = c % 4
        m = dict(shared)
        xin = np.zeros((NPASS, T, D), np.float32)
        sq0 = c * SPC
        for p in range(NPASS):
            xin[p, :TPP] = x_prompt[b, p * TPP:(p + 1) * TPP]
            xs = x_sample[sq0 + p * SPP: sq0 + (p + 1) * SPP]
            xin[p, TPP:] = xs.transpose(1, 0, 2).reshape(TS, D)
        m["xin"] = np.ascontiguousarray(xin.reshape(NPASS, T, KC, 128).transpose(0, 3, 2, 1))
        cvec = np.concatenate([c_prompt[b:b + 1], c_sample[sq0:sq0 + SPC]], 0)
        m["cv"] = np.ascontiguousarray(cvec.reshape(17, KC, 128).transpose(2, 1, 0))
        sg_ = state_gdn[sq0:sq0 + SPC].reshape(NPASS, SPP, 8, 128, 128)
        m["sgdn"] = np.ascontiguousarray(sg_.transpose(0, 2, 3, 1, 4))
        sr_ = state_ret[sq0:sq0 + SPC].reshape(NPASS, SPP, 8, 128, 128)
        m["sret"] = np.ascontiguousarray(sr_.transpose(0, 2, 3, 1, 4))
        sc_ = state_conv[sq0:sq0 + SPC].reshape(NPASS, SPP, 3, 24, 128)
        m["sconv"] = np.ascontiguousarray(sc_.transpose(0, 3, 4, 2, 1))
        for k, shp in ins_spec.items():
            assert m[k].shape == shp, (k, m[k].shape, shp)
            assert m[k].dtype == np.float32
        in_maps.append({k: m[k] for k in ins_spec})

    res = run_bass_kernel_spmd(nc, in_maps, core_ids=list(range(NCORES)))
    R = res.results
    _CACHE["last"] = R

    y_prompt = np.zeros((4, 2048, D), np.float32)
    y_sample = np.zeros((128, 4, D), np.float32)
    gdn_p = np.zeros((1, 4, 8, 128, 128), np.float32)
    conv_p = np.zeros((1, 4, 3, 3072), np.float32)
    ret_p = np.zeros((1, 4, 8, 128, 128), np.float32)
    gdn_s = np.zeros((1, 128, 8, 128, 128), np.float32)
    conv_s = np.zeros((1, 128, 3, 3072), np.float32)
    ret_s = np.zeros((1, 128, 8, 128, 128), np.float32)
    for c in range(NCORES):
        r = R[c]
        sq0 = c * SPC
        y = np.asarray(r["yout"]).transpose(0, 3, 2, 1).reshape(NPASS, T, D)
        for p in range(NPASS):
            if c < 4:
                y_prompt[c, p * TPP:(p + 1) * TPP] = y[p, :TPP]
            y_sample[sq0 + p * SPP: sq0 + (p + 1) * SPP] = y[p, TPP:].reshape(4, SPP, D).transpose(1, 0, 2)
        if c < 4:
            gdn_p[0, c] = r["gdn_p"]
            ret_p[0, c] = r["ret_p"]
            conv_p[0, c] = r["conv_p"]
        gs = np.asarray(r["gdn_s"]).transpose(0, 3, 1, 2, 4).reshape(SPC, 8, 128, 128)
        rs = np.asarray(r["ret_s"]).transpose(0, 3, 1, 2, 4).reshape(SPC, 8, 128, 128)
        cs = np.asarray(r["conv_s"]).transpose(0, 2, 1, 3).reshape(SPC, 3, 3072)
        gdn_s[0, sq0:sq0 + SPC] = gs
        ret_s[0, sq0:sq0 + SPC] = rs
        conv_s[0, sq0:sq0 + SPC] = cs
    return (y_prompt, y_sample, gdn_p, conv_p, ret_p, gdn_s, conv_s, ret_s)
```

```python
import contextlib
import numpy as np
import concourse.bass as bass
import concourse.mybir as mybir
from concourse.bass_utils import run_bass_kernel_spmd

F32 = mybir.dt.float32
BF16 = mybir.dt.bfloat16
ALU = mybir.AluOpType
AF = mybir.ActivationFunctionType
AX = mybir.AxisListType

NCORES = 8
D = 2048
KC = 16
DFF = 5632
NFC = 44
NPASS = 2
TPP = 2048 // NPASS
NPT = TPP // 128
SPC = 16
SPP = SPC // NPASS
TS = 4 * SPP
T = TPP + TS
NSEG = 3
SEG = T // NSEG
assert SEG * NSEG == T and SEG <= 512
PAST = 16384
EPS = 1e-6
NEG = -30000.0
CU = 128
NWU = 6
NWD = 4

DEBUG = {}
STAGE = None


class _Stop(Exception):
    pass


class Buf:
    __slots__ = ("name", "w", "r")

    def __init__(self, name):
        self.name = name
        self.w = None
        self.r = []


class Op:
    __slots__ = ("eng", "fns", "deps", "signal", "semval", "stream", "dval")

    def __init__(self, eng, fns):
        self.eng = eng
        self.fns = fns
        self.deps = []
        self.signal = False
        self.semval = 0
        self.stream = None
        self.dval = 0


CL = [None]


class PX:
    def __init__(self, name):
        self._n = name

    def _t(self):
        return getattr(CL[0], self._n)

    def __getitem__(self, k):
        return self._t()[k]

    def __iter__(self):
        return iter(self._t())

    def __getattr__(self, a):
        return getattr(self._t(), a)


class BX:
    def __init__(self, name):
        self._n = name

    def get(self):
        return getattr(CL[0], self._n)


class _Rec:
    def __init__(self):
        self.calls = []

    def __getattr__(self, name):
        def f(*a, **k):
            a = tuple(x._t() if isinstance(x, PX) else x for x in a)
            k = {kk: (v._t() if isinstance(v, PX) else v) for kk, v in k.items()}
            self.calls.append((name, a, k))
            return self
        return f


class Lane:
    pass


class Prog:
    ENGS = ("pe", "act", "dve", "pool", "sp")

    def __init__(self, nc):
        self.nc = nc
        self.ops = {e: [] for e in self.ENGS}
        self.streams = {}
        self.last_stream_op = {}
        self.ovset = set()
        self.pstok = {}
        self.bPH = Buf("phase")
        self.rec_target = None
        self.rec_suffix = ""
        self.cur_atomic = None

    def buf(self, name):
        return Buf(name)

    def bufs(self, name, n):
        return [Buf(f"{name}{i}") for i in range(n)]

    def _add(self, eng, fns, reads, writes, stream=None):
        if not isinstance(fns, (list, tuple)):
            fns = [fns]
        rec = _Rec()
        for f in fns:
            f(rec)
        calls = rec.calls
        reads = [b.get() if isinstance(b, BX) else b for b in reads]
        writes = [b.get() if isinstance(b, BX) else b for b in writes]
        if self.rec_target is not None:
            if stream is not None and not stream.startswith("WU"):
                stream = stream + self.rec_suffix
            self.rec_target.append((eng, calls, reads, writes, stream, self.cur_atomic))
            return None
        return self._add_real(eng, calls, reads, writes, stream)

    def replay(self, entries):
        for (eng, calls, reads, writes, stream, _a) in entries:
            self._add_real(eng, calls, reads, writes, stream)

    def _add_real(self, eng, calls, reads, writes, stream=None):
        op = Op(eng, calls)
        deps = {}
        if any(id(b) in self.ovset for b in writes) and self.bPH not in reads:
            reads = list(reads) + [self.bPH]
        if eng != "pe":
            toks = [self.pstok[id(b)] for b in reads if id(b) in self.pstok]
            if toks:
                writes = list(writes) + toks
        for b in reads:
            if b.w is not None:
                deps[id(b.w)] = b.w
        for b in writes:
            if b.w is not None:
                deps[id(b.w)] = b.w
            for r in b.r:
                deps[id(r)] = r
        for d in deps.values():
            if d is op:
                continue
            if d.eng == "pe" and eng == "pe" and d.stream is None and stream is None:
                continue
            op.deps.append(d)
            d.signal = True
        for b in reads:
            b.r.append(op)
        for b in writes:
            b.w = op
            b.r = []
        if stream is not None:
            op.stream = stream
            self.streams[stream] = self.streams.get(stream, 0) + 1
            op.dval = 16 * self.streams[stream]
            self.last_stream_op[stream] = op
        self.ops[eng].append(op)
        return op

    def barrier(self, fn):
        rec = _Rec()
        fn(rec)
        op = Op("dve", rec.calls)
        for e in self.ENGS:
            for o in reversed(self.ops[e]):
                if o.stream is None:
                    if o is not op:
                        op.deps.append(o)
                        o.signal = True
                    break
        for so in self.last_stream_op.values():
            op.deps.append(so)
        b = self.bPH
        b.w = op
        b.r = []
        self.ops["dve"].append(op)
        return op

    def op(self, eng, fns, reads=(), writes=()):
        return self._add(eng, fns, reads, writes)

    def dma(self, eng, out, in_, reads=(), writes=(), stream=None, **kw):
        assert stream is not None
        out = out._t() if isinstance(out, PX) else out
        in_ = in_._t() if isinstance(in_, PX) else in_
        return self._add(eng, [lambda e, o=out, i=in_, k=kw: e.dma_start(out=o, in_=i, **k)],
                         reads, writes, stream=stream)

    def emit(self, final_wait_eng="sp"):
        nc = self.nc
        for e in self.ENGS:
            c = 0
            for op in self.ops[e]:
                if op.stream is None and op.signal:
                    c += 1
                    op.semval = c
        with contextlib.ExitStack() as st:
            esem = {e: st.enter_context(nc.semaphore("s_" + e)) for e in self.ENGS}
            dsem = {s: st.enter_context(nc.semaphore("d_" + s)) for s in self.streams}
            block = st.enter_context(nc.Block())
            engobj = {"pe": block.tensor, "act": block.scalar, "dve": block.vector,
                      "pool": block.gpsimd, "sp": block.sync}

            def run(ename):
                def body(e):
                    seen = {}
                    for op in self.ops[ename]:
                        need = {}
                        for d in op.deps:
                            if d.stream is not None:
                                key, val, sem = "d_" + d.stream, d.dval, dsem[d.stream]
                            else:
                                key, val, sem = d.eng, d.semval, esem[d.eng]
                            if val > need.get(key, (0, None))[0]:
                                need[key] = (val, sem)
                        for key, (val, sem) in need.items():
                            if seen.get(key, 0) >= val:
                                continue
                            seen[key] = val
                            e.wait_ge(sem, val)
                        ins = None
                        for (name, a, k) in op.fns:
                            ins = getattr(e, name)(*a, **k)
                        if op.stream is not None:
                            ins.then_inc(dsem[op.stream], 16)
                        elif op.signal:
                            ins.then_inc(esem[ename], 1)
                    if ename == final_wait_eng:
                        for s, n in self.streams.items():
                            e.wait_ge(dsem[s], 16 * n)
                        for en in self.ENGS:
                            c = max([o.semval for o in self.ops[en]] + [0])
                            if c:
                                e.wait_ge(esem[en], c)
                return body

            for ename in self.ENGS:
                engobj[ename](run(ename))


def _tile_blocks(kind):
    if kind == "P":
        i = np.arange(128)
        return 128, i // 64, i % 64
    i = np.arange(TS)
    return TS, i % SPP, i // SPP


def host_consts():
    c = {}
    c["ident"] = np.eye(128, dtype=np.float32)
    c["ones"] = np.ones((128, 128), np.float32)
    rsw = np.zeros((128, 128), np.float32)
    for p in range(64):
        rsw[p + 64, p] = 1.0
        rsw[p, p + 64] = 1.0
    c["rsw"] = rsw
    lg = np.log(1.0 - 2.0 ** (-5.0 - np.arange(8, dtype=np.float32))).astype(np.float32)
    for kind in ("P", "S"):
        n, blk, pos = _tile_blocks(kind)
        same = blk[:, None] == blk[None, :]
        m = np.zeros((4, 128, 128), np.float32)
        m[0, :n, :n] = np.where(same & (pos[None, :] >= pos[:, None]), 0.0, NEG)
        m[1, :n, :n] = np.where(same & (pos[:, None] > pos[None, :]), 0.0, NEG)
        m[2, :n, :n] = np.where(same & (pos[:, None] <= pos[None, :]), 1.0, 0.0)
        m[3, :n, :n] = np.where(same, 1.0, 0.0)
        c["mask" + kind] = m if kind == "P" else np.ascontiguousarray(m[:, :, :TS])
        C = 64 if kind == "P" else 4
        rel = (pos[None, :] - pos[:, None]).astype(np.float32)
        rdm = np.zeros((8, 128, 128), np.float32)
        rqd = np.zeros((8, 128, 128), np.float32)
        rkd = np.zeros((128, 8), np.float32)
        rgc = np.zeros((128, 8), np.float32)
        for h in range(8):
            dm = np.where(same & (rel >= 0), np.exp(np.maximum(rel, 0.0) * lg[h]), 0.0)
            rdm[h, :n, :n] = dm * (128.0 ** -0.5)
            rqd[h, :, :n] = np.exp((pos.astype(np.float32) + 1.0) * lg[h])[None, :]
            rkd[:n, h] = np.exp((C - 1.0 - pos.astype(np.float32)) * lg[h]) * (128.0 ** -0.5)
            rgc[:, h] = np.exp(C * lg[h])
        c["rdm" + kind] = rdm.astype(np.float32)
        c["rqd" + kind] = rqd.astype(np.float32)
        c["rkd" + kind] = rkd.astype(np.float32)
        c["rgc" + kind] = rgc.astype(np.float32)
    oh = np.zeros((128, SPP), np.float32)
    for i in range(TS):
        oh[i, i % SPP] = 1.0
    c["onehot"] = oh
    half = 64
    inv = (10000.0 ** (-np.arange(half, dtype=np.float32) / np.float32(half))).astype(np.float32)
    rot = np.zeros((NPASS, 2, 128, T), np.float32)
    for p in range(NPASS):
        pos = np.concatenate([p * TPP + np.arange(TPP), PAST + (np.arange(TS) // SPP)]).astype(np.float32)
        ang = (pos[None, :] * inv[:, None]).astype(np.float32).astype(np.float64)
        cs, sn = np.cos(ang), np.sin(ang)
        rot[p, 0, :64] = cs
        rot[p, 0, 64:] = cs
        rot[p, 1, :64] = -sn
        rot[p, 1, 64:] = sn
    c["rot"] = rot
    cst = np.zeros((128, 8), np.float32)
    cst[:, 0] = EPS
    cst[:, 1] = 128.0 * EPS
    cst[:, 2] = 1.0
    c["cst"] = cst
    return c


def tile_w(W, cu=CU):
    K, M = W.shape
    return np.ascontiguousarray(W.reshape(K // 128, 128, M // cu, cu).transpose(2, 1, 0, 3))


def build_program():
    nc = bass.Bass("TRN2", target_bir_lowering=False)
    ins_spec = {}
    outs_spec = {}

    def din(name, shape):
        ins_spec[name] = tuple(shape)
        return nc.dram_tensor(name, list(shape), F32, kind="ExternalInput").ap()

    def dout(name, shape):
        outs_spec[name] = tuple(shape)
        return nc.dram_tensor(name, list(shape), F32, kind="ExternalOutput").ap()

    xin = din("xin", [NPASS, 128, KC, T])
    cv = din("cv", [128, KC, 17])
    sgdn = din("sgdn", [NPASS, 8, 128, SPP, 128])
    sret = din("sret", [NPASS, 8, 128, SPP, 128])
    sconv = din("sconv", [NPASS, 24, 128, 3, SPP])
    w_ada = din("w_ada", [176, 128, KC, CU])
    b_ada = din("b_ada", [128, 176])
    normw = din("normw", [128, 4, KC])
    wg = [din("w1g", [NFC, 128, KC, CU]), din("w2g", [NFC, 128, KC, CU])]
    wu = [din("w1u", [NFC, 128, KC, CU]), din("w2u", [NFC, 128, KC, CU])]
    wd = [din("w1d", [NFC, 128, D]), din("w2d", [NFC, 128, D])]
    w_in = din("w_in", [64, 128, KC, CU])
    w_ba = din("w_ba", [128, KC, 16])
    w_out = din("w_out", [16, 128, KC, CU])
    convw = din("convw", [128, 24, 4])
    smallv = din("smallv", [128, 32])
    gnb = din("gnb", [128, 8])
    cident = din("ident", [128, 128])
    cones = din("ones", [128, 128])
    crsw = din("rsw", [128, 128])
    cmask = {"P": din("maskP", [4, 128, 128]), "S": din("maskS", [4, 128, TS])}
    crdm = {"P": din("rdmP", [8, 128, 128]), "S": din("rdmS", [8, 128, 128])}
    crqd = {"P": din("rqdP", [8, 128, 128]), "S": din("rqdS", [8, 128, 128])}
    crkd = {"P": din("rkdP", [128, 8]), "S": din("rkdS", [128, 8])}
    crgc = {"P": din("rgcP", [128, 8]), "S": din("rgcS", [128, 8])}
    conehot = din("onehot", [128, SPP])
    crot = din("rot", [NPASS, 2, 128, T])
    ccst = din("cst", [128, 8])

    yout = dout("yout", [NPASS, 128, KC, T])
    gdn_p = dout("gdn_p", [8, 128, 128])
    ret_p = dout("ret_p", [8, 128, 128])
    conv_p = dout("conv_p", [3, 3072])
    gdn_s = dout("gdn_s", [NPASS, 8, 128, SPP, 128])
    ret_s = dout("ret_s", [NPASS, 8, 128, SPP, 128])
    conv_s = dout("conv_s", [NPASS, 3, SPP, 3072])
    dbg_outs = {k: dout("dbg_" + k, v) for k, v in DEBUG.items()}
    sscr = nc.dram_tensor("sscr", [16, 128, 128], F32).ap()
    xscr = nc.dram_tensor("xscr", [128, KC, T], F32).ap()

    st = contextlib.ExitStack()
    with st:
        def sb(name, shape, dt=F32):
            return st.enter_context(nc.sbuf_tensor("s_" + name, list(shape), dt))[:]

        P = Prog(nc)
        bPH = P.bPH

        def stage(k):
            if STAGE is not None and k >= STAGE:
                raise _Stop()

        xT = sb("xT", [128, KC, T])
        hT = sb("hT", [128, KC, T], BF16)
        WU = [sb(f"WU{i}", [128, KC, CU], BF16) for i in range(NWU)]
        adaT = sb("adaT", [128, 176, 17])
        Amod = sb("Amod", [128, 4, KC, 17])
        ident = sb("ident", [128, 128])
        ones = sb("ones", [128, 128])
        onesb = sb("onesb", [128, 128], BF16)
        rsw = sb("rsw", [128, 128])
        masks = {"P": sb("maskP", [128, 4, 128]), "S": sb("maskS", [128, 4, TS])}
        onehot = sb("onehot", [128, SPP])
        cst = sb("cst", [128, 8])
        normT = sb("normT", [128, 4, KC])
        badaT = sb("badaT", [128, 176])
        cwT = sb("cwT", [128, 24, 4])
        smv = sb("smv", [128, 32])
        gnbT = sb("gnbT", [128, 8])
        rkd = {"P": sb("rkdP", [128, 8]), "S": sb("rkdS", [128, 8])}
        rgc = {"P": sb("rgcP", [128, 8]), "S": sb("rgcS", [128, 8])}
        wba = sb("wba", [128, KC, 16], BF16)
        negA = sb("negA", [128, 8])
        TM = sb("TM", [128, NPT + 1, 64])
        Sw_ = [sb(f"Sw{i}", [128, 128]) for i in range(2)]
        ctail = sb("ctail", [128, 24, 3])
        sq0_ = [sb(f"sq{i}", [128, SEG]) for i in range(2)]
        tmpf0_ = [sb(f"tmpf{i}", [128, SEG]) for i in range(2)]

        OVB = 51840
        OV = sb("OV", [128, OVB // 4])

        XV = xT.rearrange("p a b -> p (a b)")
        XVB = KC * T * 4

        def carve(off, shape, dt=F32, base=None, limit=None):
            base = OV if base is None else base
            limit = OVB if limit is None else limit
            esz = 2 if dt == BF16 else 4
            nel = int(np.prod(shape))
            assert off % 4 == 0 and (nel * esz) % 4 == 0 and off + nel * esz <= limit, (off, shape)
            ap = base[:, off // 4:(off + nel * esz) // 4]
            if dt == BF16:
                ap = ap.bitcast(BF16)
            if len(shape) == 2:
                ap = ap.rearrange("p (a b) -> p a b", a=shape[0])
            return ap

        ctmp = carve(0, [KC, 17])
        scT = sb("scT", [128, KC, 17], BF16)
        act = [carve(0, [4, T], BF16), carve(8448, [4, T], BF16)]
        WD = [carve(16896 + i * 4096, [D], BF16) for i in range(NWD)]
        sg = [carve(33280 + j * 704, [SEG], BF16) for j in range(2)]
        oTall = carve(0, [16, T], BF16)
        rstd = carve(0, [T])
        Sall = carve(33792, [SPP, 128])
        wsel = carve(37888, [SPP, 128])
        tlnames = ["DTi", "Dst", "a1", "L", "LT", "Egc", "QKT", "M", "P", "PT", "kbe", "kdec", "vb", "u", "wT",
                   "qdec", "vnew"]
        lanes = []
        for li in range(2):
            L = Lane()
            L.id = li
            xo = 34688 * li
            cx = lambda off, shape, dt=F32, xo=xo: carve(xo + off, shape, dt, base=XV, limit=XVB)
            L.FM = [cx(i * 4224, [T]) for i in range(3)]
            L.szT = cx(12672, [T], BF16)
            L.cb = cx(14784, [3 + TPP])
            L.cbs = cx(18944, [7 * SPP])
            L.cvt = cx(19200, [T])
            if li == 0:
                L.rdm = cx(23424, [128])
                L.rqd = cx(23936, [128])
                L.crow = cx(24448, [384])
                L.tl = {k: cx(25984 + i * 512, [128]) for i, k in enumerate(tlnames)}
                L.Sallb = cx(25984, [SPP, 128], BF16)
                L.sq = sq0_
                L.tmpf = tmpf0_
            else:
                L.tl = {k: cx(23424 + i * 512, [128]) for i, k in enumerate(tlnames)}
                L.Sallb = cx(23424, [SPP, 128], BF16)
                L.sq = [carve(41984 + j * 1408, [SEG]) for j in range(2)]
                L.tmpf = [carve(44800 + j * 1408, [SEG]) for j in range(2)]
                L.crow = carve(47616, [384])
                L.rdm = carve(49152, [128])
                L.rqd = carve(49664, [128])
            L.Sw = Sw_[li]
            L.Sb = carve(66816 + 256 * li, [128], BF16, base=XV, limit=XVB)
            lanes.append(L)
        rot = carve(14784, [2, T], base=XV, limit=XVB)
        CL[0] = lanes[0]
        REC = [False]
        tl, FM, szT, cb, cbs, cvt, rdm, rqd, crow, sq, tmpf = (PX(n) for n in (
            "tl", "FM", "szT", "cb", "cbs", "cvt", "rdm", "rqd", "crow", "sq", "tmpf"))

        pst = [st.enter_context(nc.psum_tensor(f"ps{i}", [128, 512], F32))[:] for i in range(8)]
        bps = P.bufs("ps", 8)
        for b_ in bps:
            P.pstok[id(b_)] = Buf("tok_" + b_.name)
        psi = [0]

        def bank():
            if REC[0]:
                L = CL[0]
                i = 4 * L.id + 1 + (L.bk_i % 3)
                L.bk_i += 1
                return pst[i], bps[i]
            i = psi[0] % 8
            psi[0] += 1
            return pst[i], bps[i]

        def bank_o():
            if REC[0]:
                i = 4 * CL[0].id
                return pst[i], bps[i]
            return bank()

        def ovb(b):
            if isinstance(b, list):
                for x in b:
                    ovb(x)
            else:
                P.ovset.add(id(b))
            return b

        bx = [[P.buf(f"x{k}_{s}") for s in range(NSEG)] for k in range(KC)]
        bh = [[P.buf(f"h{k}_{s}") for s in range(NSEG)] for k in range(KC)]
        bWU = P.bufs("WU", NWU)
        bWD = ovb(P.bufs("WD", NWD))
        bC = P.buf("consts")
        bada = P.bufs("adaT", 11)
        bAmod = P.bufs("Amod", 4)
        bact = [ovb(P.bufs(f"act{i}_", 4)) for i in range(2)]
        bsg = ovb(P.bufs("sg", 2))
        brstd = ovb(P.bufs("rstd", NSEG))
        bTM = P.bufs("TM", NPT + 1)
        bsscr = P.bufs("sscr", 16)
        bxscr = P.buf("xscr")
        bctail = P.bufs("ctail", 24)
        boT = [ovb(P.bufs(f"oT{i}_", NSEG)) for i in range(16)]
        brot = ovb(P.buf("rot"))
        bSall = ovb(P.buf("Sall"))
        bwsel = ovb(P.buf("wsel"))
        for L in lanes:
            li = L.id
            L.bk_i = 0
            L.wu_i = 0
            L.bsq = P.bufs(f"sq_{li}_", 2) if li == 0 else ovb(P.bufs(f"sq_{li}_", 2))
            L.btmpf = P.bufs(f"tmpf_{li}_", 2) if li == 0 else ovb(P.bufs(f"tmpf_{li}_", 2))
            L.bSw = P.buf(f"Sw{li}")
            L.bSb = ovb(P.buf(f"Sb{li}"))
            L.bFM = [ovb(P.bufs(f"FM{li}_{i}_", NSEG)) for i in range(3)]
            L.bszT = ovb(P.bufs(f"szT{li}_", NSEG))
            for nm in ("cb", "cbs", "cvt", "rdm", "rqd", "crow"):
                setattr(L, "b" + nm, ovb(P.buf(f"{nm}{li}")))
            L.btl = {k: ovb(P.buf(f"tl{li}_" + k)) for k in tlnames}
            L.bSallb = [L.btl[k] for k in ("DTi", "Dst", "a1", "L")]
            for new, old in (("osb", "L"), ("osq", "LT"), ("rn", "P"), ("on", "PT"), ("mu", "a1"), ("msq", "DTi"),
                             ("var", "Dst"), ("cen", "M"), ("vTM", "vb")):
                L.tl[new] = L.tl[old]
                L.btl[new] = L.btl[old]
        btl, bFM, bszT = PX("btl"), PX("bFM"), PX("bszT")
        bcb, bcbs, bcvt, brdm, brqd, bcrow = (BX("b" + n) for n in ("cb", "cbs", "cvt", "rdm", "rqd", "crow"))
        bsq, btmpf = PX("bsq"), PX("btmpf")
        vexp, bvexp = wsel, bwsel
        bwba = P.buf("wba")
        bnegA = P.buf("negA")
        bscT = P.buf("scT")
        bx_all = [b for r in bx for b in r]

        segs = [(s * SEG, SEG) for s in range(NSEG)]

        def segs_of(c0, n):
            return sorted({c // SEG for c in (c0, c0 + n - 1)})

        wu_i = [0]
        wd_i = [0]

        def load_wu(src_ap, view=None):
            if REC[0]:
                L = CL[0]
                i = 3 * L.id + (L.wu_i % 3)
                L.wu_i += 1
            else:
                i = wu_i[0] % NWU
                wu_i[0] += 1
            dst = WU[i] if view is None else view(WU[i])
            P.dma("pool", dst, src_ap, writes=[bWU[i]], stream=f"WU{i}")
            return i

        def phase_barrier():
            P.barrier(lambda e: e.memset(cst[:, 7:8], 0.0))

        dbg_n = [0]

        def dbg(name, ap, reads):
            if name in dbg_outs:
                dbg_n[0] += 1
                P.dma("sp", dbg_outs[name], ap, reads=reads, stream=f"dbg{dbg_n[0]}")

        ci = [0]

        def cload(dst, src):
            ci[0] += 1
            P.dma("sp", dst, src, writes=[bC], stream=f"c{ci[0]}")

        cload(ident, cident)
        cload(ones, cones)
        cload(rsw, crsw)
        for k in ("P", "S"):
            cload(masks[k], cmask[k].rearrange("m p n -> p m n"))
            cload(rkd[k], crkd[k])
            cload(rgc[k], crgc[k])
        cload(onehot, conehot)
        cload(cst, ccst)
        cload(normT, normw)
        cload(badaT, b_ada)
        cload(cwT, convw)
        cload(smv, smallv)
        cload(gnbT, gnb)
        cload(ctmp, cv)
        P.dma("pool", wba, w_ba, writes=[bwba], stream="wba")
        P.op("act", lambda e: e.activation(out=negA, in_=smv[:, 8:16], func=AF.Exp), reads=[bC], writes=[bnegA])
        P.op("dve", lambda e: e.tensor_scalar(out=negA, in0=negA, scalar1=-1.0, scalar2=None, op0=ALU.mult),
             reads=[bnegA], writes=[bnegA])
        P.op("act", lambda e: e.activation(out=scT, in_=ctmp, func=AF.Silu), reads=[bC], writes=[bscT])
        P.op("dve", lambda e: e.memset(ctail, 0.0), writes=bctail)
        bonesb = P.buf("onesb")
        P.op("dve", lambda e: e.tensor_copy(out=onesb, in_=ones), reads=[bC], writes=[bonesb])

        sc_off = [16, 64, 112, 160]
        sh_off = [0, 48, 96, 144]
        g_off = [32, 80, 128]
        ada_done = set()

        def ada_group(gi):
            if gi in ada_done:
                return
            ada_done.add(gi)
            for u in range(16 * gi, 16 * gi + 16):
                wi = load_wu(w_ada[u])
                pt, pb = bank()
                P.op("pe", [lambda e, kc=kc, wi=wi, pt=pt: e.matmul(
                    pt[:, 0:17], WU[wi][:, kc, :], scT[:, kc, :], start=(kc == 0), stop=(kc == KC - 1))
                    for kc in range(KC)], reads=[bWU[wi], bscT], writes=[pb])
                P.op("dve", lambda e, u=u, pt=pt: e.tensor_scalar(
                    out=adaT[:, u, :], in0=pt[:, 0:17], scalar1=badaT[:, u:u + 1], scalar2=None, op0=ALU.add),
                    reads=[pb, bC] + ([bada[gi]] if u % 16 else []), writes=[bada[gi]])
            if gi * 16 in sc_off:
                s_ = sc_off.index(gi * 16)
                P.op("dve", lambda e: e.scalar_tensor_tensor(
                    out=Amod[:, s_, :, :], in0=adaT[:, sc_off[s_]:sc_off[s_] + 16, :], scalar=1.0,
                    in1=normT[:, s_, :].unsqueeze(2).to_broadcast([128, KC, 17]), op0=ALU.add, op1=ALU.mult),
                    reads=[bada[gi], bC], writes=[bAmod[s_]])
            if gi * 16 in (g_off[0], g_off[2]):
                P.op("dve", lambda e: e.tensor_scalar(
                    out=adaT[:, gi * 16:gi * 16 + 16, :], in0=adaT[:, gi * 16:gi * 16 + 16, :],
                    scalar1=0.5, scalar2=None, op0=ALU.mult), reads=[bada[gi]], writes=[bada[gi]])

        ada_pending = []
        for gi in (1, 0, 2, 3, 4, 5, 6, 7, 8, 9, 10):
            ada_group(gi)

        def samp_bc(ap2d, p):
            return ap2d[:, 1 + p * SPP:1 + (p + 1) * SPP].unsqueeze(1).to_broadcast([128, 4, SPP])

        def ts(ap):
            return ap.rearrange("p (t s) -> p t s", t=4)

        def norm_mod(p, site, final=False):
            ada_group(sc_off[site] // 16)
            ada_group(sh_off[site] // 16)
            phase_barrier()
            for s, (c0, n) in enumerate(segs):
                pt, pb = bank()
                for kc in range(KC):
                    j = kc % 2
                    P.op("act", lambda e, kc=kc, j=j, c0=c0, n=n: e.activation(
                        out=sq[j].bitcast(BF16)[:, 0:n], in_=xT[:, kc, c0:c0 + n], func=AF.Square),
                        reads=[bx[kc][s]], writes=[bsq[j]])
                    P.op("pe", lambda e, kc=kc, j=j, n=n, pt=pt: e.matmul(
                        pt[:, 0:n], onesb, sq[j].bitcast(BF16)[:, 0:n], start=(kc == 0), stop=(kc == KC - 1)),
                        reads=[bsq[j], bonesb], writes=[pb])
                P.op("act", lambda e, c0=c0, n=n, pt=pt: e.activation(
                    out=rstd[:, c0:c0 + n], in_=pt[:, 0:n], func=AF.Sqrt, bias=cst[:, 0:1], scale=1.0 / D),
                    reads=[pb, bC], writes=[brstd[s]])
                P.op("dve", lambda e, c0=c0, n=n: e.reciprocal(out=rstd[:, c0:c0 + n], in_=rstd[:, c0:c0 + n]),
                     reads=[brstd[s]], writes=[brstd[s]])
                npm = min(c0 + n, TPP) - c0
                for kc in range(KC):
                    j = kc % 2
                    P.op("dve", lambda e, kc=kc, j=j, c0=c0, n=n: e.tensor_tensor(
                        out=tmpf[j][:, 0:n], in0=xT[:, kc, c0:c0 + n], in1=rstd[:, c0:c0 + n], op=ALU.mult),
                        reads=[bx[kc][s], brstd[s]], writes=[btmpf[j]])
                    if not final:
                        wr = [bh[kc][s]]
                        o_p = lambda a, b, kc=kc: hT[:, kc, a:b]
                    else:
                        wr = [bx[kc][s]]
                        o_p = lambda a, b, kc=kc: xT[:, kc, a:b]
                    if npm > 0:
                        P.op("act", lambda e, kc=kc, j=j, c0=c0, npm=npm, o_p=o_p: e.activation(
                            out=o_p(c0, c0 + npm), in_=tmpf[j][:, 0:npm], func=AF.Identity,
                            scale=Amod[:, site, kc, 0:1], bias=adaT[:, sh_off[site] + kc, 0:1]),
                            reads=[btmpf[j], bAmod[site], bada[sh_off[site] // 16]], writes=wr)
                    if npm < n:
                        P.op("dve", lambda e, kc=kc, j=j, npm=npm, n=n: e.tensor_tensor(
                            out=ts(tmpf[j][:, npm:n]), in0=ts(tmpf[j][:, npm:n]),
                            in1=samp_bc(Amod[:, site, kc, :], p), op=ALU.mult),
                            reads=[btmpf[j], bAmod[site]], writes=[btmpf[j]])
                        P.op("dve", lambda e, kc=kc, j=j, npm=npm, n=n, c0=c0, o_p=o_p: e.tensor_tensor(
                            out=ts(o_p(c0 + npm, c0 + n)), in0=ts(tmpf[j][:, npm:n]),
                            in1=samp_bc(adaT[:, sh_off[site] + kc, :], p), op=ALU.add),
                            reads=[btmpf[j], bada[sh_off[site] // 16]], writes=wr)
            if final:
                P.dma("sp", yout[p], xT, reads=[b for r in bx for b in r], stream="yout")

        def gate_evac(pt, pb, p, dc, s, goff):
            c0, n = segs[s]
            npm = min(c0 + n, TPP) - c0
            if npm > 0:
                P.op("dve", lambda e: e.scalar_tensor_tensor(
                    out=xT[:, dc, c0:c0 + npm], in0=pt[:, 0:npm], scalar=adaT[:, goff + dc, 0:1],
                    in1=xT[:, dc, c0:c0 + npm], op0=ALU.mult, op1=ALU.add),
                    reads=[pb, bada[goff // 16], bx[dc][s]], writes=[bx[dc][s]])
            if npm < n:
                j = dc % 2
                P.op("dve", lambda e: e.tensor_tensor(
                    out=ts(tmpf[j][:, npm:n]), in0=ts(pt[:, npm:n]),
                    in1=samp_bc(adaT[:, goff + dc, :], p), op=ALU.mult),
                    reads=[pb, bada[goff // 16]], writes=[btmpf[j]])
                P.op("dve", lambda e: e.tensor_tensor(
                    out=xT[:, dc, c0 + npm:c0 + n], in0=xT[:, dc, c0 + npm:c0 + n], in1=tmpf[j][:, npm:n],
                    op=ALU.add), reads=[btmpf[j], bx[dc][s]], writes=[bx[dc][s]])

        def ffn(p, which):
            phase_barrier()
            goff = g_off[0] if which == 0 else g_off[2]
            NG = NFC // 4

            def gate_up(g):
                a = g % 2
                for f4 in range(4):
                    f = 4 * g + f4
                    wgi = load_wu(wg[which][f])
                    wui = load_wu(wu[which][f])
                    for s, (c0, n) in enumerate(segs):
                        ptg, pbg = bank()
                        ptu, pbu = bank()
                        P.op("pe", [lambda e, kc=kc, ptg=ptg, wgi=wgi, c0=c0, n=n: e.matmul(
                            ptg[:, 0:n], WU[wgi][:, kc, :], hT[:, kc, c0:c0 + n],
                            start=(kc == 0), stop=(kc == KC - 1)) for kc in range(KC)],
                            reads=[bWU[wgi]] + [bh[kc][s] for kc in range(KC)], writes=[pbg])
                        P.op("pe", [lambda e, kc=kc, ptu=ptu, wui=wui, c0=c0, n=n: e.matmul(
                            ptu[:, 0:n], WU[wui][:, kc, :], hT[:, kc, c0:c0 + n],
                            start=(kc == 0), stop=(kc == KC - 1)) for kc in range(KC)],
                            reads=[bWU[wui]] + [bh[kc][s] for kc in range(KC)], writes=[pbu])
                        j = (f4 * NSEG + s) % 2
                        P.op("act", lambda e, j=j, n=n, ptg=ptg: e.activation(
                            out=sg[j][:, 0:n], in_=ptg[:, 0:n], func=AF.Silu), reads=[pbg], writes=[bsg[j]])
                        P.op("dve", lambda e, j=j, n=n, ptu=ptu, a=a, f4=f4, c0=c0: e.tensor_tensor(
                            out=act[a][:, f4, c0:c0 + n], in0=ptu[:, 0:n], in1=sg[j][:, 0:n], op=ALU.mult),
                            reads=[pbu, bsg[j]], writes=[bact[a][f4]])

            def down_mm(g):
                a = g % 2
                wis = []
                for f4 in range(4):
                    i = wd_i[0] % NWD
                    wd_i[0] += 1
                    P.dma("pool", WD[i], wd[which][4 * g + f4], writes=[bWD[i]], stream=f"WD{i}")
                    wis.append(i)
                for dc in range(KC):
                    for s, (c0, n) in enumerate(segs):
                        pt, pb = bank()
                        P.op("pe", [lambda e, f4=f4, pt=pt, dc=dc, c0=c0, n=n, wis=wis, a=a: e.matmul(
                            pt[:, 0:n], WD[wis[f4]][:, dc * 128:(dc + 1) * 128], act[a][:, f4, c0:c0 + n],
                            start=(f4 == 0), stop=(f4 == 3)) for f4 in range(4)],
                            reads=[bWD[i] for i in wis] + bact[a], writes=[pb])
                        gate_evac(pt, pb, p, dc, s, goff)

            ada_group(goff // 16)
            gate_up(0)
            for g in range(NG):
                if g + 1 < NG:
                    gate_up(g + 1)
                if ada_pending:
                    ada_group(ada_pending.pop(0))
                down_mm(g)

        def tile_list():
            tiles = [("P", 128, i * 128) for i in range(NPT)]
            tiles.append(("S", TS, TPP))
            return tiles

        def tm_scalars(p):
            for ti, (kind, n, c0) in enumerate(tile_list()):
                ss = segs_of(c0, n)
                pt, pb = bank()
                P.op("pe", [lambda e, kc=kc, pt=pt, c0=c0, n=n: e.matmul(
                    pt[0:n, 0:16], hT[:, kc, c0:c0 + n], wba[:, kc, :], start=(kc == 0), stop=(kc == KC - 1))
                    for kc in range(KC)],
                    reads=[bwba] + [bh[kc][s] for kc in range(KC) for s in ss], writes=[pb])
                tm = TM[0:n, ti, :]
                b = bTM[ti]
                P.op("act", lambda e, pt=pt, n=n, tm=tm: e.activation(out=tm[:, 0:8], in_=pt[0:n, 0:8], func=AF.Sigmoid),
                     reads=[pb], writes=[b])
                P.op("dve", lambda e, pt=pt, n=n, tm=tm: e.tensor_tensor(
                    out=tm[:, 8:16], in0=pt[0:n, 8:16], in1=smv[0:n, 0:8], op=ALU.add),
                    reads=[pb, bC, b], writes=[b])
                P.op("act", lambda e, tm=tm: e.activation(out=tm[:, 8:16], in_=tm[:, 8:16], func=AF.Exp),
                     reads=[b], writes=[b])
                P.op("act", lambda e, tm=tm, n=n: e.activation(out=tm[:, 8:16], in_=tm[:, 8:16], func=AF.Ln,
                                                               bias=cst[0:n, 2:3], scale=1.0),
                     reads=[b, bC], writes=[b])
                P.op("dve", lambda e, tm=tm, n=n: e.tensor_tensor(out=tm[:, 8:16], in0=tm[:, 8:16], in1=negA[0:n, :],
                                                                  op=ALU.mult), reads=[b, bnegA], writes=[b])
                pt2, pb2 = bank()
                mk = masks[kind]
                P.op("pe", lambda e, pt2=pt2, n=n, tm=tm, mk=mk: e.matmul(
                    pt2[0:n, 0:8], mk[0:n, 2, 0:n], tm[:, 8:16], start=True, stop=True), reads=[b, bC], writes=[pb2])
                pt3, pb3 = bank()
                P.op("pe", lambda e, pt3=pt3, n=n, tm=tm, mk=mk: e.matmul(
                    pt3[0:n, 0:8], mk[0:n, 3, 0:n], tm[:, 8:16], start=True, stop=True), reads=[b, bC], writes=[pb3])
                P.op("dve", lambda e, pt2=pt2, n=n, tm=tm: e.tensor_copy(out=tm[:, 16:24], in_=pt2[0:n, 0:8]),
                     reads=[pb2, b], writes=[b])
                P.op("dve", lambda e, tm=tm: e.tensor_scalar(out=tm[:, 56:64], in0=tm[:, 16:24], scalar1=-1.0,
                                                             scalar2=None, op0=ALU.mult), reads=[b], writes=[b])
                P.op("act", lambda e, tm=tm: e.activation(out=tm[:, 32:40], in_=tm[:, 16:24], func=AF.Exp),
                     reads=[b], writes=[b])
                P.op("dve", lambda e, tm=tm: e.tensor_tensor(out=tm[:, 40:48], in0=tm[:, 32:40], in1=tm[:, 0:8],
                                                             op=ALU.mult), reads=[b], writes=[b])
                P.op("dve", lambda e, pt3=pt3, n=n, tm=tm: e.tensor_tensor(
                    out=tm[:, 48:56], in0=pt3[0:n, 0:8], in1=tm[:, 16:24], op=ALU.subtract),
                    reads=[pb3, b], writes=[b])
                P.op("act", lambda e, tm=tm: e.activation(out=tm[:, 48:56], in_=tm[:, 48:56], func=AF.Exp),
                     reads=[b], writes=[b])

        def fm_proj(wi, s):
            c0, n = segs[s]
            pt, pb = bank()
            P.op("pe", [lambda e, kc=kc, pt=pt, c0=c0, n=n: e.matmul(
                pt[:, 0:n], WU[wi][:, kc, :], hT[:, kc, c0:c0 + n],
                start=(kc == 0), stop=(kc == KC - 1)) for kc in range(KC)],
                reads=[bWU[wi]] + [bh[kc][s] for kc in range(KC)], writes=[pb])
            return pt, pb

        def cp(eng, out, in_, reads, writes):
            if eng == "act":
                P.op("act", lambda e: e.copy(out=out, in_=in_), reads=reads, writes=writes)
            else:
                P.op("dve", lambda e: e.tensor_copy(out=out, in_=in_), reads=reads, writes=writes)

        def mm1(out_ap, lhsT, rhs, reads, pb, **kw):
            P.op("pe", lambda e: e.matmul(out_ap, lhsT, rhs, start=True, stop=True, **kw), reads=reads, writes=[pb])

        def o_finish_gdn(hd, hl, c0, n, pt, pb):
            hl = hd
            ss = segs_of(c0, n)
            cp("dve", tl["osb"][:, 0:n], pt[:, 0:n], [pb], [btl["osb"]])
            P.op("act", lambda e: e.activation(out=tl["osq"].bitcast(BF16)[:, 0:n], in_=pt[:, 0:n], func=AF.Square),
                 reads=[pb], writes=[btl["osq"]])
            pt2, pb2 = bank()
            mm1(pt2[:, 0:n], onesb, tl["osq"].bitcast(BF16)[:, 0:n], [btl["osq"], bonesb], pb2)
            P.op("act", lambda e: e.activation(out=tl["rn"][:, 0:n], in_=pt2[:, 0:n], func=AF.Sqrt,
                                               bias=cst[:, 0:1], scale=1.0 / 128), reads=[pb2, bC],
                 writes=[btl["rn"]])
            P.op("dve", lambda e: e.reciprocal(out=tl["rn"][:, 0:n], in_=tl["rn"][:, 0:n]),
                 reads=[btl["rn"]], writes=[btl["rn"]])
            P.op("dve", lambda e: e.tensor_tensor(out=tl["on"][:, 0:n], in0=tl["osb"][:, 0:n], in1=tl["rn"][:, 0:n],
                                                  op=ALU.mult), reads=[btl["osb"], btl["rn"]], writes=[btl["on"]])
            P.op("dve", lambda e: e.scalar_tensor_tensor(
                out=oTall[:, hl, c0:c0 + n], in0=tl["on"][:, 0:n], scalar=smv[:, 16:17], in1=szT[:, c0:c0 + n],
                op0=ALU.mult, op1=ALU.mult), reads=[btl["on"], bC] + [bszT[s] for s in ss],
                writes=[boT[hl][s] for s in ss])

        def o_finish_ret(hd, hl, c0, n, pt, pb):
            hl = 8 + hd
            ss = segs_of(c0, n)
            cp("dve", tl["osb"][:, 0:n], pt[:, 0:n], [pb], [btl["osb"]])
            P.op("act", lambda e: e.activation(out=tl["osq"].bitcast(BF16)[:, 0:n], in_=pt[:, 0:n], func=AF.Square),
                 reads=[pb], writes=[btl["osq"]])
            pt2, pb2 = bank()
            mm1(pt2[:, 0:n], ones, tl["osb"][:, 0:n], [btl["osb"], bC], pb2)
            mm1(pt2[:, 128:128 + n], onesb, tl["osq"].bitcast(BF16)[:, 0:n], [btl["osq"], bonesb], pb2)
            P.op("act", lambda e: e.activation(out=tl["mu"][:, 0:n], in_=pt2[:, 0:n], func=AF.Identity,
                                               scale=1.0 / 128), reads=[pb2], writes=[btl["mu"]])
            P.op("dve", lambda e: e.tensor_tensor(out=tl["msq"][:, 0:n], in0=tl["mu"][:, 0:n], in1=tl["mu"][:, 0:n],
                                                  op=ALU.mult), reads=[btl["mu"]], writes=[btl["msq"]])
            P.op("dve", lambda e: e.scalar_tensor_tensor(
                out=tl["var"][:, 0:n], in0=pt2[:, 128:128 + n], scalar=1.0 / 128, in1=tl["msq"][:, 0:n],
                op0=ALU.mult, op1=ALU.subtract), reads=[pb2, btl["msq"]], writes=[btl["var"]])
            P.op("dve", lambda e: e.tensor_scalar(out=tl["var"][:, 0:n], in0=tl["var"][:, 0:n], scalar1=0.0,
                                                  scalar2=None, op0=ALU.max), reads=[btl["var"]], writes=[btl["var"]])
            P.op("act", lambda e: e.activation(out=tl["rn"][:, 0:n], in_=tl["var"][:, 0:n], func=AF.Sqrt,
                                               bias=cst[:, 0:1], scale=1.0), reads=[btl["var"], bC],
                 writes=[btl["rn"]])
            P.op("dve", lambda e: e.reciprocal(out=tl["rn"][:, 0:n], in_=tl["rn"][:, 0:n]),
                 reads=[btl["rn"]], writes=[btl["rn"]])
            P.op("dve", lambda e: e.tensor_tensor(out=tl["cen"][:, 0:n], in0=tl["osb"][:, 0:n], in1=tl["mu"][:, 0:n],
                                                  op=ALU.subtract), reads=[btl["osb"], btl["mu"]], writes=[btl["cen"]])
            P.op("dve", lambda e: e.tensor_tensor(out=tl["on"][:, 0:n], in0=tl["cen"][:, 0:n], in1=tl["rn"][:, 0:n],
                                                  op=ALU.mult), reads=[btl["cen"], btl["rn"]], writes=[btl["on"]])
            P.op("act", lambda e: e.activation(out=tl["on"][:, 0:n], in_=tl["on"][:, 0:n], func=AF.Identity,
                                               scale=smv[:, 17 + hd:18 + hd], bias=gnbT[:, hd:hd + 1]),
                 reads=[btl["on"], bC], writes=[btl["on"]])
            P.op("dve", lambda e: e.tensor_tensor(out=oTall[:, hl, c0:c0 + n], in0=tl["on"][:, 0:n],
                                                  in1=szT[:, c0:c0 + n], op=ALU.mult),
                 reads=[btl["on"]] + [bszT[s] for s in ss], writes=[boT[hl][s] for s in ss])

        def recur_prompt(S, bS, qdec, bqdec, QKT, bQKT, kdec, bkdec, vsrc, bvsrc, glast_ap_fn, greads,
                         wT=None, bwT=None, u=None, bu=None):
            Sb, bSb = CL[0].Sb, CL[0].bSb
            vn = tl["vnew"].bitcast(BF16)
            pto, pbo = bank_o()
            for c in range(2):
                r0 = 64 * c
                if wT is not None:
                    ptw, pbw = bank()
                    mm1(ptw[:, 0:128], wT[:, 0:128], Sb, [bwT, bSb], pbw)
                    P.op("dve", lambda e, r0=r0, ptw=ptw: e.tensor_tensor(
                        out=vn[r0:r0 + 64, 0:128], in0=u[r0:r0 + 64, :], in1=ptw[r0:r0 + 64, 0:128],
                        op=ALU.subtract), reads=[pbw, bu], writes=[btl["vnew"]])
                    v_, bv_ = vn, btl["vnew"]
                else:
                    v_, bv_ = vsrc, bvsrc
                P.op("pe", [lambda e, r0=r0: e.matmul(pto[:, r0:r0 + 64], Sb, qdec[:, r0:r0 + 64],
                                                      start=True, stop=False, skip_group_check=True),
                            lambda e, r0=r0, v_=v_: e.matmul(pto[:, r0:r0 + 64], v_[r0:r0 + 64, 0:128],
                                                             QKT[r0:r0 + 64, r0:r0 + 64],
                                                             start=False, stop=True, skip_group_check=True)],
                     reads=[bSb, bqdec, bv_, bQKT], writes=[pbo])
                pts, pbs = bank()
                mm1(pts[:, 0:128], kdec[r0:r0 + 64, 0:128], v_[r0:r0 + 64, 0:128], [bkdec, bv_], pbs)
                P.op("dve", lambda e, r0=r0, pts=pts: e.scalar_tensor_tensor(
                    out=Sb, in0=S, scalar=glast_ap_fn(r0 + 63), in1=pts[:, 0:128], op0=ALU.mult, op1=ALU.add),
                    reads=[pbs, bS] + greads, writes=[bSb])
                P.op("dve", lambda e, r0=r0, pts=pts: e.scalar_tensor_tensor(
                    out=S, in0=S, scalar=glast_ap_fn(r0 + 63), in1=pts[:, 0:128], op0=ALU.mult, op1=ALU.add),
                    reads=[pbs, bS] + greads, writes=[bS])
            return pto, pbo

        def recur_sample(p, hd, sdram_in, sdram_out, qdec, bqdec, QKT, bQKT, kdec, bkdec, vsrc, bvsrc,
                         glast3, greads, wT=None, bwT=None, u=None, bu=None):
            n = TS
            Sallb, bSallb = CL[0].Sallb, CL[0].bSallb
            vn = tl["vnew"].bitcast(BF16)
            P.cur_atomic = ("samp", CL[0].id, hd, id(sdram_in))
            P.dma("sp", Sall, sdram_in[p, hd], writes=[bSall], stream="Sall")
            P.op("act", lambda e: e.copy(out=Sallb, in_=Sall), reads=[bSall], writes=bSallb)
            nh = (SPP * 128) // 512
            Sflat = Sall.rearrange("p s d -> p (s d)")
            Sbflat = Sallb.rearrange("p s d -> p (s d)")
            if wT is not None:
                for hh in range(nh):
                    ptw, pbw = bank()
                    mm1(ptw[0:n, :], wT[:, 0:n], Sbflat[:, hh * 512:(hh + 1) * 512], [bwT] + bSallb, pbw)
                    P.op("dve", lambda e, ptw=ptw, hh=hh: e.tensor_tensor(
                        out=wsel[0:n, hh * 4:(hh + 1) * 4, :], in0=ptw[0:n, :].rearrange("p (s d) -> p s d", s=4),
                        in1=onehot[0:n, hh * 4:(hh + 1) * 4].unsqueeze(2).to_broadcast([n, 4, 128]), op=ALU.mult),
                        reads=[pbw, bC] + ([bwsel] if hh else []), writes=[bwsel])
                P.op("dve", lambda e: e.tensor_reduce(
                    out=tl["on"][0:n, :], in_=wsel[0:n, :, :].rearrange("p s d -> p d s"), axis=AX.X, op=ALU.add),
                    reads=[bwsel], writes=[btl["on"]])
                P.op("dve", lambda e: e.tensor_tensor(out=vn[0:n, 0:128], in0=u[0:n, :], in1=tl["on"][0:n, :],
                                                      op=ALU.subtract), reads=[btl["on"], bu], writes=[btl["vnew"]])
                v_, bv_ = vn, btl["vnew"]
            else:
                v_, bv_ = vsrc, bvsrc
            pto, pbo = bank_o()
            fns = []
            for s in range(SPP):
                fns.append(lambda e, s=s: e.matmul(
                    ts(pto[:, 0:n])[:, :, s], Sallb[:, s, :], ts(qdec[:, 0:n])[:, :, s],
                    start=(s == 0), stop=False, skip_group_check=True))
            fns.append(lambda e: e.matmul(pto[:, 0:n], v_[0:n, 0:128], QKT[0:n, 0:n], start=False, stop=True,
                                          skip_group_check=True))
            P.op("pe", fns, reads=bSallb + [bqdec, bv_, bQKT], writes=[pbo])
            vxb = vexp.rearrange("p s d -> p (s d)").bitcast(BF16)
            P.op("dve", lambda e: e.tensor_tensor(
                out=vxb[0:n, 0:SPP * 128].rearrange("p (s d) -> p s d", s=SPP),
                in0=v_[0:n, 0:128].unsqueeze(1).to_broadcast([n, SPP, 128]),
                in1=onehot[0:n, :].unsqueeze(2).to_broadcast([n, SPP, 128]), op=ALU.mult),
                reads=[bv_, bC], writes=[bvexp])
            P.op("dve", lambda e: e.tensor_tensor(out=Sall, in0=Sall, in1=glast3, op=ALU.mult),
                 reads=[bSall] + greads, writes=[bSall])
            for hh in range(nh):
                pts, pbs = bank()
                mm1(pts[:, :], kdec[0:n, 0:128], vxb[0:n, hh * 512:(hh + 1) * 512], [bkdec, bvexp], pbs)
                P.op("dve", lambda e, pts=pts, hh=hh: e.tensor_tensor(
                    out=Sflat[:, hh * 512:(hh + 1) * 512], in0=Sflat[:, hh * 512:(hh + 1) * 512], in1=pts[:, :],
                    op=ALU.add), reads=[pbs, bSall], writes=[bSall])
            P.dma("sp", sdram_out[p, hd], Sall, reads=[bSall], stream="Sall_o")
            P.cur_atomic = None
            return pto, pbo

        swi = [0]

        def state_in(p, idx):
            L = CL[0]
            if p == 0:
                P.op("dve", lambda e: e.memset(L.Sw, 0.0), writes=[L.bSw])
            else:
                P.dma("sp", L.Sw, sscr[idx], reads=[bsscr[idx]], writes=[L.bSw], stream="Sw")
            P.op("act", lambda e: e.copy(out=L.Sb, in_=L.Sw), reads=[L.bSw], writes=[L.bSb])
            return L.Sw, L.bSw, L.id

        def state_out(p, idx, i, out_ap):
            L = CL[0]
            if p == NPASS - 1:
                P.dma("sp", out_ap, L.Sw, reads=[L.bSw], stream="Sw_o")
            else:
                P.dma("sp", sscr[idx], L.Sw, reads=[L.bSw], writes=[bsscr[idx]], stream="Sw_o")

        def gdn_head(p, hd):
            hl = hd % 4
            wq = load_wu(w_in[hd * 4 + 0])
            wk = load_wu(w_in[hd * 4 + 1])
            wv = load_wu(w_in[hd * 4 + 2])
            wqkv = [wq, wk, wv]
            S, bS, si = state_in(p, hd)
            r0c = TPP - 3
            nr = 3 + TS
            ptc, pbc = bank()
            for comp in range(3):
                P.op("pe", [lambda e, kc=kc, comp=comp: e.matmul(
                    ptc[0:nr, comp * 128:(comp + 1) * 128], hT[:, kc, r0c:T], WU[wqkv[comp]][:, kc, :],
                    start=(kc == 0), stop=(kc == KC - 1), skip_group_check=True) for kc in range(KC)],
                    reads=[bWU[wqkv[comp]]] + [bh[kc][NSEG - 1] for kc in range(KC)], writes=[pbc])
            cp("act", crow[0:nr, :], ptc[0:nr, 0:384], [pbc], [bcrow])
            if p == NPASS - 1:
                P.dma("sp", conv_p.rearrange("i (c h x) -> i c h x", c=3, h=8)[:, :, hd, :],
                      crow[0:3, :].rearrange("i (c x) -> i c x", c=3), reads=[bcrow], stream="crow_o")
            for j in range(3):
                P.dma("sp", conv_s[p, j].rearrange("s (c h x) -> s c h x", c=3, h=8)[:, :, hd, :],
                      crow[3 + (1 + j) * SPP:3 + (2 + j) * SPP, :].rearrange("s (c x) -> s c x", c=3),
                      reads=[bcrow], stream="crow_o")
            for comp in range(3):
                wi = wqkv[comp]
                ch = comp * 8 + hd
                P.op("dve", lambda e, ch=ch: e.tensor_copy(out=cb[:, 0:3], in_=ctail[:, ch, :]),
                     reads=[bctail[ch]], writes=[bcb])
                P.dma("sp", cbs[:, 0:3 * SPP], sconv[p, ch].rearrange("c i s -> c (i s)"), writes=[bcbs],
                      stream="cbs")
                for s, (c0, n) in enumerate(segs):
                    pt, pb = fm_proj(wi, s)
                    npm = min(c0 + n, TPP) - c0
                    if npm > 0:
                        cp("act", cb[:, 3 + c0:3 + c0 + npm], pt[:, 0:npm], [pb, bcb], [bcb])
                    if npm < n:
                        cp("dve", cbs[:, 3 * SPP:7 * SPP], pt[:, npm:n], [pb, bcbs], [bcbs])
                P.op("dve", lambda e, ch=ch: e.tensor_copy(out=ctail[:, ch, :], in_=cb[:, TPP:TPP + 3]),
                     reads=[bcb], writes=[bctail[ch]])
                for (dst, src, width, bsrc, step) in ((cvt[:, 0:TPP], cb, TPP, bcb, 1),
                                                      (cvt[:, TPP:T], cbs, TS, bcbs, SPP)):
                    P.op("dve", lambda e, dst=dst, src=src, width=width, ch=ch: e.tensor_scalar(
                        out=dst, in0=src[:, 0:width], scalar1=cwT[:, ch, 0:1], scalar2=None, op0=ALU.mult),
                        reads=[bsrc, bC, bcvt], writes=[bcvt])
                    for i in range(1, 4):
                        P.op("dve", lambda e, dst=dst, src=src, width=width, ch=ch, i=i, step=step:
                             e.scalar_tensor_tensor(out=dst, in0=src[:, i * step:i * step + width],
                                                    scalar=cwT[:, ch, i:i + 1], in1=dst, op0=ALU.mult, op1=ALU.add),
                             reads=[bsrc, bC, bcvt], writes=[bcvt])
                P.op("act", lambda e, comp=comp: e.activation(out=FM[comp], in_=cvt, func=AF.Silu),
                     reads=[bcvt], writes=bFM[comp])
            wz = load_wu(w_in[hd * 4 + 3])
            for s, (c0, n) in enumerate(segs):
                pt, pb = fm_proj(wz, s)
                P.op("act", lambda e, pt=pt, c0=c0, n=n: e.activation(out=szT[:, c0:c0 + n], in_=pt[:, 0:n],
                                                                      func=AF.Silu), reads=[pb], writes=[bszT[s]])
            for comp in range(2):
                for s, (c0, n) in enumerate(segs):
                    j = s % 2
                    P.op("act", lambda e, comp=comp, j=j, c0=c0, n=n: e.activation(
                        out=sq[j].bitcast(BF16)[:, 0:n], in_=FM[comp][:, c0:c0 + n], func=AF.Square),
                        reads=[bFM[comp][s]], writes=[bsq[j]])
                    pt, pb = bank()
                    mm1(pt[:, 0:n], onesb, sq[j].bitcast(BF16)[:, 0:n], [bsq[j], bonesb], pb)
                    P.op("act", lambda e, comp=comp, j=j, n=n, pt=pt: e.activation(
                        out=tmpf[j][:, 0:n], in_=pt[:, 0:n], func=AF.Sqrt,
                        bias=(cst[:, 1:2] if comp == 0 else cst[:, 0:1]), scale=(128.0 if comp == 0 else 1.0)),
                        reads=[pb, bC], writes=[btmpf[j]])
                    P.op("dve", lambda e, j=j, n=n: e.reciprocal(out=tmpf[j][:, 0:n], in_=tmpf[j][:, 0:n]),
                         reads=[btmpf[j]], writes=[btmpf[j]])
                    P.op("dve", lambda e, comp=comp, j=j, c0=c0, n=n: e.tensor_tensor(
                        out=FM[comp][:, c0:c0 + n], in0=FM[comp][:, c0:c0 + n], in1=tmpf[j][:, 0:n], op=ALU.mult),
                        reads=[btmpf[j], bFM[comp][s]], writes=[bFM[comp][s]])
            qT, kT, vT = FM
            for ti, (kind, n, c0) in enumerate(tile_list()):
                ss = segs_of(c0, n)
                rq = [bFM[0][s] for s in ss]
                rk = [bFM[1][s] for s in ss]
                rv = [bFM[2][s] for s in ss]
                mk = masks[kind]
                tm = TM[0:n, ti, :]
                btm = bTM[ti]
                bv = lambda nm: tl[nm].bitcast(BF16)
                ptr, pbr = bank()
                mm1(ptr[:, 0:n], tm[:, 8 + hd:9 + hd].to_broadcast([n, 128]), mk[0:n, 2, 0:n], [btm, bC], pbr)
                P.op("dve", lambda e, ptr=ptr, n=n, mk=mk: e.tensor_tensor(
                    out=tl["a1"][0:n, 0:n], in0=ptr[0:n, 0:n], in1=mk[0:n, 0, 0:n], op=ALU.add),
                    reads=[pbr, bC], writes=[btl["a1"]])
                P.op("act", lambda e, n=n, tm=tm: e.activation(out=tl["DTi"][0:n, 0:n], in_=tl["a1"][0:n, 0:n],
                                                               func=AF.Exp, bias=tm[:, 56 + hd:57 + hd], scale=1.0),
                     reads=[btl["a1"], btm], writes=[btl["DTi"]])
                P.op("dve", lambda e, ptr=ptr, n=n, mk=mk: e.scalar_tensor_tensor(
                    out=tl["a1"][0:n, 0:n], in0=ptr[0:n, 0:n], scalar=-1.0, in1=mk[0:n, 1, 0:n],
                    op0=ALU.mult, op1=ALU.add), reads=[pbr, bC, btl["a1"]], writes=[btl["a1"]])
                P.op("act", lambda e, n=n, tm=tm: e.activation(out=tl["Dst"][0:n, 0:n], in_=tl["a1"][0:n, 0:n],
                                                               func=AF.Exp, bias=tm[:, 16 + hd:17 + hd], scale=1.0),
                     reads=[btl["a1"], btm], writes=[btl["Dst"]])
                P.op("act", lambda e, ptr=ptr, n=n: e.activation(out=tl["Egc"][:, 0:n], in_=ptr[:, 0:n], func=AF.Exp),
                     reads=[pbr], writes=[btl["Egc"]])
                ptg, pbg = bank()
                mm1(ptg[0:n, 0:n], kT[:, c0:c0 + n], kT[:, c0:c0 + n], rk, pbg)
                mm1(ptg[0:n, 128:128 + n], kT[:, c0:c0 + n], qT[:, c0:c0 + n], rk + rq, pbg)
                P.op("dve", lambda e, ptg=ptg, n=n, tm=tm: e.scalar_tensor_tensor(
                    out=tl["L"][0:n, 0:n], in0=ptg[0:n, 0:n], scalar=tm[:, hd:hd + 1], in1=tl["Dst"][0:n, 0:n],
                    op0=ALU.mult, op1=ALU.mult), reads=[pbg, btm, btl["Dst"]], writes=[btl["L"]])
                P.op("dve", lambda e, ptg=ptg, n=n: e.tensor_tensor(
                    out=bv("QKT")[0:n, 0:n], in0=ptg[0:n, 128:128 + n], in1=tl["DTi"][0:n, 0:n], op=ALU.mult),
                    reads=[pbg, btl["DTi"]], writes=[btl["QKT"]])
                ptt, pbt = bank()
                P.op("pe", lambda e, ptt=ptt, n=n: e.transpose(ptt[0:n, 0:n], tl["L"][0:n, 0:n], ident[0:n, 0:n]),
                     reads=[btl["L"], bC], writes=[pbt])
                cp("act", tl["LT"][0:n, 0:n], ptt[0:n, 0:n], [pbt], [btl["LT"]])
                P.op("dve", lambda e, n=n: e.tensor_tensor(out=tl["M"][0:n, 0:n], in0=ident[0:n, 0:n],
                                                           in1=tl["LT"][0:n, 0:n], op=ALU.subtract),
                     reads=[btl["LT"], bC], writes=[btl["M"]])
                nlev = 5 if kind == "P" else 1
                Pc, PTc = "L", "LT"
                for lv in range(nlev):
                    last = lv == nlev - 1
                    pa, pba = bank()
                    mm1(pa[0:n, 0:n], tl[PTc][0:n, 0:n], tl[Pc][0:n, 0:n], [btl[Pc], btl[PTc]], pba)
                    if not last:
                        mm1(pa[0:n, 128:128 + n], tl[Pc][0:n, 0:n], tl[PTc][0:n, 0:n], [btl[Pc], btl[PTc]], pba)
                    nP = "P" if Pc == "L" else "L"
                    nPT = "PT" if PTc == "LT" else "LT"
                    cp("act", tl[nP][0:n, 0:n], pa[0:n, 0:n], [pba], [btl[nP]])
                    if not last:
                        cp("dve", tl[nPT][0:n, 0:n], pa[0:n, 128:128 + n], [pba], [btl[nPT]])
                        PTc = nPT
                    Pc = nP
                    pm, pbm = bank()
                    mm1(pm[0:n, 0:n], tl[Pc][0:n, 0:n], tl["M"][0:n, 0:n], [btl[Pc], btl["M"]], pbm)
                    P.op("dve", lambda e, pm=pm, n=n: e.tensor_tensor(
                        out=tl["M"][0:n, 0:n], in0=tl["M"][0:n, 0:n], in1=pm[0:n, 0:n], op=ALU.add),
                        reads=[pbm, btl["M"]], writes=[btl["M"]])
                pk, pbk = bank()
                P.op("pe", lambda e, pk=pk, n=n, c0=c0: e.transpose(pk[0:n, 0:128], kT[:, c0:c0 + n], ident),
                     reads=rk + [bC], writes=[pbk])
                P.op("pe", lambda e, pk=pk, n=n, c0=c0: e.transpose(pk[0:n, 128:256], vT[:, c0:c0 + n], ident),
                     reads=rv + [bC], writes=[pbk])
                P.op("act", lambda e, pk=pk, n=n, tm=tm: e.activation(
                    out=tl["kbe"][0:n, :], in_=pk[0:n, 0:128], func=AF.Identity, scale=tm[:, 40 + hd:41 + hd]),
                    reads=[pbk, btm], writes=[btl["kbe"]])
                P.op("dve", lambda e, pk=pk, n=n, tm=tm: e.tensor_scalar(
                    out=bv("kdec")[0:n, 0:128], in0=pk[0:n, 0:128], scalar1=tm[:, 48 + hd:49 + hd], scalar2=None,
                    op0=ALU.mult), reads=[pbk, btm], writes=[btl["kdec"]])
                P.op("dve", lambda e, pk=pk, n=n, tm=tm: e.tensor_scalar(
                    out=tl["vb"][0:n, :], in0=pk[0:n, 128:256], scalar1=tm[:, hd:hd + 1], scalar2=None,
                    op0=ALU.mult), reads=[pbk, btm], writes=[btl["vb"]])
                pu, pbu = bank()
                mm1(pu[0:n, 0:128], tl["M"][0:n, 0:n], tl["vb"][0:n, :], [btl["M"], btl["vb"]], pbu)
                mm1(pu[:, 128:128 + n], tl["kbe"][0:n, :], tl["M"][0:n, 0:n], [btl["M"], btl["kbe"]], pbu)
                cp("act", tl["u"][0:n, :], pu[0:n, 0:128], [pbu], [btl["u"]])
                cp("dve", bv("wT")[:, 0:n], pu[:, 128:128 + n], [pbu], [btl["wT"]])
                P.op("dve", lambda e, n=n, c0=c0: e.tensor_tensor(out=bv("qdec")[:, 0:n], in0=qT[:, c0:c0 + n],
                                                                  in1=tl["Egc"][:, 0:n], op=ALU.mult),
                     reads=rq + [btl["Egc"]], writes=[btl["qdec"]])
                if kind == "P":
                    pto, pbo = recur_prompt(S, bS, bv("qdec"), btl["qdec"], bv("QKT"), btl["QKT"],
                                            bv("kdec"), btl["kdec"], None, None,
                                            lambda col: tl["Egc"][:, col:col + 1], [btl["Egc"]],
                                            wT=bv("wT"), bwT=btl["wT"], u=tl["u"], bu=btl["u"])
                else:
                    pto, pbo = recur_sample(p, hd, sgdn, gdn_s, bv("qdec"), btl["qdec"], bv("QKT"), btl["QKT"],
                                            bv("kdec"), btl["kdec"], None, None,
                                            tl["Egc"][:, 3 * SPP:4 * SPP].unsqueeze(2).to_broadcast([128, SPP, 128]),
                                            [btl["Egc"]], wT=bv("wT"), bwT=btl["wT"], u=tl["u"], bu=btl["u"])
                o_finish_gdn(hd, hl, c0, n, pto, pbo)
            state_out(p, hd, si, gdn_p[hd])

        def ret_head(p, hd):
            hl = hd % 4
            wq = load_wu(w_in[32 + hd * 4 + 0])
            wk = load_wu(w_in[32 + hd * 4 + 1])
            wv = load_wu(w_in[32 + hd * 4 + 2])
            S, bS, si = state_in(p, 8 + hd)
            P.dma("sp", rdm, crdm["P"][hd], writes=[brdm], stream="rdm")
            P.dma("sp", rqd, crqd["P"][hd], writes=[brqd], stream="rqd")
            for comp, wi in ((0, wq), (1, wk)):
                for s, (c0, n) in enumerate(segs):
                    pt, pb = fm_proj(wi, s)
                    j = s % 2
                    cp("act", tmpf[j][:, 0:n], pt[:, 0:n], [pb], [btmpf[j]])
                    pt2, pb2 = bank()
                    mm1(pt2[:, 0:n], rsw, tmpf[j][:, 0:n], [btmpf[j], bC], pb2)
                    P.op("dve", lambda e, j=j, c0=c0, n=n, pt2=pt2: e.tensor_tensor(
                        out=sq[j][:, 0:n], in0=pt2[:, 0:n], in1=rot[:, 1, c0:c0 + n], op=ALU.mult),
                        reads=[pb2, brot], writes=[bsq[j]])
                    P.op("dve", lambda e, j=j, c0=c0, n=n: e.tensor_tensor(
                        out=tmpf[j][:, 0:n], in0=tmpf[j][:, 0:n], in1=rot[:, 0, c0:c0 + n], op=ALU.mult),
                        reads=[btmpf[j], brot], writes=[btmpf[j]])
                    P.op("dve", lambda e, j=j, c0=c0, n=n, comp=comp: e.tensor_tensor(
                        out=FM[comp][:, c0:c0 + n], in0=tmpf[j][:, 0:n], in1=sq[j][:, 0:n], op=ALU.add),
                        reads=[btmpf[j], bsq[j]], writes=[bFM[comp][s]])
            wgt = load_wu(w_in[32 + hd * 4 + 3])
            for s, (c0, n) in enumerate(segs):
                pt, pb = fm_proj(wgt, s)
                P.op("act", lambda e, pt=pt, c0=c0, n=n: e.activation(out=szT[:, c0:c0 + n], in_=pt[:, 0:n],
                                                                      func=AF.Silu), reads=[pb], writes=[bszT[s]])
            qT, kT = FM[0], FM[1]
            for ti, (kind, n, c0) in enumerate(tile_list()):
                ss = segs_of(c0, n)
                rq = [bFM[0][s] for s in ss]
                rk = [bFM[1][s] for s in ss]
                bv = lambda nm: tl[nm].bitcast(BF16)
                if kind == "S":
                    P.dma("sp", rdm, crdm["S"][hd], writes=[brdm], stream="rdm")
                    P.dma("sp", rqd, crqd["S"][hd], writes=[brqd], stream="rqd")
                pv, pbv = bank()
                P.op("pe", [lambda e, kc=kc, pv=pv, c0=c0, n=n: e.matmul(
                    pv[0:n, 0:128], hT[:, kc, c0:c0 + n], WU[wv][:, kc, :], start=(kc == 0), stop=(kc == KC - 1))
                    for kc in range(KC)],
                    reads=[bWU[wv]] + [bh[kc][s] for kc in range(KC) for s in ss], writes=[pbv])
                cp("act", bv("vTM")[0:n, 0:128], pv[0:n, 0:128], [pbv], [btl["vTM"]])
                ptg, pbg = bank()
                mm1(ptg[0:n, 0:n], kT[:, c0:c0 + n], qT[:, c0:c0 + n], rk + rq, pbg)
                P.op("dve", lambda e, ptg=ptg, n=n: e.tensor_tensor(
                    out=bv("QKT")[0:n, 0:n], in0=ptg[0:n, 0:n], in1=rdm[0:n, 0:n], op=ALU.mult),
                    reads=[pbg, brdm], writes=[btl["QKT"]])
                pk, pbk = bank()
                P.op("pe", lambda e, pk=pk, n=n, c0=c0: e.transpose(pk[0:n, 0:128], kT[:, c0:c0 + n], ident),
                     reads=rk + [bC], writes=[pbk])
                P.op("dve", lambda e, pk=pk, n=n, kind=kind: e.tensor_scalar(
                    out=bv("kdec")[0:n, 0:128], in0=pk[0:n, 0:128], scalar1=rkd[kind][0:n, hd:hd + 1], scalar2=None,
                    op0=ALU.mult), reads=[pbk, bC], writes=[btl["kdec"]])
                P.op("dve", lambda e, n=n, c0=c0: e.tensor_tensor(out=bv("qdec")[:, 0:n], in0=qT[:, c0:c0 + n],
                                                                  in1=rqd[:, 0:n], op=ALU.mult),
                     reads=rq + [brqd], writes=[btl["qdec"]])
                if kind == "P":
                    pto, pbo = recur_prompt(S, bS, bv("qdec"), btl["qdec"], bv("QKT"), btl["QKT"],
                                            bv("kdec"), btl["kdec"], bv("vTM"), btl["vTM"],
                                            lambda col: rgc["P"][:, hd:hd + 1], [bC])
                else:
                    pto, pbo = recur_sample(p, hd, sret, ret_s, bv("qdec"), btl["qdec"], bv("QKT"), btl["QKT"],
                                            bv("kdec"), btl["kdec"], bv("vTM"), btl["vTM"],
                                            rgc["S"][:, hd:hd + 1].unsqueeze(2).to_broadcast([128, SPP, 128]), [bC])
                o_finish_ret(hd, hl, c0, n, pto, pbo)
            state_out(p, 8 + hd, si, ret_p[hd])

        def out_proj16(p):
            for dc in range(KC):
                wi = load_wu(w_out[dc])
                for s, (c0, n) in enumerate(segs):
                    pt, pb = bank()
                    P.op("pe", [lambda e, k=k, pt=pt, c0=c0, n=n, wi=wi: e.matmul(
                        pt[:, 0:n], WU[wi][:, k, :], oTall[:, k, c0:c0 + n],
                        start=(k == 0), stop=(k == 15)) for k in range(16)],
                        reads=[bWU[wi]] + [boT[k][s] for k in range(16)], writes=[pb])
                    gate_evac(pt, pb, p, dc, s, g_off[1])

        def run_pair(f0, f1):
            recs = []
            for li, f in enumerate((f0, f1)):
                CL[0] = lanes[li]
                REC[0] = True
                P.rec_target = []
                P.rec_suffix = f"_L{li}"
                f()
                recs.append(P.rec_target)
                P.rec_target = None
                REC[0] = False
            CL[0] = lanes[0]
            merged = []
            idx = [0, 0]
            while idx[0] < len(recs[0]) or idx[1] < len(recs[1]):
                for li in range(2):
                    r = recs[li]
                    if idx[li] >= len(r):
                        continue
                    a = r[idx[li]][5]
                    merged.append(r[idx[li]])
                    idx[li] += 1
                    if a is not None:
                        while idx[li] < len(r) and r[idx[li]][5] == a:
                            merged.append(r[idx[li]])
                            idx[li] += 1
            P.replay(merged)

        def mixer(p):
            phase_barrier()
            P.dma("sp", xscr, xT, reads=bx_all, writes=[bxscr], stream="xscr_o")
            tm_scalars(p)
            stage(3.1)
            phase_barrier()
            for pr in range(4):
                run_pair(lambda: gdn_head(p, 2 * pr), lambda: gdn_head(p, 2 * pr + 1))
                stage(3.8)
            phase_barrier()
            stage(3.9)
            P.dma("sp", rot, crot[p].rearrange("c p t -> p c t"), writes=[brot], stream="rot")
            for pr in range(4):
                run_pair(lambda: ret_head(p, 2 * pr), lambda: ret_head(p, 2 * pr + 1))
            phase_barrier()
            P.dma("sp", xT, xscr, reads=[bxscr, bPH], writes=bx_all, stream="xin")
            ada_group(g_off[1] // 16)
            out_proj16(p)

        try:
            stage(0)
            for p in range(NPASS):
                P.dma("sp", xT, xin[p], writes=[b for r in bx for b in r], stream="xin")
                norm_mod(p, 0)
                stage(1)
                ffn(p, 0)
                dbg("x1", xT, [b for r in bx for b in r])
                stage(2)
                norm_mod(p, 1)
                stage(3)
                mixer(p)
                dbg("x2", xT, [b for r in bx for b in r])
                stage(4)
                norm_mod(p, 2)
                ffn(p, 1)
                stage(5)
                norm_mod(p, 3, final=True)
                stage(6 + p)
        except _Stop:
            pass
        dbg("ada", adaT, bada)

        P.emit()
    return nc, ins_spec, outs_spec


_CACHE = {}


def _get_program():
    if "nc" not in _CACHE:
        _CACHE["nc"] = build_program()
    return _CACHE["nc"]


def kernel(x_prompt, x_sample, state_gdn, state_conv, state_ret, c_prompt, c_sample,
           w_ada, b_ada, norm_ffn1, w1_gate, w1_up, w1_down, norm_mix, w_in, conv_w, a_log, dt_bias,
           gdn_norm_w, ret_gn_w, ret_gn_b, w_out, norm_ffn2, w2_gate, w2_up, w2_down,
           w_ada_final, b_ada_final, norm_final):
    f = lambda a: np.asarray(a, dtype=np.float32)
    x_prompt, x_sample = f(x_prompt), f(x_sample)
    state_gdn, state_conv, state_ret = f(state_gdn)[0], f(state_conv)[0], f(state_ret)[0]
    c_prompt, c_sample = f(c_prompt), f(c_sample)
    nc, ins_spec, outs_spec = _get_program()
    consts = host_consts()

    shared = {}
    shared["w_ada"] = np.concatenate([tile_w(f(w_ada)[0]), tile_w(f(w_ada_final))], 0)
    ball = np.concatenate([f(b_ada)[0], f(b_ada_final)])
    shared["b_ada"] = np.ascontiguousarray(ball.reshape(176, 128).T)
    nrm = np.stack([f(norm_ffn1)[0], f(norm_mix)[0], f(norm_ffn2)[0], f(norm_final)], 0)
    shared["normw"] = np.ascontiguousarray(nrm.reshape(4, KC, 128).transpose(2, 0, 1))
    shared["w1g"], shared["w1u"] = tile_w(f(w1_gate)[0]), tile_w(f(w1_up)[0])
    shared["w2g"], shared["w2u"] = tile_w(f(w2_gate)[0]), tile_w(f(w2_up)[0])
    shared["w1d"] = np.ascontiguousarray(f(w1_down)[0].reshape(NFC, 128, D))
    shared["w2d"] = np.ascontiguousarray(f(w2_down)[0].reshape(NFC, 128, D))
    win = f(w_in)[0]
    cols = []
    for hd in range(8):
        for blk in (0, 1, 2, 3):
            o = blk * 1024 + hd * 128
            cols.append(np.arange(o, o + 128))
    o5 = 3072 + 1024 + 16
    for hd in range(8):
        for blk in (0, 1, 2, 3):
            o = o5 + blk * 1024 + hd * 128
            cols.append(np.arange(o, o + 128))
    cols = np.concatenate(cols)
    shared["w_in"] = tile_w(win[:, cols])
    wba_ = win[:, 4096:4112]
    shared["w_ba"] = np.ascontiguousarray(wba_.reshape(KC, 128, 16).transpose(1, 0, 2))
    wo = f(w_out)[0]
    shared["w_out"] = tile_w(wo)
    shared["convw"] = np.ascontiguousarray(f(conv_w)[0].reshape(4, 24, 128).transpose(2, 1, 0))
    smallv = np.zeros((128, 32), np.float32)
    smallv[:, 0:8] = f(dt_bias)[0][None, :]
    smallv[:, 8:16] = f(a_log)[0][None, :]
    smallv[:, 16] = f(gdn_norm_w)[0]
    smallv[:, 17:25] = f(ret_gn_w)[0].reshape(8, 128).T
    shared["smallv"] = smallv
    shared["gnb"] = np.ascontiguousarray(f(ret_gn_b)[0].reshape(8, 128).T)
    for k in ("ident", "ones", "rsw", "maskP", "maskS", "rdmP", "rdmS", "rqdP", "rqdS", "rkdP", "rkdS",
              "rgcP", "rgcS", "onehot", "rot", "cst"):
        shared[k] = consts[k]

    in_maps = []
    for c in range(NCORES):
        b = c % 4
        m = dict(shared)
        xin = np.zeros((NPASS, T, D), np.float32)
        sq0 = c * SPC
        for p in range(NPASS):
            xin[p, :TPP] = x_prompt[b, p * TPP:(p + 1) * TPP]
            xs = x_sample[sq0 + p * SPP: sq0 + (p + 1) * SPP]
            xin[p, TPP:] = xs.transpose(1, 0, 2).reshape(TS, D)
        m["xin"] = np.ascontiguousarray(xin.reshape(NPASS, T, KC, 128).transpose(0, 3, 2, 1))
        cvec = np.concatenate([c_prompt[b:b + 1], c_sample[sq0:sq0 + SPC]], 0)
        m["cv"] = np.ascontiguousarray(cvec.reshape(17, KC, 128).transpose(2, 1, 0))
        sg_ = state_gdn[sq0:sq0 + SPC].reshape(NPASS, SPP, 8, 128, 128)
        m["sgdn"] = np.ascontiguousarray(sg_.transpose(0, 2, 3, 1, 4))
        sr_ = state_ret[sq0:sq0 + SPC].reshape(NPASS, SPP, 8, 128, 128)
        m["sret"] = np.ascontiguousarray(sr_.transpose(0, 2, 3, 1, 4))
        sc_ = state_conv[sq0:sq0 + SPC].reshape(NPASS, SPP, 3, 24, 128)
        m["sconv"] = np.ascontiguousarray(sc_.transpose(0, 3, 4, 2, 1))
        for k, shp in ins_spec.items():
            assert m[k].shape == shp, (k, m[k].shape, shp)
            assert m[k].dtype == np.float32
        in_maps.append({k: m[k] for k in ins_spec})

    res = run_bass_kernel_spmd(nc, in_maps, core_ids=list(range(NCORES)))
    R = res.results
    _CACHE["last"] = R

    y_prompt = np.zeros((4, 2048, D), np.float32)
    y_sample = np.zeros((128, 4, D), np.float32)
    gdn_p = np.zeros((1, 4, 8, 128, 128), np.float32)
    conv_p = np.zeros((1, 4, 3, 3072), np.float32)
    ret_p = np.zeros((1, 4, 8, 128, 128), np.float32)
    gdn_s = np.zeros((1, 128, 8, 128, 128), np.float32)
    conv_s = np.zeros((1, 128, 3, 3072), np.float32)
    ret_s = np.zeros((1, 128, 8, 128, 128), np.float32)
    for c in range(NCORES):
        r = R[c]
        sq0 = c * SPC
        y = np.asarray(r["yout"]).transpose(0, 3, 2, 1).reshape(NPASS, T, D)
        for p in range(NPASS):
            if c < 4:
                y_prompt[c, p * TPP:(p + 1) * TPP] = y[p, :TPP]
            y_sample[sq0 + p * SPP: sq0 + (p + 1) * SPP] = y[p, TPP:].reshape(4, SPP, D).transpose(1, 0, 2)
        if c < 4:
            gdn_p[0, c] = r["gdn_p"]
            ret_p[0, c] = r["ret_p"]
            conv_p[0, c] = r["conv_p"]
        gs = np.asarray(r["gdn_s"]).transpose(0, 3, 1, 2, 4).reshape(SPC, 8, 128, 128)
        rs = np.asarray(r["ret_s"]).transpose(0, 3, 1, 2, 4).reshape(SPC, 8, 128, 128)
        cs = np.asarray(r["conv_s"]).transpose(0, 2, 1, 3).reshape(SPC, 3, 3072)
        gdn_s[0, sq0:sq0 + SPC] = gs
        ret_s[0, sq0:sq0 + SPC] = rs
        conv_s[0, sq0:sq0 + SPC] = cs
    return (y_prompt, y_sample, gdn_p, conv_p, ret_p, gdn_s, conv_s, ret_s)
```
